# Optimizing a Trainium2 kernel written in Bass

```python
import jax, jax.numpy as jnp
from jax import lax
import numpy as np

D_MODEL = 1024
BATCH = 8
SEQ = 2048
DEPTH = 1
DEC_BATCH = 128
DEC_SEQ = 4
PAST_LEN = 16384
PAGE_SIZE = 128

CHUNK = 128
A_WIDTH = D_MODEL // 2
A_GROUPS = 4
A_GDIM = A_WIDTH // A_GROUPS
POOL_WINDOWS = (2, 4, 8, 16)
B_GROUPS = len(POOL_WINDOWS)
B_WIDTH = D_MODEL // 4
B_GDIM = B_WIDTH // B_GROUPS
POOL_STATE = max(POOL_WINDOWS) - 1
MEM_LEN = 256
C_HEADS = 4
C_WIDTH = D_MODEL // 4
C_HEAD_DIM = C_WIDTH // C_HEADS
N_BRANCH = 3
D_IN = 2 * A_WIDTH + B_WIDTH + C_WIDTH + N_BRANCH * D_MODEL
D_FF = ((8 * D_MODEL // 3) + 127) // 128 * 128
CONV_W = 3
EPS = 1e-6

kernel_name = "hybrid_gmlp_pool_memxattn_decoder_step"


def rmsnorm(x, g):
    xf = x.astype(jnp.float32)
    y = xf * lax.rsqrt(jnp.mean(xf * xf, axis=-1, keepdims=True) + EPS)
    return (y * g.astype(jnp.float32)).astype(x.dtype)


def layernorm(x, g, b):
    xf = x.astype(jnp.float32)
    mu = jnp.mean(xf, axis=-1, keepdims=True)
    var = jnp.mean(jnp.square(xf - mu), axis=-1, keepdims=True)
    y = (xf - mu) * lax.rsqrt(var + EPS)
    return (y * g.astype(jnp.float32) + b.astype(jnp.float32)).astype(x.dtype)


def chunk_spatial(v, w_s, b_s):
    B, T, G, dg = v.shape
    L = min(T, CHUNK)
    nc = T // L
    vc = v.reshape(B, nc, L, G, dg)
    ws = w_s[:, :L, :L] * jnp.tril(jnp.ones((L, L), v.dtype))
    out = jnp.einsum('gts,bcsgd->bctgd', ws, vc) + b_s[:, :L].T[None, None, :, :, None]
    return out.reshape(B, T, G, dg)


def multi_scale_pool(xb_ext, n_prev):
    P = POOL_STATE
    xf = xb_ext.astype(jnp.float32)
    T = xf.shape[1] - P
    cs = jnp.concatenate([jnp.zeros_like(xf[:, :1]), jnp.cumsum(xf, axis=1)], axis=1)
    win = jnp.array(POOL_WINDOWS, dtype=jnp.int32)
    t = jnp.arange(T, dtype=jnp.int32)
    lo_idx = (P + 1 + t)[:, None] - win[None, :]
    lo = cs[:, lo_idx, jnp.arange(B_GROUPS)[None, :]]
    hi = cs[:, P + 1:]
    count = jnp.minimum(win[None, :], t[:, None] + 1 + n_prev).astype(jnp.float32)
    mean = (hi - lo) / count[None, :, :, None]
    return (mean - xf[:, P:]).astype(xb_ext.dtype)


def memory_kv(mem, g_mem, w_kv):
    B = mem.shape[0]
    kv = (rmsnorm(mem, g_mem) @ w_kv).reshape(B, MEM_LEN, 2, C_HEADS, C_HEAD_DIM)
    return kv[:, :, 0], kv[:, :, 1]


def cross_attn(q, k, v):
    s = jnp.einsum('bthd,bmhd->bhtm', q, k).astype(jnp.float32) * (C_HEAD_DIM ** -0.5)
    p = jax.nn.softmax(s, axis=-1).astype(v.dtype)
    return jnp.einsum('bhtm,bmhd->bthd', p, v)


def token_mixer(xn, mem_k, mem_v, pool_prev, n_prev, lp):
    B, T, _ = xn.shape
    proj = xn @ lp['w_in']
    s1 = 2 * A_WIDTH
    s2 = s1 + B_WIDTH
    s3 = s2 + C_WIDTH
    a_part, b_part, q_part, g_part = proj[..., :s1], proj[..., s1:s2], proj[..., s2:s3], proj[..., s3:]
    u, v = jnp.split(jax.nn.gelu(a_part), 2, axis=-1)
    vn = layernorm(v, lp['gmlp_ln_g'], lp['gmlp_ln_b'])
    sg = chunk_spatial(vn.reshape(B, T, A_GROUPS, A_GDIM), lp['w_spatial'], lp['b_spatial']).reshape(B, T, A_WIDTH)
    br_a = (u * sg) @ lp['w_branch_a']
    xb_ext = jnp.concatenate([pool_prev.astype(b_part.dtype), b_part], axis=1)
    pooled = multi_scale_pool(xb_ext.reshape(B, POOL_STATE + T, B_GROUPS, B_GDIM), n_prev)
    mixed = jnp.einsum('btgd,gde->btge', pooled, lp['w_pool']) * lp['pool_scale'].reshape(B_GROUPS, B_GDIM)
    br_b = mixed.reshape(B, T, B_WIDTH) @ lp['w_branch_b']
    new_pool = xb_ext[:, -POOL_STATE:]
    q = q_part.reshape(B, T, C_HEADS, C_HEAD_DIM)
    br_c = cross_attn(q, mem_k, mem_v).reshape(B, T, C_WIDTH) @ lp['w_branch_c']
    gates = jax.nn.sigmoid((g_part + lp['b_gate']).astype(jnp.float32)).astype(xn.dtype).reshape(B, T, N_BRANCH, D_MODEL)
    merged = gates[:, :, 0] * br_a + gates[:, :, 1] * br_b + gates[:, :, 2] * br_c
    return merged @ lp['w_out'], vn, new_pool


def conv_ffn(hn, conv_prev, lp):
    T = hn.shape[1]
    g, u = jnp.split(hn @ lp['w_up'], 2, axis=-1)
    g_ext = jnp.concatenate([conv_prev.astype(g.dtype), g], axis=1)
    c = lp['conv_b'] + sum(lp['conv_w'][k] * g_ext[:, k:k + T] for k in range(CONV_W))
    y = (jax.nn.gelu(c) * u) @ lp['w_down']
    return y, g_ext[:, -(CONV_W - 1):]


def decoder_layer(x, mem_k, mem_v, pool_prev, n_prev, conv_prev, lp):
    mix, vn, new_pool = token_mixer(rmsnorm(x, lp['norm_mix_pre']), mem_k, mem_v, pool_prev, n_prev, lp)
    h = x + rmsnorm(mix, lp['norm_mix_post'])
    f, new_conv = conv_ffn(rmsnorm(h, lp['norm_ffn_pre']), conv_prev, lp)
    out = h + rmsnorm(f, lp['norm_ffn_post'])
    return out, vn, new_pool, new_conv


def setup_inputs(seed: int = 0) -> dict:
    key = jax.random.key(seed)
    ks = jax.random.split(key, 29)
    f32 = jnp.float32

    def nrm(k, shape, scale=1.0):
        return jax.random.normal(k, shape, f32) * scale

    def gain(k, shape):
        return 1.0 + 0.05 * jax.random.normal(k, shape, f32)

    L = DEPTH
    return {
        'x_prompt': nrm(ks[0], (BATCH, SEQ, D_MODEL)),
        'x_sample': nrm(ks[1], (DEC_BATCH, DEC_SEQ, D_MODEL)),
        'mem_prompt': nrm(ks[2], (BATCH, MEM_LEN, D_MODEL)),
        'cache_mem_k': nrm(ks[3], (L, DEC_BATCH, MEM_LEN, C_HEADS, C_HEAD_DIM)),
        'cache_mem_v': nrm(ks[4], (L, DEC_BATCH, MEM_LEN, C_HEADS, C_HEAD_DIM)),
        'state_pool': nrm(ks[5], (L, DEC_BATCH, POOL_STATE, B_WIDTH)),
        'state_conv': nrm(ks[6], (L, DEC_BATCH, CONV_W - 1, D_FF)),
        'norm_mix_pre': gain(ks[7], (L, D_MODEL)),
        'norm_mix_post': gain(ks[8], (L, D_MODEL)),
        'norm_ffn_pre': gain(ks[9], (L, D_MODEL)),
        'norm_ffn_post': gain(ks[10], (L, D_MODEL)),
        'w_in': nrm(ks[11], (L, D_MODEL, D_IN), D_MODEL ** -0.5),
        'b_gate': nrm(ks[12], (L, N_BRANCH * D_MODEL), 0.01),
        'gmlp_ln_g': gain(ks[13], (L, A_WIDTH)),
        'gmlp_ln_b': nrm(ks[14], (L, A_WIDTH), 0.01),
        'w_spatial': nrm(ks[15], (L, A_GROUPS, CHUNK, CHUNK), CHUNK ** -0.5),
        'b_spatial': gain(ks[16], (L, A_GROUPS, CHUNK)),
        'w_pool': nrm(ks[17], (L, B_GROUPS, B_GDIM, B_GDIM), B_GDIM ** -0.5),
        'pool_scale': gain(ks[18], (L, B_WIDTH)),
        'mem_norm': gain(ks[19], (L, D_MODEL)),
        'w_mem_kv': nrm(ks[20], (L, D_MODEL, 2 * C_WIDTH), D_MODEL ** -0.5),
        'w_branch_a': nrm(ks[21], (L, A_WIDTH, D_MODEL), A_WIDTH ** -0.5),
        'w_branch_b': nrm(ks[22], (L, B_WIDTH, D_MODEL), B_WIDTH ** -0.5),
        'w_branch_c': nrm(ks[23], (L, C_WIDTH, D_MODEL), C_WIDTH ** -0.5),
        'w_out': nrm(ks[24], (L, D_MODEL, D_MODEL), D_MODEL ** -0.5),
        'w_up': nrm(ks[25], (L, D_MODEL, 2 * D_FF), D_MODEL ** -0.5),
        'conv_w': nrm(ks[26], (L, CONV_W, D_FF), CONV_W ** -0.5),
        'conv_b': nrm(ks[27], (L, D_FF), 0.01),
        'w_down': nrm(ks[28], (L, D_FF, D_MODEL), D_FF ** -0.5),
    }


def reference(x_prompt, x_sample, mem_prompt, cache_mem_k, cache_mem_v, state_pool, state_conv,
              norm_mix_pre, norm_mix_post, norm_ffn_pre, norm_ffn_post, w_in, b_gate,
              gmlp_ln_g, gmlp_ln_b, w_spatial, b_spatial, w_pool, pool_scale, mem_norm, w_mem_kv,
              w_branch_a, w_branch_b, w_branch_c, w_out, w_up, conv_w, conv_b, w_down):
    hp = x_prompt
    hs = x_sample
    mk_p, mv_p, pool_p, conv_p = [], [], [], []
    v_s, pool_s, conv_s = [], [], []
    for l in range(DEPTH):
        lp = {
            'norm_mix_pre': norm_mix_pre[l], 'norm_mix_post': norm_mix_post[l],
            'norm_ffn_pre': norm_ffn_pre[l], 'norm_ffn_post': norm_ffn_post[l],
            'w_in': w_in[l], 'b_gate': b_gate[l], 'gmlp_ln_g': gmlp_ln_g[l], 'gmlp_ln_b': gmlp_ln_b[l],
            'w_spatial': w_spatial[l], 'b_spatial': b_spatial[l], 'w_pool': w_pool[l],
            'pool_scale': pool_scale[l], 'w_branch_a': w_branch_a[l], 'w_branch_b': w_branch_b[l],
            'w_branch_c': w_branch_c[l], 'w_out': w_out[l], 'w_up': w_up[l], 'conv_w': conv_w[l],
            'conv_b': conv_b[l], 'w_down': w_down[l],
        }
        mk, mv = memory_kv(mem_prompt, mem_norm[l], w_mem_kv[l])
        pool0 = jnp.zeros((hp.shape[0], POOL_STATE, B_WIDTH), hp.dtype)
        conv0 = jnp.zeros((hp.shape[0], CONV_W - 1, D_FF), hp.dtype)
        hp, _, npool_p, nconv_p = decoder_layer(hp, mk, mv, pool0, 0, conv0, lp)
        mk_p.append(mk)
        mv_p.append(mv)
        pool_p.append(npool_p)
        conv_p.append(nconv_p)
        hs, vn_s, npool_s, nconv_s = decoder_layer(hs, cache_mem_k[l], cache_mem_v[l], state_pool[l],
                                                   POOL_STATE, state_conv[l], lp)
        v_s.append(vn_s)
        pool_s.append(npool_s)
        conv_s.append(nconv_s)
    return (hp, hs, jnp.stack(mk_p), jnp.stack(mv_p), jnp.stack(pool_p), jnp.stack(conv_p),
            jnp.stack(v_s), jnp.stack(pool_s), jnp.stack(conv_s))
```

```python
import contextlib
from math import prod

import numpy as np
import ml_dtypes

import concourse.bass as bass
import concourse.mybir as mybir
from concourse.bass_utils import run_bass_kernel_spmd

F32 = mybir.dt.float32
BF16 = mybir.dt.bfloat16
AF = mybir.ActivationFunctionType
ALU = mybir.AluOpType
DTSZ = {F32: 4, BF16: 2}

import os
EVAC_ENGS = os.environ.get('KEVAC', 'dve,dve').split(',')
SAME_ENG_GAP = int(os.environ.get('KGAP', '2'))
NCORES = 8
D = 1024
KC = 8
SEQ = 2048
NG = 512
NGRP = SEQ // NG
NS = 64
NB = 16
NCOL = NG + NS
AW = 512
BW = 256
DFF = 2816
FC = 22
MEM = 256
EPS = 1e-6
D_IN = 4608


class Op:
    __slots__ = ("eng", "fn", "deps", "signal", "sigval", "dma", "sem", "semval", "prevval", "is_out", "idx", "lidx")

    def __init__(self, eng, fn, dma=False):
        self.eng = eng
        self.fn = fn
        self.deps = set()
        self.signal = False
        self.sigval = 0
        self.dma = dma
        self.sem = None
        self.semval = 0
        self.prevval = 0
        self.is_out = False
        self.idx = 0
        self.lidx = 0


class Sched:
    def __init__(self, nc, es):
        self.nc = nc
        self.ops = []
        self.rowb = {}
        self.wrec = {}
        self.rrec = {}
        self.engs = {"pe": nc.tensor, "act": nc.scalar, "dve": nc.vector, "pool": nc.gpsimd, "sp": nc.sync}
        self.esem = {k: es.enter_context(nc.semaphore("s_" + k)) for k in ("pe", "act", "dve", "pool")}
        self.ndsem = 12
        self.dsem = {q: [es.enter_context(nc.semaphore(f"d_{q}{i}")) for i in range(self.ndsem)] for q in ("sp", "pool")}
        self.dcount = {"sp": 0, "pool": 0}
        self.lcount = {}
        self.marks = []

    def mark(self, name):
        self.marks.append((name, len(self.ops)))

    def region(self, ap):
        name = ap.name
        rb = self.rowb.get(name)
        if rb is None:
            return None
        if name.startswith("ps"):
            return name, 0, 128, [(0, 2048)]
        esz = DTSZ[ap.dtype]
        offb = ap.offset * esz
        pat = list(ap.ap)
        p0 = offb // rb
        f0 = offb % rb
        if pat[0][0] * esz == rb:
            pc = pat[0][1]
            dims = pat[1:]
        else:
            pc = 1
            dims = pat
        dims = [(s * esz, c) for s, c in dims if c > 1 and s != 0]
        if not dims:
            ivs = [(f0, f0 + esz)]
        else:
            inner = dims[-1]
            outer = dims[:-1]
            run = (inner[1] - 1) * inner[0] + esz
            nouter = prod(c for _, c in outer) if outer else 1
            if nouter <= 64:
                offs = [0]
                for s, c in outer:
                    offs = [o + i * s for o in offs for i in range(c)]
                ivs = sorted((f0 + o, f0 + o + run) for o in offs)
                merged = [ivs[0]]
                for lo, hi in ivs[1:]:
                    if lo <= merged[-1][1]:
                        merged[-1] = (merged[-1][0], max(hi, merged[-1][1]))
                    else:
                        merged.append((lo, hi))
                ivs = merged
            else:
                ivs = [(f0, f0 + sum((c - 1) * s for s, c in dims) + esz)]
        return name, p0, p0 + pc, ivs

    def _add_dep(self, op, prod_op, kind):
        if prod_op is op:
            return
        if not op.dma and not prod_op.dma and op.eng == prod_op.eng:
            if op.eng == "pe":
                return
            if kind != "raw":
                return
        op.deps.add(prod_op)

    def _read(self, op, ap):
        r = self.region(ap)
        if r is None:
            return
        name, p0, p1, ivs = r
        wl = self.wrec.setdefault(name, [])
        for rec in wl:
            if rec[0] < p1 and p0 < rec[1]:
                for lo, hi in ivs:
                    if rec[2] < hi and lo < rec[3]:
                        self._add_dep(op, rec[4], "raw")
                        break
        rl = self.rrec.setdefault(name, [])
        if name.startswith("ps"):
            for rec in rl:
                if rec[4].eng != op.eng:
                    self._add_dep(op, rec[4], "rar")
        for lo, hi in ivs:
            if not op.dma:
                for i, rec in enumerate(rl):
                    if rec[0] == p0 and rec[1] == p1 and rec[2] == lo and rec[3] == hi and (not rec[4].dma) and rec[4].eng == op.eng:
                        rl[i] = (p0, p1, lo, hi, op)
                        break
                else:
                    rl.append((p0, p1, lo, hi, op))
            else:
                rl.append((p0, p1, lo, hi, op))

    def _write(self, op, ap):
        r = self.region(ap)
        if r is None:
            return
        name, p0, p1, ivs = r
        wl = self.wrec.setdefault(name, [])
        rl = self.rrec.setdefault(name, [])
        for lst, kind in ((wl, "waw"), (rl, "war")):
            keep = []
            for rec in lst:
                hit = False
                contained = False
                if rec[0] < p1 and p0 < rec[1]:
                    for lo, hi in ivs:
                        if rec[2] < hi and lo < rec[3]:
                            hit = True
                            if lo <= rec[2] and rec[3] <= hi and p0 <= rec[0] and rec[1] <= p1:
                                contained = True
                            break
                if hit:
                    self._add_dep(op, rec[4], kind)
                if not contained:
                    keep.append(rec)
            lst[:] = keep
        for lo, hi in ivs:
            wl.append((p0, p1, lo, hi, op))

    def add(self, eng, fn, reads, writes, dma=False):
        op = Op(eng, fn, dma)
        op.idx = len(self.ops)
        if not dma:
            op.lidx = self.lcount.get(eng, 0)
            self.lcount[eng] = op.lidx + 1
        for ap in reads:
            if ap is not None and not isinstance(ap, (int, float)):
                self._read(op, ap)
        for ap in writes:
            if ap is not None:
                self._write(op, ap)
        latest = {}
        keep = set()
        for p in op.deps:
            if p.dma:
                keep.add(p)
            elif p.eng not in latest or latest[p.eng].idx < p.idx:
                latest[p.eng] = p
        for e, p in latest.items():
            if (not dma) and e == eng and op.lidx - p.lidx > SAME_ENG_GAP:
                continue
            keep.add(p)
            p.signal = True
        op.deps = keep
        if dma:
            k = self.dcount[eng]
            self.dcount[eng] = k + 1
            op.sem = self.dsem[eng][k % self.ndsem]
            op.semval = 16 * (k // self.ndsem + 1)
            op.prevval = 16 * (k // self.ndsem)
        self.ops.append(op)
        return op

    def mm(self, out, lhsT, rhs, start=True, stop=True):
        return self.add("pe", lambda e: e.matmul(out, lhsT, rhs, start=start, stop=stop), [lhsT, rhs], [out])

    def tr(self, out, in_, ident):
        return self.add("pe", lambda e: e.transpose(out, in_, ident), [in_, ident], [out])

    def act(self, out, in_, func, bias=None, scale=None, accum_out=None):
        kw = {}
        if bias is not None:
            kw["bias"] = bias
        if scale is not None:
            kw["scale"] = scale
        if accum_out is not None:
            kw["accum_out"] = accum_out
        if func == AF.Copy and any(v is not None and not isinstance(v, (int, float)) for v in (bias, scale)):
            func = AF.Identity
        return self.add("act", lambda e: e.activation(out=out, in_=in_, func=func, **kw), [in_, bias, scale], [out, accum_out])

    def tt(self, eng, out, in0, in1, op):
        return self.add(eng, lambda e: e.tensor_tensor(out=out, in0=in0, in1=in1, op=op), [in0, in1], [out])

    def ts(self, eng, out, in0, s1, s2, op0, op1=None):
        if op1 is None:
            return self.add(eng, lambda e: e.tensor_scalar(out=out, in0=in0, scalar1=s1, scalar2=None, op0=op0), [in0, s1], [out])
        return self.add(eng, lambda e: e.tensor_scalar(out=out, in0=in0, scalar1=s1, scalar2=s2, op0=op0, op1=op1), [in0, s1, s2], [out])

    def stt(self, eng, out, in0, scalar, in1, op0, op1):
        return self.add(eng, lambda e: e.scalar_tensor_tensor(out=out, in0=in0, scalar=scalar, in1=in1, op0=op0, op1=op1), [in0, scalar, in1], [out])

    def copy(self, eng, out, in_):
        if eng == "act":
            return self.act(out, in_, AF.Copy)
        return self.add(eng, lambda e: e.tensor_copy(out=out, in_=in_), [in_], [out])

    def memset(self, eng, out, val):
        return self.add(eng, lambda e: e.memset(out, val), [], [out])

    def recip(self, out, in_):
        return self.add("dve", lambda e: e.reciprocal(out=out, in_=in_), [in_], [out])

    def bn_stats(self, out, in_):
        return self.add("dve", lambda e: e.bn_stats(out=out, in_=in_), [in_], [out])

    def bn_aggr(self, out, in_):
        return self.add("dve", lambda e: e.bn_aggr(out=out, in_=in_), [in_], [out])

    def dma(self, q, out, in_, is_out=False):
        op = self.add(q, lambda e: e.dma_start(out=out, in_=in_), [in_], [out], dma=True)
        op.is_out = is_out
        return op

    def emit(self, limit=None):
        if limit:
            self.ops = self.ops[:limit]
            for op in self.ops:
                if not op.dma:
                    op.signal = False
            live = set(map(id, self.ops))
            for op in self.ops:
                for p in op.deps:
                    if not p.dma:
                        p.signal = True
        cnt = {k: 0 for k in self.esem}
        for op in self.ops:
            if not op.dma and op.signal:
                cnt[op.eng] += 1
                op.sigval = cnt[op.eng]
        waited = {k: {} for k in self.engs}
        outs = []

        def wait(engname, sem, val):
            w = waited[engname]
            key = id(sem)
            if w.get(key, 0) >= val:
                return
            w[key] = val
            self.engs[engname].wait_ge(sem, val)

        for op in self.ops:
            need = {}
            for p in op.deps:
                if p.dma:
                    sem, val = p.sem, p.semval
                else:
                    sem, val = self.esem[p.eng], p.sigval
                k = id(sem)
                if k not in need or need[k][1] < val:
                    need[k] = (sem, val)
            if op.dma and op.prevval > 0:
                k = id(op.sem)
                if k not in need or need[k][1] < op.prevval:
                    need[k] = (op.sem, op.prevval)
            for sem, val in need.values():
                wait(op.eng, sem, val)
            inst = op.fn(self.engs[op.eng])
            if op.dma:
                inst.then_inc(op.sem, 16)
                if op.is_out:
                    outs.append(op)
            elif op.signal:
                inst.then_inc(self.esem[op.eng], 1)
        for engname in ("sp", "act", "pool"):
            for op in outs:
                wait(engname, op.sem, op.semval)


def rs(ap, *dims):
    names = "abcdefgh"[: len(dims)]
    pat = "p (" + " ".join(names) + ") -> p " + " ".join(names)
    return ap.rearrange(pat, **{n: d for n, d in zip(names, dims)})


class Arena:
    def __init__(self, nc, es, S, name, nbytes):
        self.nbytes = nbytes
        self.t = es.enter_context(nc.sbuf_tensor(name, [128, nbytes // 2], BF16))
        S.rowb[name] = nbytes
        self.off = 0
        self.hi = 0

    def at(self, off, dtype, *shape):
        n = prod(shape)
        nb = n * DTSZ[dtype]
        assert off % 4 == 0 and off + nb <= self.nbytes, (off, nb, self.nbytes)
        a = self.t[:, off // 2: (off + nb) // 2]
        if dtype != BF16:
            a = a.bitcast(dtype)
        if len(shape) > 1:
            a = rs(a, *shape)
        return a

    def alloc(self, dtype, *shape):
        nb = prod(shape) * DTSZ[dtype]
        off = self.off
        self.off = (off + nb + 63) // 64 * 64
        self.hi = max(self.hi, self.off)
        return self.at(off, dtype, *shape)


def build_program():
    nc = bass.Bass("TRN2", target_bir_lowering=False)
    es = contextlib.ExitStack()

    def din(name, shape, dt=F32):
        return nc.dram_tensor(name, list(shape), dt, kind="ExternalInput").ap()

    def dout(name, shape):
        return nc.dram_tensor(name, list(shape), F32, kind="ExternalOutput").ap()

    x_p = din("x_p", [SEQ, D])
    x_s = din("x_s", [NS, D])
    mem = din("mem", [MEM, D])
    ck = din("ck", [NB, MEM, 256])
    cv = din("cv", [NB, MEM, 256])
    st_pool = din("st_pool", [NB, 15, BW])
    st_conv = din("st_conv", [NB, 2, DFF])
    norm_mix_pre = din("norm_mix_pre", [1, D])
    norm_mix_post = din("norm_mix_post", [1, D])
    norm_ffn_pre = din("norm_ffn_pre", [1, D])
    norm_ffn_post = din("norm_ffn_post", [1, D])
    w_in = din("w_in", [D, D_IN])
    b_gate = din("b_gate", [1, 3 * D])
    gmlp_ln_g = din("gmlp_ln_g", [1, AW])
    gmlp_ln_b = din("gmlp_ln_b", [1, AW])
    w_spatial = din("w_spatial", [4, 128, 128])
    b_spatial = din("b_spatial", [4, 128])
    w_pool = din("w_pool", [4, 64, 64])
    pool_scale = din("pool_scale", [1, BW])
    mem_norm = din("mem_norm", [1, D])
    w_mem_kv = din("w_mem_kv", [D, 512])
    w_branch_a = din("w_branch_a", [AW, D])
    w_branch_b = din("w_branch_b", [BW, D])
    w_branch_c = din("w_branch_c", [BW, D])
    w_out = din("w_out", [D, D])
    w_up = din("w_up", [D, 2 * DFF])
    conv_w = din("conv_w", [3, DFF])
    conv_b = din("conv_b", [1, DFF])
    w_down = din("w_down", [DFF, D])
    c_identb = din("c_identb", [128, 128], BF16)
    c_identf = din("c_identf", [128, 128])
    c_maskT = din("c_maskT", [128, 128])
    c_rcnt = din("c_rcnt", [128, 32])

    y_p = dout("y_p", [SEQ, D])
    y_s = dout("y_s", [NS, D])
    o_mk = dout("o_mk", [MEM, 256])
    o_mv = dout("o_mv", [MEM, 256])
    o_pool_p = dout("o_pool_p", [15, BW])
    o_conv_p = dout("o_conv_p", [2, DFF])
    o_v_s = dout("o_v_s", [NS, AW])
    o_pool_s = dout("o_pool_s", [NB, 15, BW])
    o_conv_s = dout("o_conv_s", [NB, 2, DFF])

    with es:
        S = Sched(nc, es)
        banks = []
        for i in range(8):
            t = es.enter_context(nc.psum_tensor(f"ps{i}", [128, 512], F32))
            S.rowb[f"ps{i}"] = 2048
            banks.append(t)
        bank_i = [0]
        bank_n = [8]

        def nbank():
            b = banks[bank_i[0] % bank_n[0]]
            bank_i[0] += 1
            assert not (S.wrec.get(b.name) and not S.rrec.get(b.name)), ("PSUM bank reused before consumption", b.name)
            return b

        PB = 90624
        UB = 121856
        P = Arena(nc, es, S, "arenaP", PB)
        U = Arena(nc, es, S, "arenaU", UB)

        identb = P.alloc(BF16, 128)
        identf = P.alloc(F32, 128)
        onesb = P.alloc(BF16, 128)
        maskT = P.alloc(F32, 128)
        rcnt = P.alloc(F32, 2, 16)
        neghalf = P.alloc(F32, 8)
        V1T = P.alloc(F32, 72)
        V2T = P.alloc(F32, 66)
        hb = P.alloc(F32, 24)
        gpost1h = P.alloc(F32, D)
        gpost2 = P.alloc(F32, D)
        lng = P.alloc(F32, AW)
        lnb = P.alloc(F32, AW)
        bsb = P.alloc(F32, 4, 128)
        ws4 = P.alloc(F32, 4, 4, 4)
        bs4 = P.alloc(F32, 4, 4)
        wsT = P.alloc(BF16, 4, 128)
        Wbd = P.alloc(BF16, 2, 128)
        kT = P.alloc(BF16, 2, MEM)
        Vb = P.alloc(BF16, 2, 256)
        X = P.alloc(F32, 5, D)
        xnT = P.alloc(BF16, KC, NCOL)
        RING_N = 2
        ring = [P.alloc(BF16, 6144) for _ in range(RING_N)]
        bTs = P.alloc(F32, 2, NB, 19)
        gTs = P.alloc(F32, FC, NB, 6)
        halo_g = P.alloc(F32, FC, 2)
        halo_b = P.alloc(F32, 2, 15)
        PSt = P.alloc(F32, 2, 80)
        CS = P.alloc(F32, FC, 34)
        stats = P.alloc(F32, 128)
        assert P.hi <= PB, P.hi
        print('P arena used', P.hi, 'of', PB)

        U.off = 0
        uT = U.alloc(BF16, 4, NCOL)
        qT = U.alloc(BF16, 2, NCOL)
        attnT = U.alloc(BF16, 2, NCOL)
        mixedT = U.alloc(BF16, 2, NCOL)
        pooledT = U.alloc(BF16, 2, NCOL)
        bT = U.alloc(F32, 2, 15 + NG)
        TMP0 = U.off
        junk = U.alloc(BF16, D)
        xnb = [U.alloc(BF16, D) for _ in range(3)]
        vg = [U.alloc(F32, AW) for _ in range(4)]
        vn1 = U.alloc(F32, AW)
        vn2 = U.alloc(F32, AW)
        vnb = [U.alloc(BF16, AW) for _ in range(2)]
        vnS = U.alloc(F32, AW)
        spt = U.alloc(F32, 4, 128)
        vnTs = U.alloc(F32, 4, NB, 4)
        sgs = U.alloc(F32, 4, NB, 4)
        s2 = U.alloc(F32, 2, 15 + NG)
        s4 = U.alloc(F32, 2, 15 + NG)
        s8 = U.alloc(F32, 15 + NG)
        s16 = U.alloc(F32, 15 + NG)
        ptmp = U.alloc(F32, 2, 16)
        expT = [U.alloc(BF16, 2, 512) for _ in range(2)]
        rd = [U.alloc(F32, 512) for _ in range(2)]
        Ksb = [U.alloc(BF16, 2, 256) for _ in range(2)]
        Vsb = [U.alloc(BF16, 2, 256) for _ in range(2)]
        kTb = [U.alloc(BF16, 2, 256) for _ in range(2)]
        expS = [U.alloc(BF16, 2, 16) for _ in range(2)]
        rdS = U.alloc(F32, NB, 16)
        TMP1 = U.off
        stage = U.alloc(F32, 2816)
        U.off = TMP0
        tg = [U.alloc(F32, 512) for _ in range(3)]
        pj = [U.alloc(F32, 512) for _ in range(3)]
        dtmp = [U.alloc(F32, 512) for _ in range(2)]
        junk2 = U.alloc(BF16, D)
        hnb = [U.alloc(BF16, D) for _ in range(3)]
        assert U.off <= TMP1
        U.off = TMP1
        mergedT = U.alloc(BF16, KC, NCOL)
        wba = U.alloc(BF16, 4, D)
        WBB_OFF = U.off
        wbb = U.alloc(BF16, 2, D)
        wbc = U.alloc(BF16, 2, D)
        wout = U.alloc(BF16, KC, D)
        MIX_END = U.off
        assert MIX_END <= UB, MIX_END
        print('U mixer layout end', MIX_END, 'of', UB)
        U.off = 0
        wdn = U.alloc(BF16, FC, D)
        actT = U.alloc(BF16, FC, NCOL)
        gTp = [U.alloc(F32, 2 + NG) for _ in range(2)]
        cT = [U.alloc(F32, NG) for _ in range(2)]
        ge = [U.alloc(F32, NG) for _ in range(2)]
        cTs = U.alloc(F32, NB, 4)
        geS = U.alloc(F32, NB, 4)
        junk3 = U.alloc(BF16, 512)
        ftmp = [U.alloc(F32, 512) for _ in range(2)]
        PStT = U.alloc(F32, 256)
        CST = U.alloc(F32, DFF)
        junkA = U.alloc(BF16, D)
        xnbA = [U.alloc(BF16, D) for _ in range(3)]
        assert U.off <= UB, U.off
        print('U ffn layout end', U.off, 'of', UB)
        U.off = max(U.off, MIX_END)
        gffnb = U.alloc(F32, D)
        assert U.off <= UB, U.off

        sp = "sp"
        S.dma(sp, identb, c_identb[:, :])
        S.dma(sp, identf, c_identf[:, :])
        S.dma(sp, maskT, c_maskT[:, :])
        S.dma(sp, rcnt, c_rcnt[:, :].rearrange("p (c t) -> p c t", c=2))
        S.dma(sp, X[0:NS, 4, :], x_s[:, :])
        for i in range(NG // 128):
            S.dma(sp, X[:, i, :], x_p[i * 128:(i + 1) * 128, :])
        for mt in range(2):
            S.dma(sp, [s2.rearrange("p c l -> p (c l)")[:, 0:D], s4.rearrange("p c l -> p (c l)")[:, 0:D]][mt], mem[mt * 128:(mt + 1) * 128, :])
        S.memset("dve", onesb, 1.0)
        S.memset("dve", neghalf, -0.5)
        S.memset("dve", halo_g, 0.0)
        S.memset("dve", halo_b, 0.0)

        st1 = stage[:, 0:128]
        S.dma(sp, st1[0:8, :], norm_mix_pre.rearrange("o (r p) -> (o r) p", p=128))
        S.dma(sp, st1[8:16, :], norm_ffn_pre.rearrange("o (r p) -> (o r) p", p=128))
        S.dma(sp, st1[16:40, :], b_gate.rearrange("o (r p) -> (o r) p", p=128))
        S.dma(sp, st1[40:42, :], pool_scale.rearrange("o (r p) -> (o r) p", p=128))
        S.dma(sp, st1[42:64, :], conv_b.rearrange("o (r p) -> (o r) p", p=128))
        S.dma(sp, st1[64:72, :], mem_norm.rearrange("o (r p) -> (o r) p", p=128))
        bk = nbank()
        S.tr(bk[:, 0:72], st1[0:72, :], identf[0:72, 0:72])
        S.copy("dve", V1T, bk[:, 0:72])
        st2 = stage[:, 128:256]
        S.dma(sp, st2[0:66, :], conv_w.rearrange("k (r p) -> (k r) p", p=128))
        bk = nbank()
        S.tr(bk[:, 0:66], st2[0:66, :], identf[0:66, 0:66])
        S.copy("dve", V2T, bk[:, 0:66])
        S.ts("dve", hb, V1T[:, 16:40], 0.5, None, ALU.mult)

        S.dma(sp, gpost1h, norm_mix_post[0, :].partition_broadcast(128))
        S.ts("dve", gpost1h, gpost1h, 0.5, None, ALU.mult)
        S.dma(sp, gpost2, norm_ffn_post[0, :].partition_broadcast(128))
        S.dma(sp, gffnb, norm_ffn_pre[0, :].partition_broadcast(128))
        S.dma(sp, lng, gmlp_ln_g[0, :].partition_broadcast(128))
        S.dma(sp, lnb, gmlp_ln_b[0, :].partition_broadcast(128))
        S.dma(sp, bsb, b_spatial.partition_broadcast(128))
        for g in range(4):
            S.dma(sp, ws4[:, g, :, :], w_spatial[g, 0:4, 0:4].partition_broadcast(128))
        S.dma(sp, bs4, b_spatial[:, 0:4].partition_broadcast(128))

        wst = rs(stage[:, 256:768], 4, 128)
        for g in range(4):
            S.dma(sp, wst[:, g, :], w_spatial[g, :, :])
            bk = nbank()
            S.tr(bk[:, 0:128], wst[:, g, :], identf)
            S.tt("dve", wsT[:, g, :], bk[:, 0:128], maskT, ALU.mult)

        wbs = rs(stage[:, 768:1024], 2, 128)
        S.memset("dve", wbs, 0.0)
        for c in range(2):
            S.dma(sp, wbs[0:64, c, 0:64], w_pool[2 * c, :, :])
            S.dma(sp, wbs[64:128, c, 64:128], w_pool[2 * c + 1, :, :])
        S.copy("dve", Wbd, wbs)

        sps = rs(stage[:, 1024:1536], 2, 256)
        stp = st_pool.rearrange("b r f -> (b r) f")
        for hh in range(2):
            S.dma(sp, sps[0:120, hh, :], stp[hh * 120:(hh + 1) * 120, :])
        for hh in range(2):
            for c in range(2):
                bk = nbank()
                S.tr(bk[:, 0:120], sps[0:120, hh, c * 128:(c + 1) * 128], identf[0:120, 0:120])
                S.copy("dve", bTs[:, c, hh * 8:(hh + 1) * 8, 0:15], rs(bk[:, 0:120], 8, 15))
        S.dma(sp, o_pool_s[:, 0:11, :], st_pool[:, 4:15, :], is_out=True)

        scs = stage[0:32, 0:DFF]
        S.dma(sp, scs, st_conv.rearrange("b r f -> (b r) f"))
        for half in range(2):
            bk = nbank()
            for j in range(11):
                fc = half * 11 + j
                S.tr(bk[:, j * 32:(j + 1) * 32], scs[:, fc * 128:(fc + 1) * 128], identf[0:32, 0:32])
            S.copy("dve", gTs[:, half * 11:(half + 1) * 11, :, 0:2], rs(bk[:, 0:352], 11, NB, 2))

        S.mark('setup_done')
        def rstd_pool(out, acc, rows):
            S.ts("pool", out[0:rows], acc[0:rows], EPS, None, ALU.add)
            S.tt("pool", out[0:rows], out[0:rows], neghalf[0:rows, 0:1], ALU.pow)

        def pipeline(stages, n):
            ns = len(stages)
            for k in range(n + ns - 1):
                for st in range(ns - 1, -1, -1):
                    i = k - st
                    if 0 <= i < n:
                        stages[st](i)

        def norm_transpose_stages(items, gcol0, dstT, jk, xbs, col0, gb=None):
            st = {}

            def s_sq(i):
                src, rows, c0 = items[i]
                acc = stats[:, col0 + 2 * i: col0 + 2 * i + 1]
                S.act(jk[0:rows], src, AF.Square, scale=1.0 / 32.0, accum_out=acc[0:rows])

            def s_rstd(i):
                src, rows, c0 = items[i]
                rstd_pool(stats[:, col0 + 2 * i + 1: col0 + 2 * i + 2], stats[:, col0 + 2 * i: col0 + 2 * i + 1], rows)

            def s_norm(i):
                src, rows, c0 = items[i]
                xb = xbs[i % len(xbs)]
                if gb is not None:
                    S.stt("dve", xb[0:rows], src, stats[0:rows, col0 + 2 * i + 1: col0 + 2 * i + 2], gb[0:rows], ALU.mult, ALU.mult)
                else:
                    S.ts("dve", xb[0:rows], src, stats[0:rows, col0 + 2 * i + 1: col0 + 2 * i + 2], None, ALU.mult)

            def s_tr(i):
                src, rows, c0 = items[i]
                xb = xbs[i % len(xbs)]
                bk = nbank()
                st[i] = bk
                bkb = bk[:].bitcast(BF16)
                for kc in range(KC):
                    S.tr(bkb[:, kc * 128: kc * 128 + rows], xb[0:rows, kc * 128:(kc + 1) * 128], identb[0:rows, 0:rows])

            def s_evac(i):
                src, rows, c0 = items[i]
                bkb = st[i][:].bitcast(BF16)
                if gb is not None:
                    S.act(dstT[:, :, c0:c0 + rows], bkb.rearrange("p (k t) -> p k t", k=KC)[:, :, 0:rows], AF.Copy)
                    return
                for kc in range(KC):
                    S.ts("dve", dstT[:, kc, c0:c0 + rows], bkb[:, kc * 128: kc * 128 + rows], V1T[:, gcol0 + kc:gcol0 + kc + 1], None, ALU.mult)

            return [s_sq, s_rstd, s_norm, s_tr, s_evac]

        ring_i = [0]

        def ring_slot():
            r = ring[ring_i[0] % RING_N]
            ring_i[0] += 1
            return r

        def load_w(dst3, src2d):
            S.dma("pool", dst3, src2d.rearrange("(kc p) f -> p kc f", p=128))

        memX = [s2.rearrange("p c l -> p (c l)")[:, 0:D], s4.rearrange("p c l -> p (c l)")[:, 0:D]]
        wb_first = rs(ring_slot()[:, 0:KC * 512], KC, 512)
        load_w(wb_first, w_in[:, 0:512])
        wkv = U.at(WBB_OFF, BF16, KC, 512)
        load_w(wkv, w_mem_kv)
        wb1_first = rs(ring_slot()[:, 0:KC * 512], KC, 512)
        load_w(wb1_first, w_in[:, 512:1024])
        memnT = mergedT
        pipeline(norm_transpose_stages([(memX[mt], 128, mt * 128) for mt in range(2)], 64, memnT, junk, xnb, 112), 2)

        def kv_part2():
            for hp in range(2):
                bk = nbank()
                for kc in range(KC):
                    S.mm(bk[:, 0:MEM], wkv[:, kc, hp * 128:(hp + 1) * 128], memnT[:, kc, 0:MEM], start=kc == 0, stop=kc == KC - 1)
                S.copy("act", kT[:, hp, :], bk[:, 0:MEM])
            kvt = [pj[0], pj[1]]
            for mc in range(2):
                bk = nbank()
                for kc in range(KC):
                    S.mm(bk[:, :], memnT[:, kc, mc * 128:(mc + 1) * 128], wkv[:, kc, :], start=kc == 0, stop=kc == KC - 1)
                S.copy("act", kvt[mc], bk[:, :])
                S.copy("dve", Vb[:, mc, :], bk[:, 256:512])
                S.dma(sp, o_mk[mc * 128:(mc + 1) * 128, :], kvt[mc][:, 0:256], is_out=True)
                S.dma(sp, o_mv[mc * 128:(mc + 1) * 128, :], kvt[mc][:, 256:512], is_out=True)

        S.mark('kv_done')
        def pooling(xb4, s2v, s4v, s8v, s16v, outv, L, first):
            x = xb4
            S.tt("dve", s2v[:, :, :, 1:L], x[:, :, :, 1:L], x[:, :, :, 0:L - 1], ALU.add)
            S.tt("dve", s4v[64:128, 0, :, 3:L], s2v[64:128, 0, :, 3:L], s2v[64:128, 0, :, 1:L - 2], ALU.add)
            S.tt("dve", s4v[:, 1, :, 3:L], s2v[:, 1, :, 3:L], s2v[:, 1, :, 1:L - 2], ALU.add)
            S.tt("dve", s8v[:, :, 7:L], s4v[:, 1, :, 7:L], s4v[:, 1, :, 3:L - 4], ALU.add)
            S.tt("dve", s16v[64:128, :, 15:L], s8v[64:128, :, 15:L], s8v[64:128, :, 7:L - 8], ALU.add)
            srcs = [(0, 64, 0, s2v[0:64, 0], 0.5), (64, 128, 0, s4v[64:128, 0], 0.25),
                    (0, 64, 1, s8v[0:64], 0.125), (64, 128, 1, s16v[64:128], 1.0 / 16)]
            for p0, p1, c, sv, rw in srcs:
                S.stt("dve", outv[p0:p1, c, :, :], sv[:, :, 15:L], rw, x[p0:p1, c, :, 15:L], ALU.mult, ALU.subtract)
            if first:
                for p0, p1, c, sv, rw in srcs:
                    S.tt("dve", ptmp[p0:p1, c, :], sv[:, 0, 15:31], rcnt[p0:p1, c, :], ALU.mult)
                    S.tt("dve", outv[p0:p1, c, 0, 0:16], ptmp[p0:p1, c, :], x[p0:p1, c, 0, 15:31], ALU.subtract)

        def make_tiles(g):
            t = [((4 * g + i) % 5, 128, i * 128, i) for i in range(NG // 128)]
            if g == 0:
                t = [(4, NS, NG, -1)] + t
            return t

        def a_load(g, t):
            xi, rows, c0, pi = t
            if pi >= 0:
                S.dma(sp, X[:, xi, :], x_p[g * NG + pi * 128: g * NG + (pi + 1) * 128, :])
            else:
                S.dma(sp, X[0:NS, xi, :], x_s[:, :])

        def a_stages(tl):
            return norm_transpose_stages([(X[0:rows, xi, :], rows, c0) for (xi, rows, c0, pi) in tl], 0, xnT, junkA, xnbA, 0)

        def prefetch_win01():
            a = rs(ring_slot()[:, 0:KC * 512], KC, 512)
            load_w(a, w_in[:, 0:512])
            b = rs(ring_slot()[:, 0:KC * 512], KC, 512)
            load_w(b, w_in[:, 512:1024])
            return a, b

        class Stepper:
            def __init__(self, stages, n):
                self.stages, self.n, self.k = stages, n, 0

            def done(self):
                return self.k >= self.n + len(self.stages) - 1

            def step(self):
                ns = len(self.stages)
                for st in range(ns - 1, -1, -1):
                    i = self.k - st
                    if 0 <= i < self.n:
                        self.stages[st](i)
                self.k += 1

        for g in range(NGRP):
            first = g == 0
            last = g == NGRP - 1
            tiles = make_tiles(g)
            blocks = [(0, NG)]
            if first:
                blocks.append((NG, NS))

            if first:
                pipeline(a_stages(tiles), len(tiles))
                kv_part2()


            S.mark(f'g{g}_A_done')
            if first:
                wb = wb_first
                wb1 = wb1_first
            else:
                wb, wb1 = nxt_w01
            for (c0, n) in blocks:
                for fc in range(4):
                    bk = nbank()
                    for kc in range(KC):
                        S.mm(bk[:, 0:n], wb[:, kc, fc * 128:(fc + 1) * 128], xnT[:, kc, c0:c0 + n], start=kc == 0, stop=kc == KC - 1)
                    S.act(uT[:, fc, c0:c0 + n], bk[:, 0:n], AF.Gelu_apprx_tanh)

            S.mark(f'g{g}_B0_done')
            wb2 = rs(ring_slot()[:, 0:KC * 512], KC, 512)
            load_w(wb2, w_in[:, 1024:1536])
            b1 = {}
            SB1 = 48

            def b1_mm(ti):
                xi, rows, c0, pi = tiles[ti]
                bk = nbank()
                b1[ti] = bk
                for kc in range(KC):
                    S.mm(bk[0:rows, :], xnT[:, kc, c0:c0 + rows], wb1[:, kc, :], start=kc == 0, stop=kc == KC - 1)

            def b1_gelu(ti):
                xi, rows, c0, pi = tiles[ti]
                S.act(vg[ti % 4][0:rows], b1[ti][0:rows, :], AF.Gelu_apprx_tanh)

            def b1_stats(ti):
                xi, rows, c0, pi = tiles[ti]
                cb = SB1 + 9 * ti
                S.bn_stats(stats[0:rows, cb:cb + 6], vg[ti % 4][0:rows])
                S.bn_aggr(stats[0:rows, cb + 6:cb + 8], stats[0:rows, cb:cb + 6])

            def b1_rstd(ti):
                xi, rows, c0, pi = tiles[ti]
                cb = SB1 + 9 * ti
                rstd_pool(stats[:, cb + 8:cb + 9], stats[:, cb + 7:cb + 8], rows)

            def b1_norm(ti):
                xi, rows, c0, pi = tiles[ti]
                cb = SB1 + 9 * ti
                S.ts("dve", vn1[0:rows], vg[ti % 4][0:rows], stats[0:rows, cb + 6:cb + 7], stats[0:rows, cb + 8:cb + 9], ALU.subtract, ALU.mult)
                S.tt("dve", vn2[0:rows], vn1[0:rows], lng[0:rows], ALU.mult)
                if pi >= 0:
                    S.tt("dve", vnb[ti % 2], vn2, lnb, ALU.add)
                else:
                    S.tt("dve", vnS[0:NS], vn2[0:NS], lnb[0:NS], ALU.add)
                    S.dma(sp, o_v_s[:, :], vnS[0:NS, :], is_out=True)

            def b1_spatial(ti):
                xi, rows, c0, pi = tiles[ti]
                bk2 = nbank()
                b1[("sp", ti)] = bk2
                if pi >= 0:
                    vb = vnb[ti % 2]
                    for gch in range(4):
                        S.mm(bk2[:, gch * 128:(gch + 1) * 128], vb[:, gch * 128:(gch + 1) * 128], wsT[:, gch, :], start=True, stop=True)
                else:
                    for gch in range(4):
                        S.tr(bk2[:, gch * 64:(gch + 1) * 64], vnS[0:NS, gch * 128:(gch + 1) * 128], identf[0:NS, 0:NS])

            def b1_gate(ti):
                xi, rows, c0, pi = tiles[ti]
                bk2 = b1[("sp", ti)]
                if pi >= 0:
                    S.tt("dve", spt, rs(bk2[:, :], 4, 128), bsb, ALU.add)
                    S.tt("dve", uT[:, :, c0:c0 + 128], spt, uT[:, :, c0:c0 + 128], ALU.mult)
                else:
                    S.copy("dve", vnTs, rs(bk2[:, 0:256], 4, NB, 4))
                    for sidx in range(4):
                        for gch in range(4):
                            for t in range(sidx, 4):
                                acc = sgs[:, gch, :, t]
                                if sidx == 0:
                                    S.ts("dve", acc, vnTs[:, gch, :, 0], ws4[:, gch, t, 0:1], bs4[:, gch, t:t + 1], ALU.mult, ALU.add)
                                else:
                                    S.stt("dve", acc, vnTs[:, gch, :, sidx], ws4[:, gch, t, sidx:sidx + 1], acc, ALU.mult, ALU.add)
                    S.tt("dve", uT[:, :, NG:NG + NS], sgs.rearrange("p g b t -> p g (b t)"), uT[:, :, NG:NG + NS], ALU.mult)

            def b2_unit(c0, n, j):
                def f():
                    bk = nbank()
                    for kc in range(KC):
                        S.mm(bk[:, 0:n], wb2[:, kc, j * 128:(j + 1) * 128], xnT[:, kc, c0:c0 + n], start=kc == 0, stop=kc == KC - 1)
                    if j < 2:
                        if c0 == 0:
                            S.copy("act", bT[:, j, 15:15 + NG], bk[:, 0:NG])
                        else:
                            S.copy("act", bTs[:, j, :, 15:19], rs(bk[:, 0:NS], NB, 4))
                    else:
                        S.copy("act", qT[:, j - 2, c0:c0 + n], bk[:, 0:n])
                return f

            b2_units = [b2_unit(c0, n, j) for (c0, n) in blocks for j in range(4)]
            b1step = Stepper([b1_mm, b1_gelu, b1_stats, b1_rstd, b1_norm, b1_spatial, b1_gate], len(tiles))
            while not b1step.done():
                b1step.step()
                if b2_units and b1step.k >= 5:
                    b2_units.pop(0)()
            aT = uT

            def gate_block(i):
                wg = rs(ring_slot()[:, 0:KC * 768], KC, 3, 256)
                for j in range(3):
                    c = 1536 + j * 1024 + i * 256
                    load_w(wg[:, :, j, :], w_in[:, c:c + 256])
                return wg

            S.mark(f'g{g}_B1_done')
            while b2_units:
                b2_units.pop(0)()
            wg_next = gate_block(0)
            load_w(wba, w_branch_a)
            load_w(wbb, w_branch_b)
            load_w(wbc, w_branch_c)

            S.mark(f'g{g}_B2_done')
            S.mark(f'g{g}_pool_done')
            sb = {}

            def att_scores(h):
                hp, base = h // 2, (h % 2) * 64
                ex = expT[h % 2]
                for mc in range(2):
                    bk = nbank()
                    S.mm(bk[:, 0:NG], kT[base:base + 64, hp, mc * 128:(mc + 1) * 128], qT[base:base + 64, hp, 0:NG], start=True, stop=True)
                    S.act(ex[:, mc, :], bk[:, 0:NG], AF.Exp, scale=0.125)

            def att_pv(h):
                hp, base = h // 2, (h % 2) * 64
                ex = expT[h % 2]
                pv = nbank()
                for mc in range(2):
                    S.mm(pv[:, 0:NG], Vb[:, mc, hp * 128:(hp + 1) * 128], ex[:, mc, :], start=mc == 0, stop=mc == 1)
                dn = nbank()
                for mc in range(2):
                    S.mm(dn[:, 0:NG], onesb, ex[:, mc, :], start=mc == 0, stop=mc == 1)
                r = rd[h % 2]
                S.recip(r[base:base + 64, :], dn[base:base + 64, 0:NG])
                S.tt("dve", attnT[base:base + 64, hp, 0:NG], pv[base:base + 64, 0:NG], r[base:base + 64, :], ALU.mult)

            for i in range(5):
                if i < 4:
                    att_scores(i)
                if i >= 1:
                    att_pv(i - 1)

            L = 15 + NG
            S.copy("dve", bT[:, :, 0:15], halo_b)
            pooling(bT.rearrange("p c (b l) -> p c b l", b=1), s2.rearrange("p c (b l) -> p c b l", b=1),
                    s4.rearrange("p c (b l) -> p c b l", b=1), s8.rearrange("p (b l) -> p b l", b=1),
                    s16.rearrange("p (b l) -> p b l", b=1),
                    pooledT[:, :, 0:NG].rearrange("p c (b l) -> p c b l", b=1), L, first)
            if last:
                S.copy("dve", PSt[:, :, 64:79], bT[:, :, L - 15:L])
            else:
                S.copy("dve", halo_b, bT[:, :, L - 15:L])
            if first:
                s2s = rs(s2.rearrange("p c l -> p (c l)")[:, 0:2 * NB * 19], 2, NB, 19)
                s4s = rs(s4.rearrange("p c l -> p (c l)")[:, 0:2 * NB * 19], 2, NB, 19)
                s8s = rs(s8[:, 0:NB * 19], NB, 19)
                s16s = rs(s16[:, 0:NB * 19], NB, 19)
                pooling(bTs, s2s, s4s, s8s, s16s, pooledT[:, :, NG:NG + NS].rearrange("p c (b t) -> p c b t", b=NB), 19, False)
                S.copy("dve", PSt[:, :, 0:64].rearrange("p c (t b) -> p c b t", t=4), bTs[:, :, :, 15:19])

            def glin():
                for (c0, n) in blocks:
                    for c in range(2):
                        bk = nbank()
                        S.mm(bk[:, 0:n], Wbd[:, c, :], pooledT[:, c, c0:c0 + n], start=True, stop=True)
                        S.act(mixedT[:, c, c0:c0 + n], bk[:, 0:n], AF.Copy, scale=V1T[:, 40 + c:41 + c])

            glin_pending = [glin]

            S.mark(f'g{g}_attp_done')
            if first:
                scol = NG

                def s_load(b):
                    S.dma("pool", Ksb[b % 2], ck[b].rearrange("(mc m) f -> m mc f", m=128))
                    S.dma("pool", Vsb[b % 2], cv[b].rearrange("(mc m) f -> m mc f", m=128))

                def s_tr(b):
                    bk = nbank()
                    bkb = bk[:].bitcast(BF16)
                    for hp in range(2):
                        for mc in range(2):
                            S.tr(bkb[:, hp * 256 + mc * 128: hp * 256 + (mc + 1) * 128], Ksb[b % 2][:, mc, hp * 128:(hp + 1) * 128], identb)
                    S.copy("dve", kTb[b % 2], rs(bkb[:, 0:512], 2, 256))

                bank_n[0] = 6
                OS = banks[6]
                DS = banks[7]
                OSv = rs(OS[:, 0:256], NB, 4, 4)
                DSv = rs(DS[:, 0:256], NB, 4, 4)

                def s_scores(b):
                    bkp = [nbank(), nbank()]
                    ex5 = expS[b % 2].rearrange("p mc (hp par t) -> p mc hp par t", hp=2, par=2)
                    for par in range(2):
                        base = par * 64
                        sv = rs(bkp[par][:, 0:16], 2, 2, 4)
                        for hp in range(2):
                            for mc in range(2):
                                S.mm(sv[:, mc, hp, :], kTb[b % 2][base:base + 64, hp, mc * 128:(mc + 1) * 128],
                                     qT[base:base + 64, hp, scol + b * 4: scol + b * 4 + 4], start=True, stop=True)
                    for par in range(2):
                        S.act(ex5[:, :, :, par, :], rs(bkp[par][:, 0:16], 2, 2, 4), AF.Exp, scale=0.125)

                def s_pv(b):
                    ex = rs(expS[b % 2].rearrange("p mc x -> p (mc x)"), 2, 4, 4)
                    for h in range(4):
                        hp = h // 2
                        for mc in range(2):
                            S.mm(OSv[:, b, h, :], Vsb[b % 2][:, mc, hp * 128:(hp + 1) * 128], ex[:, mc, h, :], start=mc == 0, stop=mc == 1)
                    for mc in range(2):
                        S.mm(DSv[:, b, :, :].rearrange("p h t -> p (h t)"), onesb, expS[b % 2][:, mc, :], start=mc == 0, stop=mc == 1)

                s_load(0)
                s_load(1)
                s_tr(0)
                for b in range(NB):
                    if b + 1 < NB:
                        s_tr(b + 1)
                    s_scores(b)
                    s_pv(b)
                    if b + 2 < NB:
                        s_load(b + 2)
                S.recip(rdS.rearrange("p b x -> p (b x)"), DS[:, 0:256])
                rdv = rs(rdS.rearrange("p b x -> p (b x)"), NB, 4, 4)
                for h in range(4):
                    hp, base = h // 2, (h % 2) * 64
                    S.tt("dve", attnT[base:base + 64, hp, scol:scol + NS].rearrange("p (b t) -> p b t", b=NB),
                         OSv[base:base + 64, :, h, :], rdv[base:base + 64, :, h, :], ALU.mult)
                bank_n[0] = 8

            S.mark(f'g{g}_atts_done')
            load_w(wout, w_out)
            for i in range(4):
                wg = wg_next
                if i + 1 < 4:
                    wg_next = gate_block(i + 1)
                for fl in range(2):
                    fo = 2 * i + fl
                    for (c0, n) in blocks:
                        gb = []
                        for j in range(3):
                            bk = nbank()
                            for kc in range(KC):
                                S.mm(bk[:, 0:n], wg[:, kc, j, fl * 128:(fl + 1) * 128], xnT[:, kc, c0:c0 + n], start=kc == 0, stop=kc == KC - 1)
                            S.act(tg[j][:, 0:n], bk[:, 0:n], AF.Tanh, bias=hb[:, j * 8 + fo: j * 8 + fo + 1], scale=0.5)
                            gb.append(bk)
                        if glin_pending:
                            glin_pending.pop()()
                        srcs = [(wba, aT, 4), (wbb, mixedT, 2), (wbc, attnT, 2)]
                        for j, (wbr, actv, nk) in enumerate(srcs):
                            bk = nbank()
                            for kc in range(nk):
                                S.mm(bk[:, 0:n], wbr[:, kc, fo * 128:(fo + 1) * 128], actv[:, kc, c0:c0 + n], start=kc == 0, stop=kc == nk - 1)
                            S.stt("dve", pj[j][:, 0:n], tg[j][:, 0:n], 1.0, bk[:, 0:n], ALU.add, ALU.mult)
                        S.tt("pool", pj[0][:, 0:n], pj[0][:, 0:n], pj[1][:, 0:n], ALU.add)
                        S.tt("pool", mergedT[:, fo, c0:c0 + n], pj[0][:, 0:n], pj[2][:, 0:n], ALU.add)

            S.mark(f'g{g}_C_done')
            def up_block(jb):
                wu = rs(ring_slot()[:, 0:KC * 512], KC, 2, 256)
                load_w(wu[:, :, 0, :], w_up[:, jb * 256:(jb + 1) * 256])
                load_w(wu[:, :, 1, :], w_up[:, DFF + jb * 256: DFF + (jb + 1) * 256])
                return wu

            wu_first = up_block(0)
            for k0 in (0, 5, 10):
                k1 = min(k0 + 5, 14)
                load_w(wdn[:, k0:k1, :], w_down[k0 * 128:k1 * 128, :])
            dd = {}
            SD = 96

            def d_mm(ti):
                xi, rows, c0, pi = tiles[ti]
                bks = []
                for half in range(2):
                    bk = nbank()
                    for kc in range(KC):
                        S.mm(bk[0:rows, :], mergedT[:, kc, c0:c0 + rows], wout[:, kc, half * 512:(half + 1) * 512], start=kc == 0, stop=kc == KC - 1)
                    bks.append(bk)
                dd[ti] = bks

            def d_sq(ti):
                xi, rows, c0, pi = tiles[ti]
                for half in range(2):
                    S.act(junk2[0:rows, 0:512], dd[ti][half][0:rows, :], AF.Square, scale=1.0 / 64.0,
                          accum_out=stats[0:rows, SD + 3 * ti + half: SD + 3 * ti + half + 1])

            def d_rstd(ti):
                xi, rows, c0, pi = tiles[ti]
                racc = stats[:, SD + 3 * ti + 2: SD + 3 * ti + 3]
                S.tt("pool", racc[0:rows], stats[0:rows, SD + 3 * ti: SD + 3 * ti + 1], stats[0:rows, SD + 3 * ti + 1: SD + 3 * ti + 2], ALU.add)
                rstd_pool(racc, racc, rows)

            def d_res(ti):
                xi, rows, c0, pi = tiles[ti]
                racc = stats[:, SD + 3 * ti + 2: SD + 3 * ti + 3]
                for half in range(2):
                    dt_ = dtmp[half]
                    S.stt("dve", dt_[0:rows], dd[ti][half][0:rows, :], racc[0:rows], gpost1h[0:rows, half * 512:(half + 1) * 512], ALU.mult, ALU.mult)
                for half in range(2):
                    S.tt("dve" if half == 0 else "pool", X[0:rows, xi, half * 512:(half + 1) * 512], X[0:rows, xi, half * 512:(half + 1) * 512], dtmp[half][0:rows], ALU.add)

            hn_stages = norm_transpose_stages([(X[0:rows, xi, :], rows, c0) for (xi, rows, c0, pi) in tiles], 8, xnT, junk2, hnb, 16, gb=gffnb)
            def d_sq_rstd(ti):
                d_sq(ti)
                d_rstd(ti)

            def d_res_sq(ti):
                d_res(ti)
                hn_stages[0](ti)

            pipeline([d_mm, d_sq_rstd, d_res_sq] + hn_stages[1:], len(tiles))
            hnT = xnT

            S.mark(f'g{g}_D_done')
            if not last:
                ntiles = make_tiles(g + 1)
                nloaded = 0
                cur_slots = [t[0] for t in tiles]
                while nloaded < len(ntiles) and ntiles[nloaded][0] not in cur_slots:
                    a_load(g + 1, ntiles[nloaded])
                    nloaded += 1
                astep = Stepper(a_stages(ntiles), len(ntiles))
            wu_next = wu_first
            for jb in range(FC // 2):
                wu = wu_next
                if jb + 1 < FC // 2:
                    wu_next = up_block(jb + 1)
                if jb < 4:
                    k0 = 14 + 2 * jb
                    load_w(wdn[:, k0:k0 + 2, :], w_down[k0 * 128:(k0 + 2) * 128, :])
                for fl in range(2):
                    fc = 2 * jb + fl
                    cw0 = V2T[:, fc:fc + 1]
                    cw1 = V2T[:, 22 + fc:23 + fc]
                    cw2 = V2T[:, 44 + fc:45 + fc]
                    cbv = V1T[:, 42 + fc:43 + fc]
                    for (c0, n) in blocks:
                        gbk = nbank()
                        for kc in range(KC):
                            S.mm(gbk[:, 0:n], wu[:, kc, 0, fl * 128:(fl + 1) * 128], hnT[:, kc, c0:c0 + n], start=kc == 0, stop=kc == KC - 1)
                        ubk = nbank()
                        for kc in range(KC):
                            S.mm(ubk[:, 0:n], wu[:, kc, 1, fl * 128:(fl + 1) * 128], hnT[:, kc, c0:c0 + n], start=kc == 0, stop=kc == KC - 1)
                        if c0 == 0:
                            gt = gTp[fc % 2]
                            ct = cT[fc % 2]
                            gg = ge[fc % 2]
                            S.copy("pool", gt[:, 0:2], halo_g[:, fc, :])
                            S.copy("act", gt[:, 2:2 + NG], gbk[:, 0:NG])
                            if last:
                                S.copy("pool", CS[:, fc, 32:34], gt[:, NG:NG + 2])
                            else:
                                S.copy("pool", halo_g[:, fc, :], gt[:, NG:NG + 2])
                            S.ts("dve", ct, gt[:, 2:2 + NG], cw2, cbv, ALU.mult, ALU.add)
                            S.stt("dve", ct, gt[:, 1:1 + NG], cw1, ct, ALU.mult, ALU.add)
                            S.stt("dve", ct, gt[:, 0:NG], cw0, ct, ALU.mult, ALU.add)
                            S.act(gg, ct, AF.Gelu_apprx_tanh)
                            S.tt("dve", actT[:, fc, 0:NG], ubk[:, 0:NG], gg, ALU.mult)
                        else:
                            S.copy("act", gTs[:, fc, :, 2:6], rs(gbk[:, 0:NS], NB, 4))
                            S.copy("pool", CS[:, fc, 0:32].rearrange("p (r b) -> p b r", r=2), gTs[:, fc, :, 4:6])
                            S.ts("dve", cTs, gTs[:, fc, :, 2:6], cw2, cbv, ALU.mult, ALU.add)
                            S.stt("dve", cTs, gTs[:, fc, :, 1:5], cw1, cTs, ALU.mult, ALU.add)
                            S.stt("dve", cTs, gTs[:, fc, :, 0:4], cw0, cTs, ALU.mult, ALU.add)
                            S.act(geS, cTs, AF.Gelu_apprx_tanh)
                            S.tt("dve", actT[:, fc, NG:NG + NS], ubk[:, 0:NS], geS.rearrange("p b t -> p (b t)"), ALU.mult)

            S.mark(f'g{g}_E_done')
            if not last:
                nxt_w01 = prefetch_win01()
            for ti, (xi, rows, c0, pi) in enumerate(tiles):
                acc2 = stats[:, 40:42]
                racc = stats[:, 42:43]
                bks = []
                for half in range(2):
                    bk = nbank()
                    for kc in range(FC):
                        S.mm(bk[0:rows, :], actT[:, kc, c0:c0 + rows], wdn[:, kc, half * 512:(half + 1) * 512], start=kc == 0, stop=kc == FC - 1)
                    S.act(junk3[0:rows], bk[0:rows, :], AF.Square, scale=1.0 / 32.0, accum_out=acc2[0:rows, half:half + 1])
                    bks.append(bk)
                S.tt("pool", racc[0:rows], acc2[0:rows, 0:1], acc2[0:rows, 1:2], ALU.add)
                rstd_pool(racc, racc, rows)
                for half in range(2):
                    ft = ftmp[half]
                    S.stt("dve", ft[0:rows], bks[half][0:rows, :], racc[0:rows], gpost2[0:rows, half * 512:(half + 1) * 512], ALU.mult, ALU.mult)
                    S.tt("pool", X[0:rows, xi, half * 512:(half + 1) * 512], X[0:rows, xi, half * 512:(half + 1) * 512], ft[0:rows], ALU.add)
                if pi >= 0:
                    S.dma(sp, y_p[g * NG + pi * 128: g * NG + (pi + 1) * 128, :], X[:, xi, :], is_out=True)
                else:
                    S.dma(sp, y_s[:, :], X[0:NS, xi, :], is_out=True)
                if not last:
                    while nloaded < len(ntiles) and ntiles[nloaded][0] in [t[0] for t in tiles[:ti + 1]] + [x for x in range(5) if x not in cur_slots]:
                        a_load(g + 1, ntiles[nloaded])
                        nloaded += 1
                    while (not astep.done()) and min(astep.k, astep.n - 1) < nloaded:
                        astep.step()
                        if astep.k <= astep.n:
                            break
            if not last:
                assert nloaded == len(ntiles)
                while not astep.done():
                    astep.step()

        S.mark('groups_done')
        for c in range(2):
            bk = nbank()
            S.tr(bk[0:79, 0:128], PSt[:, c, 0:79], identf)
            S.copy("dve", PStT[0:79, c * 128:(c + 1) * 128], bk[0:79, 0:128])
        for t in range(4):
            S.dma(sp, o_pool_s[:, 11 + t, :], PStT[t * 16:(t + 1) * 16, :], is_out=True)
        S.dma(sp, o_pool_p[:, :], PStT[64:79, :], is_out=True)
        for q in range(6):
            bk = nbank()
            nfc = 4 if q < 5 else 2
            for j in range(nfc):
                fc = q * 4 + j
                S.tr(bk[0:34, j * 128:(j + 1) * 128], CS[:, fc, :], identf)
            S.copy("act" if q % 2 else "dve", CST[0:34, q * 512: q * 512 + nfc * 128], bk[0:34, 0:nfc * 128])
        for r in range(2):
            S.dma(sp, o_conv_s[:, r, :], CST[r * 16:(r + 1) * 16, :], is_out=True)
        S.dma(sp, o_conv_p[:, :], CST[32:34, :], is_out=True)

        import os
        lim = int(os.environ.get("KSTOP", "0")) or None
        if os.environ.get("KMARKS"):
            print("MARKS", S.marks, "total", len(S.ops))
        S.emit(limit=lim)
    return nc


_CACHE = {}


def _consts():
    identb = np.eye(128, dtype=np.float32).astype(ml_dtypes.bfloat16)
    identf = np.eye(128, dtype=np.float32)
    s = np.arange(128)
    maskT = (s[:, None] <= s[None, :]).astype(np.float32)
    rc = np.zeros((128, 2, 16), np.float32)
    wins = {(0, 0): 2, (1, 0): 4, (0, 1): 8, (1, 1): 16}
    for p in range(128):
        for c in range(2):
            w = wins[(p // 64, c)]
            for t in range(16):
                rc[p, c, t] = 1.0 / min(w, t + 1)
    return identb, identf, maskT, rc.reshape(128, 32)


def make_in_maps(inputs):
    f = lambda k: np.ascontiguousarray(np.asarray(inputs[k], dtype=np.float32))
    x_prompt = f("x_prompt")
    x_sample = f("x_sample")
    mem_prompt = f("mem_prompt")
    cache_k = f("cache_mem_k")[0].reshape(128, MEM, 256)
    cache_v = f("cache_mem_v")[0].reshape(128, MEM, 256)
    state_pool = f("state_pool")[0]
    state_conv = f("state_conv")[0]
    identb, identf, maskT, rcnt = _consts()
    shared = {
        "norm_mix_pre": f("norm_mix_pre"), "norm_mix_post": f("norm_mix_post"),
        "norm_ffn_pre": f("norm_ffn_pre"), "norm_ffn_post": f("norm_ffn_post"),
        "w_in": f("w_in")[0], "b_gate": f("b_gate"),
        "gmlp_ln_g": f("gmlp_ln_g"), "gmlp_ln_b": f("gmlp_ln_b"),
        "w_spatial": f("w_spatial")[0], "b_spatial": f("b_spatial")[0],
        "w_pool": f("w_pool")[0], "pool_scale": f("pool_scale"),
        "mem_norm": f("mem_norm"), "w_mem_kv": f("w_mem_kv")[0],
        "w_branch_a": f("w_branch_a")[0], "w_branch_b": f("w_branch_b")[0], "w_branch_c": f("w_branch_c")[0],
        "w_out": f("w_out")[0], "w_up": f("w_up")[0], "conv_w": f("conv_w")[0], "conv_b": f("conv_b"),
        "w_down": f("w_down")[0],
        "c_identb": identb, "c_identf": identf, "c_maskT": maskT, "c_rcnt": rcnt,
    }
    in_maps = []
    for c in range(NCORES):
        m = dict(shared)
        m["x_p"] = x_prompt[c]
        m["x_s"] = np.ascontiguousarray(x_sample[c * NB:(c + 1) * NB].reshape(NS, D))
        m["mem"] = mem_prompt[c]
        m["ck"] = np.ascontiguousarray(cache_k[c * NB:(c + 1) * NB])
        m["cv"] = np.ascontiguousarray(cache_v[c * NB:(c + 1) * NB])
        m["st_pool"] = np.ascontiguousarray(state_pool[c * NB:(c + 1) * NB])
        m["st_conv"] = np.ascontiguousarray(state_conv[c * NB:(c + 1) * NB])
        in_maps.append(m)
    return in_maps


def kernel(**inputs):
    in_maps = make_in_maps(inputs)
    if "nc" not in _CACHE:
        _CACHE["nc"] = build_program()
    nc = _CACHE["nc"]
    res = run_bass_kernel_spmd(nc, in_maps, core_ids=list(range(NCORES)))
    R = res.results
    y_prompt = np.stack([R[c]["y_p"] for c in range(NCORES)]).astype(np.float32)
    y_sample = np.concatenate([R[c]["y_s"].reshape(NB, 4, D) for c in range(NCORES)]).astype(np.float32)
    mk = np.stack([R[c]["o_mk"].reshape(MEM, 4, 64) for c in range(NCORES)])[None].astype(np.float32)
    mv = np.stack([R[c]["o_mv"].reshape(MEM, 4, 64) for c in range(NCORES)])[None].astype(np.float32)
    pool_p = np.stack([R[c]["o_pool_p"] for c in range(NCORES)])[None].astype(np.float32)
    conv_p = np.stack([R[c]["o_conv_p"] for c in range(NCORES)])[None].astype(np.float32)
    v_s = np.concatenate([R[c]["o_v_s"].reshape(NB, 4, AW) for c in range(NCORES)])[None].astype(np.float32)
    pool_s = np.concatenate([R[c]["o_pool_s"] for c in range(NCORES)])[None].astype(np.float32)
    conv_s = np.concatenate([R[c]["o_conv_s"] for c in range(NCORES)])[None].astype(np.float32)
    return (y_prompt, y_sample, mk, mv, pool_p, conv_p, v_s, pool_s, conv_s)
```

```python
import contextlib
from math import prod

import numpy as np
import ml_dtypes

import concourse.bass as bass
import concourse.mybir as mybir
from concourse.bass_utils import run_bass_kernel_spmd

F32 = mybir.dt.float32
BF16 = mybir.dt.bfloat16
AF = mybir.ActivationFunctionType
ALU = mybir.AluOpType
DTSZ = {F32: 4, BF16: 2}

import os
EVAC_ENGS = os.environ.get('KEVAC', 'dve,dve').split(',')
SAME_ENG_GAP = int(os.environ.get('KGAP', '2'))
NCORES = 8
D = 1024
KC = 8
SEQ = 2048
NG = 512
NGRP = SEQ // NG
NS = 64
NB = 16
NCOL = NG + NS
AW = 512
BW = 256
DFF = 2816
FC = 22
MEM = 256
EPS = 1e-6
D_IN = 4608


class Op:
    __slots__ = ("eng", "fn", "deps", "signal", "sigval", "dma", "sem", "semval", "prevval", "is_out", "idx", "lidx")

    def __init__(self, eng, fn, dma=False):
        self.eng = eng
        self.fn = fn
        self.deps = set()
        self.signal = False
        self.sigval = 0
        self.dma = dma
        self.sem = None
        self.semval = 0
        self.prevval = 0
        self.is_out = False
        self.idx = 0
        self.lidx = 0


class Sched:
    def __init__(self, nc, es):
        self.nc = nc
        self.ops = []
        self.rowb = {}
        self.wrec = {}
        self.rrec = {}
        self.engs = {"pe": nc.tensor, "act": nc.scalar, "dve": nc.vector, "pool": nc.gpsimd, "sp": nc.sync}
        self.esem = {k: es.enter_context(nc.semaphore("s_" + k)) for k in ("pe", "act", "dve", "pool")}
        self.ndsem = 12
        self.dsem = {q: [es.enter_context(nc.semaphore(f"d_{q}{i}")) for i in range(self.ndsem)] for q in ("sp", "pool")}
        self.dcount = {"sp": 0, "pool": 0}
        self.lcount = {}
        self.marks = []

    def mark(self, name):
        self.marks.append((name, len(self.ops)))

    def region(self, ap):
        name = ap.name
        rb = self.rowb.get(name)
        if rb is None:
            return None
        if name.startswith("ps"):
            return name, 0, 128, [(0, 2048)]
        esz = DTSZ[ap.dtype]
        offb = ap.offset * esz
        pat = list(ap.ap)
        p0 = offb // rb
        f0 = offb % rb
        if pat[0][0] * esz == rb:
            pc = pat[0][1]
            dims = pat[1:]
        else:
            pc = 1
            dims = pat
        dims = [(s * esz, c) for s, c in dims if c > 1 and s != 0]
        if not dims:
            ivs = [(f0, f0 + esz)]
        else:
            inner = dims[-1]
            outer = dims[:-1]
            run = (inner[1] - 1) * inner[0] + esz
            nouter = prod(c for _, c in outer) if outer else 1
            if nouter <= 64:
                offs = [0]
                for s, c in outer:
                    offs = [o + i * s for o in offs for i in range(c)]
                ivs = sorted((f0 + o, f0 + o + run) for o in offs)
                merged = [ivs[0]]
                for lo, hi in ivs[1:]:
                    if lo <= merged[-1][1]:
                        merged[-1] = (merged[-1][0], max(hi, merged[-1][1]))
                    else:
                        merged.append((lo, hi))
                ivs = merged
            else:
                ivs = [(f0, f0 + sum((c - 1) * s for s, c in dims) + esz)]
        return name, p0, p0 + pc, ivs

    def _add_dep(self, op, prod_op, kind):
        if prod_op is op:
            return
        if not op.dma and not prod_op.dma and op.eng == prod_op.eng:
            if op.eng == "pe":
                return
            if kind != "raw":
                return
        op.deps.add(prod_op)

    def _read(self, op, ap):
        r = self.region(ap)
        if r is None:
            return
        name, p0, p1, ivs = r
        wl = self.wrec.setdefault(name, [])
        for rec in wl:
            if rec[0] < p1 and p0 < rec[1]:
                for lo, hi in ivs:
                    if rec[2] < hi and lo < rec[3]:
                        self._add_dep(op, rec[4], "raw")
                        break
        rl = self.rrec.setdefault(name, [])
        if name.startswith("ps"):
            for rec in rl:
                if rec[4].eng != op.eng:
                    self._add_dep(op, rec[4], "rar")
        for lo, hi in ivs:
            if not op.dma:
                for i, rec in enumerate(rl):
                    if rec[0] == p0 and rec[1] == p1 and rec[2] == lo and rec[3] == hi and (not rec[4].dma) and rec[4].eng == op.eng:
                        rl[i] = (p0, p1, lo, hi, op)
                        break
                else:
                    rl.append((p0, p1, lo, hi, op))
            else:
                rl.append((p0, p1, lo, hi, op))

    def _write(self, op, ap):
        r = self.region(ap)
        if r is None:
            return
        name, p0, p1, ivs = r
        wl = self.wrec.setdefault(name, [])
        rl = self.rrec.setdefault(name, [])
        for lst, kind in ((wl, "waw"), (rl, "war")):
            keep = []
            for rec in lst:
                hit = False
                contained = False
                if rec[0] < p1 and p0 < rec[1]:
                    for lo, hi in ivs:
                        if rec[2] < hi and lo < rec[3]:
                            hit = True
                            if lo <= rec[2] and rec[3] <= hi and p0 <= rec[0] and rec[1] <= p1:
                                contained = True
                            break
                if hit:
                    self._add_dep(op, rec[4], kind)
                if not contained:
                    keep.append(rec)
            lst[:] = keep
        for lo, hi in ivs:
            wl.append((p0, p1, lo, hi, op))

    def add(self, eng, fn, reads, writes, dma=False):
        op = Op(eng, fn, dma)
        op.idx = len(self.ops)
        if not dma:
            op.lidx = self.lcount.get(eng, 0)
            self.lcount[eng] = op.lidx + 1
        for ap in reads:
            if ap is not None and not isinstance(ap, (int, float)):
                self._read(op, ap)
        for ap in writes:
            if ap is not None:
                self._write(op, ap)
        latest = {}
        keep = set()
        for p in op.deps:
            if p.dma:
                keep.add(p)
            elif p.eng not in latest or latest[p.eng].idx < p.idx:
                latest[p.eng] = p
        for e, p in latest.items():
            if (not dma) and e == eng and op.lidx - p.lidx > SAME_ENG_GAP:
                continue
            keep.add(p)
            p.signal = True
        op.deps = keep
        if dma:
            k = self.dcount[eng]
            self.dcount[eng] = k + 1
            op.sem = self.dsem[eng][k % self.ndsem]
            op.semval = 16 * (k // self.ndsem + 1)
            op.prevval = 16 * (k // self.ndsem)
        self.ops.append(op)
        return op

    def mm(self, out, lhsT, rhs, start=True, stop=True):
        return self.add("pe", lambda e: e.matmul(out, lhsT, rhs, start=start, stop=stop), [lhsT, rhs], [out])

    def tr(self, out, in_, ident):
        return self.add("pe", lambda e: e.transpose(out, in_, ident), [in_, ident], [out])

    def act(self, out, in_, func, bias=None, scale=None, accum_out=None):
        kw = {}
        if bias is not None:
            kw["bias"] = bias
        if scale is not None:
            kw["scale"] = scale
        if accum_out is not None:
            kw["accum_out"] = accum_out
        if func == AF.Copy and any(v is not None and not isinstance(v, (int, float)) for v in (bias, scale)):
            func = AF.Identity
        return self.add("act", lambda e: e.activation(out=out, in_=in_, func=func, **kw), [in_, bias, scale], [out, accum_out])

    def tt(self, eng, out, in0, in1, op):
        return self.add(eng, lambda e: e.tensor_tensor(out=out, in0=in0, in1=in1, op=op), [in0, in1], [out])

    def ts(self, eng, out, in0, s1, s2, op0, op1=None):
        if op1 is None:
            return self.add(eng, lambda e: e.tensor_scalar(out=out, in0=in0, scalar1=s1, scalar2=None, op0=op0), [in0, s1], [out])
        return self.add(eng, lambda e: e.tensor_scalar(out=out, in0=in0, scalar1=s1, scalar2=s2, op0=op0, op1=op1), [in0, s1, s2], [out])

    def stt(self, eng, out, in0, scalar, in1, op0, op1):
        return self.add(eng, lambda e: e.scalar_tensor_tensor(out=out, in0=in0, scalar=scalar, in1=in1, op0=op0, op1=op1), [in0, scalar, in1], [out])

    def copy(self, eng, out, in_):
        if eng == "act":
            return self.act(out, in_, AF.Copy)
        return self.add(eng, lambda e: e.tensor_copy(out=out, in_=in_), [in_], [out])

    def memset(self, eng, out, val):
        return self.add(eng, lambda e: e.memset(out, val), [], [out])

    def recip(self, out, in_):
        return self.add("dve", lambda e: e.reciprocal(out=out, in_=in_), [in_], [out])

    def bn_stats(self, out, in_):
        return self.add("dve", lambda e: e.bn_stats(out=out, in_=in_), [in_], [out])

    def bn_aggr(self, out, in_):
        return self.add("dve", lambda e: e.bn_aggr(out=out, in_=in_), [in_], [out])

    def dma(self, q, out, in_, is_out=False):
        op = self.add(q, lambda e: e.dma_start(out=out, in_=in_), [in_], [out], dma=True)
        op.is_out = is_out
        return op

    def emit(self, limit=None):
        if limit:
            self.ops = self.ops[:limit]
            for op in self.ops:
                if not op.dma:
                    op.signal = False
            live = set(map(id, self.ops))
            for op in self.ops:
                for p in op.deps:
                    if not p.dma:
                        p.signal = True
        cnt = {k: 0 for k in self.esem}
        for op in self.ops:
            if not op.dma and op.signal:
                cnt[op.eng] += 1
                op.sigval = cnt[op.eng]
        waited = {k: {} for k in self.engs}
        outs = []

        def wait(engname, sem, val):
            w = waited[engname]
            key = id(sem)
            if w.get(key, 0) >= val:
                return
            w[key] = val
            self.engs[engname].wait_ge(sem, val)

        for op in self.ops:
            need = {}
            for p in op.deps:
                if p.dma:
                    sem, val = p.sem, p.semval
                else:
                    sem, val = self.esem[p.eng], p.sigval
                k = id(sem)
                if k not in need or need[k][1] < val:
                    need[k] = (sem, val)
            if op.dma and op.prevval > 0:
                k = id(op.sem)
                if k not in need or need[k][1] < op.prevval:
                    need[k] = (op.sem, op.prevval)
            for sem, val in need.values():
                wait(op.eng, sem, val)
            inst = op.fn(self.engs[op.eng])
            if op.dma:
                inst.then_inc(op.sem, 16)
                if op.is_out:
                    outs.append(op)
            elif op.signal:
                inst.then_inc(self.esem[op.eng], 1)
        for engname in ("sp", "act", "pool"):
            for op in outs:
                wait(engname, op.sem, op.semval)


def rs(ap, *dims):
    names = "abcdefgh"[: len(dims)]
    pat = "p (" + " ".join(names) + ") -> p " + " ".join(names)
    return ap.rearrange(pat, **{n: d for n, d in zip(names, dims)})


class Arena:
    def __init__(self, nc, es, S, name, nbytes):
        self.nbytes = nbytes
        self.t = es.enter_context(nc.sbuf_tensor(name, [128, nbytes // 2], BF16))
        S.rowb[name] = nbytes
        self.off = 0
        self.hi = 0

    def at(self, off, dtype, *shape):
        n = prod(shape)
        nb = n * DTSZ[dtype]
        assert off % 4 == 0 and off + nb <= self.nbytes, (off, nb, self.nbytes)
        a = self.t[:, off // 2: (off + nb) // 2]
        if dtype != BF16:
            a = a.bitcast(dtype)
        if len(shape) > 1:
            a = rs(a, *shape)
        return a

    def alloc(self, dtype, *shape):
        nb = prod(shape) * DTSZ[dtype]
        off = self.off
        self.off = (off + nb + 63) // 64 * 64
        self.hi = max(self.hi, self.off)
        return self.at(off, dtype, *shape)


def build_program():
    nc = bass.Bass("TRN2", target_bir_lowering=False)
    es = contextlib.ExitStack()

    def din(name, shape, dt=F32):
        return nc.dram_tensor(name, list(shape), dt, kind="ExternalInput").ap()

    def dout(name, shape):
        return nc.dram_tensor(name, list(shape), F32, kind="ExternalOutput").ap()

    x_p = din("x_p", [SEQ, D])
    x_s = din("x_s", [NS, D])
    mem = din("mem", [MEM, D])
    ck = din("ck", [NB, MEM, 256])
    cv = din("cv", [NB, MEM, 256])
    st_pool = din("st_pool", [NB, 15, BW])
    st_conv = din("st_conv", [NB, 2, DFF])
    norm_mix_pre = din("norm_mix_pre", [1, D])
    norm_mix_post = din("norm_mix_post", [1, D])
    norm_ffn_pre = din("norm_ffn_pre", [1, D])
    norm_ffn_post = din("norm_ffn_post", [1, D])
    w_in = din("w_in", [D, D_IN])
    b_gate = din("b_gate", [1, 3 * D])
    gmlp_ln_g = din("gmlp_ln_g", [1, AW])
    gmlp_ln_b = din("gmlp_ln_b", [1, AW])
    w_spatial = din("w_spatial", [4, 128, 128])
    b_spatial = din("b_spatial", [4, 128])
    w_pool = din("w_pool", [4, 64, 64])
    pool_scale = din("pool_scale", [1, BW])
    mem_norm = din("mem_norm", [1, D])
    w_mem_kv = din("w_mem_kv", [D, 512])
    w_branch_a = din("w_branch_a", [AW, D])
    w_branch_b = din("w_branch_b", [BW, D])
    w_branch_c = din("w_branch_c", [BW, D])
    w_out = din("w_out", [D, D])
    w_up = din("w_up", [D, 2 * DFF])
    conv_w = din("conv_w", [3, DFF])
    conv_b = din("conv_b", [1, DFF])
    w_down = din("w_down", [DFF, D])
    c_identb = din("c_identb", [128, 128], BF16)
    c_identf = din("c_identf", [128, 128])
    c_maskT = din("c_maskT", [128, 128])
    c_rcnt = din("c_rcnt", [128, 32])

    y_p = dout("y_p", [SEQ, D])
    y_s = dout("y_s", [NS, D])
    o_mk = dout("o_mk", [MEM, 256])
    o_mv = dout("o_mv", [MEM, 256])
    o_pool_p = dout("o_pool_p", [15, BW])
    o_conv_p = dout("o_conv_p", [2, DFF])
    o_v_s = dout("o_v_s", [NS, AW])
    o_pool_s = dout("o_pool_s", [NB, 15, BW])
    o_conv_s = dout("o_conv_s", [NB, 2, DFF])

    with es:
        S = Sched(nc, es)
        banks = []
        for i in range(8):
            t = es.enter_context(nc.psum_tensor(f"ps{i}", [128, 512], F32))
            S.rowb[f"ps{i}"] = 2048
            banks.append(t)
        bank_i = [0]
        bank_n = [8]

        def nbank():
            b = banks[bank_i[0] % bank_n[0]]
            bank_i[0] += 1
            assert not (S.wrec.get(b.name) and not S.rrec.get(b.name)), ("PSUM bank reused before consumption", b.name)
            return b

        PB = 90624
        UB = 121856
        P = Arena(nc, es, S, "arenaP", PB)
        U = Arena(nc, es, S, "arenaU", UB)

        identb = P.alloc(BF16, 128)
        identf = P.alloc(F32, 128)
        onesb = P.alloc(BF16, 128)
        maskT = P.alloc(F32, 128)
        rcnt = P.alloc(F32, 2, 16)
        neghalf = P.alloc(F32, 8)
        V1T = P.alloc(F32, 72)
        V2T = P.alloc(F32, 66)
        hb = P.alloc(F32, 24)
        gpost1h = P.alloc(F32, D)
        gpost2 = P.alloc(F32, D)
        lng = P.alloc(F32, AW)
        lnb = P.alloc(F32, AW)
        bsb = P.alloc(F32, 4, 128)
        ws4 = P.alloc(F32, 4, 4, 4)
        bs4 = P.alloc(F32, 4, 4)
        wsT = P.alloc(BF16, 4, 128)
        Wbd = P.alloc(BF16, 2, 128)
        kT = P.alloc(BF16, 2, MEM)
        Vb = P.alloc(BF16, 2, 256)
        X = P.alloc(F32, 5, D)
        xnT = P.alloc(BF16, KC, NCOL)
        RING_N = 2
        ring = [P.alloc(BF16, 6144) for _ in range(RING_N)]
        bTs = P.alloc(F32, 2, NB, 19)
        gTs = P.alloc(F32, FC, NB, 6)
        halo_g = P.alloc(F32, FC, 2)
        halo_b = P.alloc(F32, 2, 15)
        PSt = P.alloc(F32, 2, 80)
        CS = P.alloc(F32, FC, 34)
        stats = P.alloc(F32, 128)
        assert P.hi <= PB, P.hi
        print('P arena used', P.hi, 'of', PB)

        U.off = 0
        uT = U.alloc(BF16, 4, NCOL)
        qT = U.alloc(BF16, 2, NCOL)
        attnT = U.alloc(BF16, 2, NCOL)
        mixedT = U.alloc(BF16, 2, NCOL)
        pooledT = U.alloc(BF16, 2, NCOL)
        bT = U.alloc(F32, 2, 15 + NG)
        TMP0 = U.off
        junk = U.alloc(BF16, D)
        xnb = [U.alloc(BF16, D) for _ in range(3)]
        vg = [U.alloc(F32, AW) for _ in range(4)]
        vn1 = U.alloc(F32, AW)
        vn2 = U.alloc(F32, AW)
        vnb = [U.alloc(BF16, AW) for _ in range(2)]
        vnS = U.alloc(F32, AW)
        spt = U.alloc(F32, 4, 128)
        vnTs = U.alloc(F32, 4, NB, 4)
        sgs = U.alloc(F32, 4, NB, 4)
        s2 = U.alloc(F32, 2, 15 + NG)
        s4 = U.alloc(F32, 2, 15 + NG)
        s8 = U.alloc(F32, 15 + NG)
        s16 = U.alloc(F32, 15 + NG)
        ptmp = U.alloc(F32, 2, 16)
        expT = [U.alloc(BF16, 2, 512) for _ in range(2)]
        rd = [U.alloc(F32, 512) for _ in range(2)]
        Ksb = [U.alloc(BF16, 2, 256) for _ in range(2)]
        Vsb = [U.alloc(BF16, 2, 256) for _ in range(2)]
        kTb = [U.alloc(BF16, 2, 256) for _ in range(2)]
        expS = [U.alloc(BF16, 2, 16) for _ in range(2)]
        rdS = U.alloc(F32, NB, 16)
        TMP1 = U.off
        stage = U.alloc(F32, 2816)
        U.off = TMP0
        tg = [U.alloc(F32, 512) for _ in range(3)]
        pj = [U.alloc(F32, 512) for _ in range(3)]
        dtmp = [U.alloc(F32, 512) for _ in range(2)]
        junk2 = U.alloc(BF16, D)
        hnb = [U.alloc(BF16, D) for _ in range(3)]
        assert U.off <= TMP1
        U.off = TMP1
        mergedT = U.alloc(BF16, KC, NCOL)
        wba = U.alloc(BF16, 4, D)
        WBB_OFF = U.off
        wbb = U.alloc(BF16, 2, D)
        wbc = U.alloc(BF16, 2, D)
        wout = U.alloc(BF16, KC, D)
        MIX_END = U.off
        assert MIX_END <= UB, MIX_END
        print('U mixer layout end', MIX_END, 'of', UB)
        U.off = 0
        wdn = U.alloc(BF16, FC, D)
        actT = U.alloc(BF16, FC, NCOL)
        gTp = [U.alloc(F32, 2 + NG) for _ in range(2)]
        cT = [U.alloc(F32, NG) for _ in range(2)]
        ge = [U.alloc(F32, NG) for _ in range(2)]
        cTs = U.alloc(F32, NB, 4)
        geS = U.alloc(F32, NB, 4)
        junk3 = U.alloc(BF16, 512)
        ftmp = [U.alloc(F32, 512) for _ in range(2)]
        PStT = U.alloc(F32, 256)
        CST = U.alloc(F32, DFF)
        junkA = U.alloc(BF16, D)
        xnbA = [U.alloc(BF16, D) for _ in range(3)]
        assert U.off <= UB, U.off
        print('U ffn layout end', U.off, 'of', UB)
        U.off = max(U.off, MIX_END)
        gffnb = U.alloc(F32, D)
        assert U.off <= UB, U.off

        sp = "sp"
        S.dma(sp, identb, c_identb[:, :])
        S.dma(sp, identf, c_identf[:, :])
        S.dma(sp, maskT, c_maskT[:, :])
        S.dma(sp, rcnt, c_rcnt[:, :].rearrange("p (c t) -> p c t", c=2))
        S.dma(sp, X[0:NS, 4, :], x_s[:, :])
        for i in range(NG // 128):
            S.dma(sp, X[:, i, :], x_p[i * 128:(i + 1) * 128, :])
        for mt in range(2):
            S.dma(sp, [s2.rearrange("p c l -> p (c l)")[:, 0:D], s4.rearrange("p c l -> p (c l)")[:, 0:D]][mt], mem[mt * 128:(mt + 1) * 128, :])
        S.memset("dve", onesb, 1.0)
        S.memset("dve", neghalf, -0.5)
        S.memset("dve", halo_g, 0.0)
        S.memset("dve", halo_b, 0.0)

        st1 = stage[:, 0:128]
        S.dma(sp, st1[0:8, :], norm_mix_pre.rearrange("o (r p) -> (o r) p", p=128))
        S.dma(sp, st1[8:16, :], norm_ffn_pre.rearrange("o (r p) -> (o r) p", p=128))
        S.dma(sp, st1[16:40, :], b_gate.rearrange("o (r p) -> (o r) p", p=128))
        S.dma(sp, st1[40:42, :], pool_scale.rearrange("o (r p) -> (o r) p", p=128))
        S.dma(sp, st1[42:64, :], conv_b.rearrange("o (r p) -> (o r) p", p=128))
        S.dma(sp, st1[64:72, :], mem_norm.rearrange("o (r p) -> (o r) p", p=128))
        bk = nbank()
        S.tr(bk[:, 0:72], st1[0:72, :], identf[0:72, 0:72])
        S.copy("dve", V1T, bk[:, 0:72])
        st2 = stage[:, 128:256]
        S.dma(sp, st2[0:66, :], conv_w.rearrange("k (r p) -> (k r) p", p=128))
        bk = nbank()
        S.tr(bk[:, 0:66], st2[0:66, :], identf[0:66, 0:66])
        S.copy("dve", V2T, bk[:, 0:66])
        S.ts("dve", hb, V1T[:, 16:40], 0.5, None, ALU.mult)

        S.dma(sp, gpost1h, norm_mix_post[0, :].partition_broadcast(128))
        S.ts("dve", gpost1h, gpost1h, 0.5, None, ALU.mult)
        S.dma(sp, gpost2, norm_ffn_post[0, :].partition_broadcast(128))
        S.dma(sp, gffnb, norm_ffn_pre[0, :].partition_broadcast(128))
        S.dma(sp, lng, gmlp_ln_g[0, :].partition_broadcast(128))
        S.dma(sp, lnb, gmlp_ln_b[0, :].partition_broadcast(128))
        S.dma(sp, bsb, b_spatial.partition_broadcast(128))
        for g in range(4):
            S.dma(sp, ws4[:, g, :, :], w_spatial[g, 0:4, 0:4].partition_broadcast(128))
        S.dma(sp, bs4, b_spatial[:, 0:4].partition_broadcast(128))

        wst = rs(stage[:, 256:768], 4, 128)
        for g in range(4):
            S.dma(sp, wst[:, g, :], w_spatial[g, :, :])
            bk = nbank()
            S.tr(bk[:, 0:128], wst[:, g, :], identf)
            S.tt("dve", wsT[:, g, :], bk[:, 0:128], maskT, ALU.mult)

        wbs = rs(stage[:, 768:1024], 2, 128)
        S.memset("dve", wbs, 0.0)
        for c in range(2):
            S.dma(sp, wbs[0:64, c, 0:64], w_pool[2 * c, :, :])
            S.dma(sp, wbs[64:128, c, 64:128], w_pool[2 * c + 1, :, :])
        S.copy("dve", Wbd, wbs)

        sps = rs(stage[:, 1024:1536], 2, 256)
        stp = st_pool.rearrange("b r f -> (b r) f")
        for hh in range(2):
            S.dma(sp, sps[0:120, hh, :], stp[hh * 120:(hh + 1) * 120, :])
        for hh in range(2):
            for c in range(2):
                bk = nbank()
                S.tr(bk[:, 0:120], sps[0:120, hh, c * 128:(c + 1) * 128], identf[0:120, 0:120])
                S.copy("dve", bTs[:, c, hh * 8:(hh + 1) * 8, 0:15], rs(bk[:, 0:120], 8, 15))
        S.dma(sp, o_pool_s[:, 0:11, :], st_pool[:, 4:15, :], is_out=True)

        scs = stage[0:32, 0:DFF]
        S.dma(sp, scs, st_conv.rearrange("b r f -> (b r) f"))
        for half in range(2):
            bk = nbank()
            for j in range(11):
                fc = half * 11 + j
                S.tr(bk[:, j * 32:(j + 1) * 32], scs[:, fc * 128:(fc + 1) * 128], identf[0:32, 0:32])
            S.copy("dve", gTs[:, half * 11:(half + 1) * 11, :, 0:2], rs(bk[:, 0:352], 11, NB, 2))

        S.mark('setup_done')
        def rstd_pool(out, acc, rows):
            S.ts("pool", out[0:rows], acc[0:rows], EPS, None, ALU.add)
            S.tt("pool", out[0:rows], out[0:rows], neghalf[0:rows, 0:1], ALU.pow)

        def pipeline(stages, n):
            ns = len(stages)
            for k in range(n + ns - 1):
                for st in range(ns - 1, -1, -1):
                    i = k - st
                    if 0 <= i < n:
                        stages[st](i)

        def norm_transpose_stages(items, gcol0, dstT, jk, xbs, col0, gb=None):
            st = {}

            def s_sq(i):
                src, rows, c0 = items[i]
                acc = stats[:, col0 + 2 * i: col0 + 2 * i + 1]
                S.act(jk[0:rows], src, AF.Square, scale=1.0 / 32.0, accum_out=acc[0:rows])

            def s_rstd(i):
                src, rows, c0 = items[i]
                rstd_pool(stats[:, col0 + 2 * i + 1: col0 + 2 * i + 2], stats[:, col0 + 2 * i: col0 + 2 * i + 1], rows)

            def s_norm(i):
                src, rows, c0 = items[i]
                xb = xbs[i % len(xbs)]
                if gb is not None:
                    S.stt("dve", xb[0:rows], src, stats[0:rows, col0 + 2 * i + 1: col0 + 2 * i + 2], gb[0:rows], ALU.mult, ALU.mult)
                else:
                    S.ts("dve", xb[0:rows], src, stats[0:rows, col0 + 2 * i + 1: col0 + 2 * i + 2], None, ALU.mult)

            def s_tr(i):
                src, rows, c0 = items[i]
                xb = xbs[i % len(xbs)]
                bk = nbank()
                st[i] = bk
                bkb = bk[:].bitcast(BF16)
                for kc in range(KC):
                    S.tr(bkb[:, kc * 128: kc * 128 + rows], xb[0:rows, kc * 128:(kc + 1) * 128], identb[0:rows, 0:rows])

            def s_evac(i):
                src, rows, c0 = items[i]
                bkb = st[i][:].bitcast(BF16)
                if gb is not None:
                    S.act(dstT[:, :, c0:c0 + rows], bkb.rearrange("p (k t) -> p k t", k=KC)[:, :, 0:rows], AF.Copy)
                    return
                for kc in range(KC):
                    S.ts("dve", dstT[:, kc, c0:c0 + rows], bkb[:, kc * 128: kc * 128 + rows], V1T[:, gcol0 + kc:gcol0 + kc + 1], None, ALU.mult)

            return [s_sq, s_rstd, s_norm, s_tr, s_evac]

        ring_i = [0]

        def ring_slot():
            r = ring[ring_i[0] % RING_N]
            ring_i[0] += 1
            return r

        def load_w(dst3, src2d):
            S.dma("pool", dst3, src2d.rearrange("(kc p) f -> p kc f", p=128))

        memX = [s2.rearrange("p c l -> p (c l)")[:, 0:D], s4.rearrange("p c l -> p (c l)")[:, 0:D]]
        wb_first = rs(ring_slot()[:, 0:KC * 512], KC, 512)
        load_w(wb_first, w_in[:, 0:512])
        wkv = U.at(WBB_OFF, BF16, KC, 512)
        load_w(wkv, w_mem_kv)
        wb1_first = rs(ring_slot()[:, 0:KC * 512], KC, 512)
        load_w(wb1_first, w_in[:, 512:1024])
        memnT = mergedT
        pipeline(norm_transpose_stages([(memX[mt], 128, mt * 128) for mt in range(2)], 64, memnT, junk, xnb, 112), 2)

        def kv_part2():
            for hp in range(2):
                bk = nbank()
                for kc in range(KC):
                    S.mm(bk[:, 0:MEM], wkv[:, kc, hp * 128:(hp + 1) * 128], memnT[:, kc, 0:MEM], start=kc == 0, stop=kc == KC - 1)
                S.copy("act", kT[:, hp, :], bk[:, 0:MEM])
            kvt = [pj[0], pj[1]]
            for mc in range(2):
                bk = nbank()
                for kc in range(KC):
                    S.mm(bk[:, :], memnT[:, kc, mc * 128:(mc + 1) * 128], wkv[:, kc, :], start=kc == 0, stop=kc == KC - 1)
                S.copy("act", kvt[mc], bk[:, :])
                S.copy("dve", Vb[:, mc, :], bk[:, 256:512])
                S.dma(sp, o_mk[mc * 128:(mc + 1) * 128, :], kvt[mc][:, 0:256], is_out=True)
                S.dma(sp, o_mv[mc * 128:(mc + 1) * 128, :], kvt[mc][:, 256:512], is_out=True)

        S.mark('kv_done')
        def pooling(xb4, s2v, s4v, s8v, s16v, outv, L, first):
            x = xb4
            S.tt("dve", s2v[:, :, :, 1:L], x[:, :, :, 1:L], x[:, :, :, 0:L - 1], ALU.add)
            S.tt("dve", s4v[64:128, 0, :, 3:L], s2v[64:128, 0, :, 3:L], s2v[64:128, 0, :, 1:L - 2], ALU.add)
            S.tt("dve", s4v[:, 1, :, 3:L], s2v[:, 1, :, 3:L], s2v[:, 1, :, 1:L - 2], ALU.add)
            S.tt("dve", s8v[:, :, 7:L], s4v[:, 1, :, 7:L], s4v[:, 1, :, 3:L - 4], ALU.add)
            S.tt("dve", s16v[64:128, :, 15:L], s8v[64:128, :, 15:L], s8v[64:128, :, 7:L - 8], ALU.add)
            srcs = [(0, 64, 0, s2v[0:64, 0], 0.5), (64, 128, 0, s4v[64:128, 0], 0.25),
                    (0, 64, 1, s8v[0:64], 0.125), (64, 128, 1, s16v[64:128], 1.0 / 16)]
            for p0, p1, c, sv, rw in srcs:
                S.stt("dve", outv[p0:p1, c, :, :], sv[:, :, 15:L], rw, x[p0:p1, c, :, 15:L], ALU.mult, ALU.subtract)
            if first:
                for p0, p1, c, sv, rw in srcs:
                    S.tt("dve", ptmp[p0:p1, c, :], sv[:, 0, 15:31], rcnt[p0:p1, c, :], ALU.mult)
                    S.tt("dve", outv[p0:p1, c, 0, 0:16], ptmp[p0:p1, c, :], x[p0:p1, c, 0, 15:31], ALU.subtract)

        def make_tiles(g):
            t = [((4 * g + i) % 5, 128, i * 128, i) for i in range(NG // 128)]
            if g == 0:
                t = [(4, NS, NG, -1)] + t
            return t

        def a_load(g, t):
            xi, rows, c0, pi = t
            if pi >= 0:
                S.dma(sp, X[:, xi, :], x_p[g * NG + pi * 128: g * NG + (pi + 1) * 128, :])
            else:
                S.dma(sp, X[0:NS, xi, :], x_s[:, :])

        def a_stages(tl):
            return norm_transpose_stages([(X[0:rows, xi, :], rows, c0) for (xi, rows, c0, pi) in tl], 0, xnT, junkA, xnbA, 0)

        def prefetch_win01():
            a = rs(ring_slot()[:, 0:KC * 512], KC, 512)
            load_w(a, w_in[:, 0:512])
            b = rs(ring_slot()[:, 0:KC * 512], KC, 512)
            load_w(b, w_in[:, 512:1024])
            return a, b

        class Stepper:
            def __init__(self, stages, n):
                self.stages, self.n, self.k = stages, n, 0

            def done(self):
                return self.k >= self.n + len(self.stages) - 1

            def step(self):
                ns = len(self.stages)
                for st in range(ns - 1, -1, -1):
                    i = self.k - st
                    if 0 <= i < self.n:
                        self.stages[st](i)
                self.k += 1

        for g in range(NGRP):
            first = g == 0
            last = g == NGRP - 1
            tiles = make_tiles(g)
            blocks = [(0, NG)]
            if first:
                blocks.append((NG, NS))

            if first:
                pipeline(a_stages(tiles), len(tiles))
                kv_part2()


            S.mark(f'g{g}_A_done')
            if first:
                wb = wb_first
                wb1 = wb1_first
            else:
                wb, wb1 = nxt_w01
            for (c0, n) in blocks:
                for fc in range(4):
                    bk = nbank()
                    for kc in range(KC):
                        S.mm(bk[:, 0:n], wb[:, kc, fc * 128:(fc + 1) * 128], xnT[:, kc, c0:c0 + n], start=kc == 0, stop=kc == KC - 1)
                    S.act(uT[:, fc, c0:c0 + n], bk[:, 0:n], AF.Gelu_apprx_tanh)

            S.mark(f'g{g}_B0_done')
            wb2 = rs(ring_slot()[:, 0:KC * 512], KC, 512)
            load_w(wb2, w_in[:, 1024:1536])
            b1 = {}
            SB1 = 48

            def b1_mm(ti):
                xi, rows, c0, pi = tiles[ti]
                bk = nbank()
                b1[ti] = bk
                for kc in range(KC):
                    S.mm(bk[0:rows, :], xnT[:, kc, c0:c0 + rows], wb1[:, kc, :], start=kc == 0, stop=kc == KC - 1)

            def b1_gelu(ti):
                xi, rows, c0, pi = tiles[ti]
                S.act(vg[ti % 4][0:rows], b1[ti][0:rows, :], AF.Gelu_apprx_tanh)

            def b1_stats(ti):
                xi, rows, c0, pi = tiles[ti]
                cb = SB1 + 9 * ti
                S.bn_stats(stats[0:rows, cb:cb + 6], vg[ti % 4][0:rows])
                S.bn_aggr(stats[0:rows, cb + 6:cb + 8], stats[0:rows, cb:cb + 6])

            def b1_rstd(ti):
                xi, rows, c0, pi = tiles[ti]
                cb = SB1 + 9 * ti
                rstd_pool(stats[:, cb + 8:cb + 9], stats[:, cb + 7:cb + 8], rows)

            def b1_norm(ti):
                xi, rows, c0, pi = tiles[ti]
                cb = SB1 + 9 * ti
                S.ts("dve", vn1[0:rows], vg[ti % 4][0:rows], stats[0:rows, cb + 6:cb + 7], stats[0:rows, cb + 8:cb + 9], ALU.subtract, ALU.mult)
                S.tt("dve", vn2[0:rows], vn1[0:rows], lng[0:rows], ALU.mult)
                if pi >= 0:
                    S.tt("dve", vnb[ti % 2], vn2, lnb, ALU.add)
                else:
                    S.tt("dve", vnS[0:NS], vn2[0:NS], lnb[0:NS], ALU.add)
                    S.dma(sp, o_v_s[:, :], vnS[0:NS, :], is_out=True)

            def b1_spatial(ti):
                xi, rows, c0, pi = tiles[ti]
                bk2 = nbank()
                b1[("sp", ti)] = bk2
                if pi >= 0:
                    vb = vnb[ti % 2]
                    for gch in range(4):
                        S.mm(bk2[:, gch * 128:(gch + 1) * 128], vb[:, gch * 128:(gch + 1) * 128], wsT[:, gch, :], start=True, stop=True)
                else:
                    for gch in range(4):
                        S.tr(bk2[:, gch * 64:(gch + 1) * 64], vnS[0:NS, gch * 128:(gch + 1) * 128], identf[0:NS, 0:NS])

            def b1_gate(ti):
                xi, rows, c0, pi = tiles[ti]
                bk2 = b1[("sp", ti)]
                if pi >= 0:
                    S.tt("dve", spt, rs(bk2[:, :], 4, 128), bsb, ALU.add)
                    S.tt("dve", uT[:, :, c0:c0 + 128], spt, uT[:, :, c0:c0 + 128], ALU.mult)
                else:
                    S.copy("dve", vnTs, rs(bk2[:, 0:256], 4, NB, 4))
                    for sidx in range(4):
                        for gch in range(4):
                            for t in range(sidx, 4):
                                acc = sgs[:, gch, :, t]
                                if sidx == 0:
                                    S.ts("dve", acc, vnTs[:, gch, :, 0], ws4[:, gch, t, 0:1], bs4[:, gch, t:t + 1], ALU.mult, ALU.add)
                                else:
                                    S.stt("dve", acc, vnTs[:, gch, :, sidx], ws4[:, gch, t, sidx:sidx + 1], acc, ALU.mult, ALU.add)
                    S.tt("dve", uT[:, :, NG:NG + NS], sgs.rearrange("p g b t -> p g (b t)"), uT[:, :, NG:NG + NS], ALU.mult)

            def b2_unit(c0, n, j):
                def f():
                    bk = nbank()
                    for kc in range(KC):
                        S.mm(bk[:, 0:n], wb2[:, kc, j * 128:(j + 1) * 128], xnT[:, kc, c0:c0 + n], start=kc == 0, stop=kc == KC - 1)
                    if j < 2:
                        if c0 == 0:
                            S.copy("act", bT[:, j, 15:15 + NG], bk[:, 0:NG])
                        else:
                            S.copy("act", bTs[:, j, :, 15:19], rs(bk[:, 0:NS], NB, 4))
                    else:
                        S.copy("act", qT[:, j - 2, c0:c0 + n], bk[:, 0:n])
                return f

            b2_units = [b2_unit(c0, n, j) for (c0, n) in blocks for j in range(4)]
            b1step = Stepper([b1_mm, b1_gelu, b1_stats, b1_rstd, b1_norm, b1_spatial, b1_gate], len(tiles))
            while not b1step.done():
                b1step.step()
                if b2_units and b1step.k >= 5:
                    b2_units.pop(0)()
            aT = uT

            def gate_block(i):
                wg = rs(ring_slot()[:, 0:KC * 768], KC, 3, 256)
                for j in range(3):
                    c = 1536 + j * 1024 + i * 256
                    load_w(wg[:, :, j, :], w_in[:, c:c + 256])
                return wg

            S.mark(f'g{g}_B1_done')
            while b2_units:
                b2_units.pop(0)()
            wg_next = gate_block(0)
            load_w(wba, w_branch_a)
            load_w(wbb, w_branch_b)
            load_w(wbc, w_branch_c)

            S.mark(f'g{g}_B2_done')
            S.mark(f'g{g}_pool_done')
            sb = {}

            def att_scores(h):
                hp, base = h // 2, (h % 2) * 64
                ex = expT[h % 2]
                for mc in range(2):
                    bk = nbank()
                    S.mm(bk[:, 0:NG], kT[base:base + 64, hp, mc * 128:(mc + 1) * 128], qT[base:base + 64, hp, 0:NG], start=True, stop=True)
                    S.act(ex[:, mc, :], bk[:, 0:NG], AF.Exp, scale=0.125)

            def att_pv(h):
                hp, base = h // 2, (h % 2) * 64
                ex = expT[h % 2]
                pv = nbank()
                for mc in range(2):
                    S.mm(pv[:, 0:NG], Vb[:, mc, hp * 128:(hp + 1) * 128], ex[:, mc, :], start=mc == 0, stop=mc == 1)
                dn = nbank()
                for mc in range(2):
                    S.mm(dn[:, 0:NG], onesb, ex[:, mc, :], start=mc == 0, stop=mc == 1)
                r = rd[h % 2]
                S.recip(r[base:base + 64, :], dn[base:base + 64, 0:NG])
                S.tt("dve", attnT[base:base + 64, hp, 0:NG], pv[base:base + 64, 0:NG], r[base:base + 64, :], ALU.mult)

            for i in range(5):
                if i < 4:
                    att_scores(i)
                if i >= 1:
                    att_pv(i - 1)

            L = 15 + NG
            S.copy("dve", bT[:, :, 0:15], halo_b)
            pooling(bT.rearrange("p c (b l) -> p c b l", b=1), s2.rearrange("p c (b l) -> p c b l", b=1),
                    s4.rearrange("p c (b l) -> p c b l", b=1), s8.rearrange("p (b l) -> p b l", b=1),
                    s16.rearrange("p (b l) -> p b l", b=1),
                    pooledT[:, :, 0:NG].rearrange("p c (b l) -> p c b l", b=1), L, first)
            if last:
                S.copy("dve", PSt[:, :, 64:79], bT[:, :, L - 15:L])
            else:
                S.copy("dve", halo_b, bT[:, :, L - 15:L])
            if first:
                s2s = rs(s2.rearrange("p c l -> p (c l)")[:, 0:2 * NB * 19], 2, NB, 19)
                s4s = rs(s4.rearrange("p c l -> p (c l)")[:, 0:2 * NB * 19], 2, NB, 19)
                s8s = rs(s8[:, 0:NB * 19], NB, 19)
                s16s = rs(s16[:, 0:NB * 19], NB, 19)
                pooling(bTs, s2s, s4s, s8s, s16s, pooledT[:, :, NG:NG + NS].rearrange("p c (b t) -> p c b t", b=NB), 19, False)
                S.copy("dve", PSt[:, :, 0:64].rearrange("p c (t b) -> p c b t", t=4), bTs[:, :, :, 15:19])

            def glin():
                for (c0, n) in blocks:
                    for c in range(2):
                        bk = nbank()
                        S.mm(bk[:, 0:n], Wbd[:, c, :], pooledT[:, c, c0:c0 + n], start=True, stop=True)
                        S.act(mixedT[:, c, c0:c0 + n], bk[:, 0:n], AF.Copy, scale=V1T[:, 40 + c:41 + c])

            glin_pending = [glin]

            S.mark(f'g{g}_attp_done')
            if first:
                scol = NG

                def s_load(b):
                    S.dma("pool", Ksb[b % 2], ck[b].rearrange("(mc m) f -> m mc f", m=128))
                    S.dma("pool", Vsb[b % 2], cv[b].rearrange("(mc m) f -> m mc f", m=128))

                def s_tr(b):
                    bk = nbank()
                    bkb = bk[:].bitcast(BF16)
                    for hp in range(2):
                        for mc in range(2):
                            S.tr(bkb[:, hp * 256 + mc * 128: hp * 256 + (mc + 1) * 128], Ksb[b % 2][:, mc, hp * 128:(hp + 1) * 128], identb)
                    S.copy("act", kTb[b % 2], rs(bkb[:, 0:512], 2, 256))

                bank_n[0] = 6
                OS = banks[6]
                DS = banks[7]
                OSv = rs(OS[:, 0:256], NB, 4, 4)
                DSv = rs(DS[:, 0:256], NB, 4, 4)

                def s_scores(b):
                    bkp = [nbank(), nbank()]
                    ex5 = expS[b % 2].rearrange("p mc (hp par t) -> p mc hp par t", hp=2, par=2)
                    for par in range(2):
                        base = par * 64
                        sv = rs(bkp[par][:, 0:16], 2, 2, 4)
                        for hp in range(2):
                            for mc in range(2):
                                S.mm(sv[:, mc, hp, :], kTb[b % 2][base:base + 64, hp, mc * 128:(mc + 1) * 128],
                                     qT[base:base + 64, hp, scol + b * 4: scol + b * 4 + 4], start=True, stop=True)
                    for par in range(2):
                        S.act(ex5[:, :, :, par, :], rs(bkp[par][:, 0:16], 2, 2, 4), AF.Exp, scale=0.125)

                def s_pv(b):
                    ex = rs(expS[b % 2].rearrange("p mc x -> p (mc x)"), 2, 4, 4)
                    for h in range(4):
                        hp = h // 2
                        for mc in range(2):
                            S.mm(OSv[:, b, h, :], Vsb[b % 2][:, mc, hp * 128:(hp + 1) * 128], ex[:, mc, h, :], start=mc == 0, stop=mc == 1)
                    for mc in range(2):
                        S.mm(DSv[:, b, :, :].rearrange("p h t -> p (h t)"), onesb, expS[b % 2][:, mc, :], start=mc == 0, stop=mc == 1)

                s_load(0)
                s_load(1)
                s_tr(0)
                for b in range(NB):
                    if b + 1 < NB:
                        s_tr(b + 1)
                    s_scores(b)
                    s_pv(b)
                    if b + 2 < NB:
                        s_load(b + 2)
                S.recip(rdS.rearrange("p b x -> p (b x)"), DS[:, 0:256])
                rdv = rs(rdS.rearrange("p b x -> p (b x)"), NB, 4, 4)
                for h in range(4):
                    hp, base = h // 2, (h % 2) * 64
                    S.tt("dve", attnT[base:base + 64, hp, scol:scol + NS].rearrange("p (b t) -> p b t", b=NB),
                         OSv[base:base + 64, :, h, :], rdv[base:base + 64, :, h, :], ALU.mult)
                bank_n[0] = 8

            S.mark(f'g{g}_atts_done')
            load_w(wout, w_out)
            for i in range(4):
                wg = wg_next
                if i + 1 < 4:
                    wg_next = gate_block(i + 1)
                for fl in range(2):
                    fo = 2 * i + fl
                    for (c0, n) in blocks:
                        gb = []
                        for j in range(3):
                            bk = nbank()
                            for kc in range(KC):
                                S.mm(bk[:, 0:n], wg[:, kc, j, fl * 128:(fl + 1) * 128], xnT[:, kc, c0:c0 + n], start=kc == 0, stop=kc == KC - 1)
                            S.act(tg[j][:, 0:n], bk[:, 0:n], AF.Tanh, bias=hb[:, j * 8 + fo: j * 8 + fo + 1], scale=0.5)
                            gb.append(bk)
                        if glin_pending:
                            glin_pending.pop()()
                        srcs = [(wba, aT, 4), (wbb, mixedT, 2), (wbc, attnT, 2)]
                        for j, (wbr, actv, nk) in enumerate(srcs):
                            bk = nbank()
                            for kc in range(nk):
                                S.mm(bk[:, 0:n], wbr[:, kc, fo * 128:(fo + 1) * 128], actv[:, kc, c0:c0 + n], start=kc == 0, stop=kc == nk - 1)
                            S.stt("dve", pj[j][:, 0:n], tg[j][:, 0:n], 1.0, bk[:, 0:n], ALU.add, ALU.mult)
                        S.tt("pool", pj[0][:, 0:n], pj[0][:, 0:n], pj[1][:, 0:n], ALU.add)
                        S.tt("pool", mergedT[:, fo, c0:c0 + n], pj[0][:, 0:n], pj[2][:, 0:n], ALU.add)

            S.mark(f'g{g}_C_done')
            def up_block(jb):
                wu = rs(ring_slot()[:, 0:KC * 512], KC, 2, 256)
                load_w(wu[:, :, 0, :], w_up[:, jb * 256:(jb + 1) * 256])
                load_w(wu[:, :, 1, :], w_up[:, DFF + jb * 256: DFF + (jb + 1) * 256])
                return wu

            wu_first = up_block(0)
            for k0 in (0, 5, 10):
                k1 = min(k0 + 5, 14)
                load_w(wdn[:, k0:k1, :], w_down[k0 * 128:k1 * 128, :])
            dd = {}
            SD = 96

            def d_mm(ti):
                xi, rows, c0, pi = tiles[ti]
                bks = []
                for half in range(2):
                    bk = nbank()
                    for kc in range(KC):
                        S.mm(bk[0:rows, :], mergedT[:, kc, c0:c0 + rows], wout[:, kc, half * 512:(half + 1) * 512], start=kc == 0, stop=kc == KC - 1)
                    bks.append(bk)
                dd[ti] = bks

            def d_sq(ti):
                xi, rows, c0, pi = tiles[ti]
                for half in range(2):
                    S.act(junk2[0:rows, 0:512], dd[ti][half][0:rows, :], AF.Square, scale=1.0 / 64.0,
                          accum_out=stats[0:rows, SD + 3 * ti + half: SD + 3 * ti + half + 1])

            def d_rstd(ti):
                xi, rows, c0, pi = tiles[ti]
                racc = stats[:, SD + 3 * ti + 2: SD + 3 * ti + 3]
                S.tt("pool", racc[0:rows], stats[0:rows, SD + 3 * ti: SD + 3 * ti + 1], stats[0:rows, SD + 3 * ti + 1: SD + 3 * ti + 2], ALU.add)
                rstd_pool(racc, racc, rows)

            def d_res(ti):
                xi, rows, c0, pi = tiles[ti]
                racc = stats[:, SD + 3 * ti + 2: SD + 3 * ti + 3]
                for half in range(2):
                    dt_ = dtmp[half]
                    S.stt("dve", dt_[0:rows], dd[ti][half][0:rows, :], racc[0:rows], gpost1h[0:rows, half * 512:(half + 1) * 512], ALU.mult, ALU.mult)
                for half in range(2):
                    S.tt("dve" if half == 0 else "pool", X[0:rows, xi, half * 512:(half + 1) * 512], X[0:rows, xi, half * 512:(half + 1) * 512], dtmp[half][0:rows], ALU.add)

            hn_stages = norm_transpose_stages([(X[0:rows, xi, :], rows, c0) for (xi, rows, c0, pi) in tiles], 8, xnT, junk2, hnb, 16, gb=gffnb)
            def d_sq_rstd(ti):
                d_sq(ti)
                d_rstd(ti)

            def d_res_sq(ti):
                d_res(ti)
                hn_stages[0](ti)

            pipeline([d_mm, d_sq_rstd, d_res_sq] + hn_stages[1:], len(tiles))
            hnT = xnT

            S.mark(f'g{g}_D_done')
            if not last:
                ntiles = make_tiles(g + 1)
                nloaded = 0
                cur_slots = [t[0] for t in tiles]
                while nloaded < len(ntiles) and ntiles[nloaded][0] not in cur_slots:
                    a_load(g + 1, ntiles[nloaded])
                    nloaded += 1
                astep = Stepper(a_stages(ntiles), len(ntiles))
            wu_next = wu_first
            for jb in range(FC // 2):
                wu = wu_next
                if jb + 1 < FC // 2:
                    wu_next = up_block(jb + 1)
                if jb < 4:
                    k0 = 14 + 2 * jb
                    load_w(wdn[:, k0:k0 + 2, :], w_down[k0 * 128:(k0 + 2) * 128, :])
                for fl in range(2):
                    fc = 2 * jb + fl
                    cw0 = V2T[:, fc:fc + 1]
                    cw1 = V2T[:, 22 + fc:23 + fc]
                    cw2 = V2T[:, 44 + fc:45 + fc]
                    cbv = V1T[:, 42 + fc:43 + fc]
                    for (c0, n) in blocks:
                        gbk = nbank()
                        for kc in range(KC):
                            S.mm(gbk[:, 0:n], wu[:, kc, 0, fl * 128:(fl + 1) * 128], hnT[:, kc, c0:c0 + n], start=kc == 0, stop=kc == KC - 1)
                        ubk = nbank()
                        for kc in range(KC):
                            S.mm(ubk[:, 0:n], wu[:, kc, 1, fl * 128:(fl + 1) * 128], hnT[:, kc, c0:c0 + n], start=kc == 0, stop=kc == KC - 1)
                        if c0 == 0:
                            gt = gTp[fc % 2]
                            ct = cT[fc % 2]
                            gg = ge[fc % 2]
                            S.copy("pool", gt[:, 0:2], halo_g[:, fc, :])
                            S.copy("act", gt[:, 2:2 + NG], gbk[:, 0:NG])
                            if last:
                                S.copy("pool", CS[:, fc, 32:34], gt[:, NG:NG + 2])
                            else:
                                S.copy("pool", halo_g[:, fc, :], gt[:, NG:NG + 2])
                            S.ts("dve", ct, gt[:, 2:2 + NG], cw2, cbv, ALU.mult, ALU.add)
                            S.stt("dve", ct, gt[:, 1:1 + NG], cw1, ct, ALU.mult, ALU.add)
                            S.stt("dve", ct, gt[:, 0:NG], cw0, ct, ALU.mult, ALU.add)
                            S.act(gg, ct, AF.Gelu_apprx_tanh)
                            S.tt("dve", actT[:, fc, 0:NG], ubk[:, 0:NG], gg, ALU.mult)
                        else:
                            S.copy("act", gTs[:, fc, :, 2:6], rs(gbk[:, 0:NS], NB, 4))
                            S.copy("pool", CS[:, fc, 0:32].rearrange("p (r b) -> p b r", r=2), gTs[:, fc, :, 4:6])
                            S.ts("dve", cTs, gTs[:, fc, :, 2:6], cw2, cbv, ALU.mult, ALU.add)
                            S.stt("dve", cTs, gTs[:, fc, :, 1:5], cw1, cTs, ALU.mult, ALU.add)
                            S.stt("dve", cTs, gTs[:, fc, :, 0:4], cw0, cTs, ALU.mult, ALU.add)
                            S.act(geS, cTs, AF.Gelu_apprx_tanh)
                            S.tt("dve", actT[:, fc, NG:NG + NS], ubk[:, 0:NS], geS.rearrange("p b t -> p (b t)"), ALU.mult)

            S.mark(f'g{g}_E_done')
            if not last:
                nxt_w01 = prefetch_win01()
            for ti, (xi, rows, c0, pi) in enumerate(tiles):
                acc2 = stats[:, 40:42]
                racc = stats[:, 42:43]
                bks = []
                for half in range(2):
                    bk = nbank()
                    for kc in range(FC):
                        S.mm(bk[0:rows, :], actT[:, kc, c0:c0 + rows], wdn[:, kc, half * 512:(half + 1) * 512], start=kc == 0, stop=kc == FC - 1)
                    S.act(junk3[0:rows], bk[0:rows, :], AF.Square, scale=1.0 / 32.0, accum_out=acc2[0:rows, half:half + 1])
                    bks.append(bk)
                S.tt("pool", racc[0:rows], acc2[0:rows, 0:1], acc2[0:rows, 1:2], ALU.add)
                rstd_pool(racc, racc, rows)
                for half in range(2):
                    ft = ftmp[half]
                    S.stt("dve", ft[0:rows], bks[half][0:rows, :], racc[0:rows], gpost2[0:rows, half * 512:(half + 1) * 512], ALU.mult, ALU.mult)
                    S.tt("pool", X[0:rows, xi, half * 512:(half + 1) * 512], X[0:rows, xi, half * 512:(half + 1) * 512], ft[0:rows], ALU.add)
                if pi >= 0:
                    S.dma(sp, y_p[g * NG + pi * 128: g * NG + (pi + 1) * 128, :], X[:, xi, :], is_out=True)
                else:
                    S.dma(sp, y_s[:, :], X[0:NS, xi, :], is_out=True)
                if not last:
                    while nloaded < len(ntiles) and ntiles[nloaded][0] in [t[0] for t in tiles[:ti + 1]] + [x for x in range(5) if x not in cur_slots]:
                        a_load(g + 1, ntiles[nloaded])
                        nloaded += 1
                    while (not astep.done()) and min(astep.k, astep.n - 1) < nloaded:
                        astep.step()
                        if astep.k <= astep.n:
                            break
            if not last:
                assert nloaded == len(ntiles)
                while not astep.done():
                    astep.step()

        S.mark('groups_done')
        for c in range(2):
            bk = nbank()
            S.tr(bk[0:79, 0:128], PSt[:, c, 0:79], identf)
            S.copy("dve", PStT[0:79, c * 128:(c + 1) * 128], bk[0:79, 0:128])
        for t in range(4):
            S.dma(sp, o_pool_s[:, 11 + t, :], PStT[t * 16:(t + 1) * 16, :], is_out=True)
        S.dma(sp, o_pool_p[:, :], PStT[64:79, :], is_out=True)
        for q in range(6):
            bk = nbank()
            nfc = 4 if q < 5 else 2
            for j in range(nfc):
                fc = q * 4 + j
                S.tr(bk[0:34, j * 128:(j + 1) * 128], CS[:, fc, :], identf)
            S.copy("act" if q % 2 else "dve", CST[0:34, q * 512: q * 512 + nfc * 128], bk[0:34, 0:nfc * 128])
        for r in range(2):
            S.dma(sp, o_conv_s[:, r, :], CST[r * 16:(r + 1) * 16, :], is_out=True)
        S.dma(sp, o_conv_p[:, :], CST[32:34, :], is_out=True)

        import os
        lim = int(os.environ.get("KSTOP", "0")) or None
        if os.environ.get("KMARKS"):
            print("MARKS", S.marks, "total", len(S.ops))
        S.emit(limit=lim)
    return nc


_CACHE = {}


def _consts():
    identb = np.eye(128, dtype=np.float32).astype(ml_dtypes.bfloat16)
    identf = np.eye(128, dtype=np.float32)
    s = np.arange(128)
    maskT = (s[:, None] <= s[None, :]).astype(np.float32)
    rc = np.zeros((128, 2, 16), np.float32)
    wins = {(0, 0): 2, (1, 0): 4, (0, 1): 8, (1, 1): 16}
    for p in range(128):
        for c in range(2):
            w = wins[(p // 64, c)]
            for t in range(16):
                rc[p, c, t] = 1.0 / min(w, t + 1)
    return identb, identf, maskT, rc.reshape(128, 32)


def make_in_maps(inputs):
    f = lambda k: np.ascontiguousarray(np.asarray(inputs[k], dtype=np.float32))
    x_prompt = f("x_prompt")
    x_sample = f("x_sample")
    mem_prompt = f("mem_prompt")
    cache_k = f("cache_mem_k")[0].reshape(128, MEM, 256)
    cache_v = f("cache_mem_v")[0].reshape(128, MEM, 256)
    state_pool = f("state_pool")[0]
    state_conv = f("state_conv")[0]
    identb, identf, maskT, rcnt = _consts()
    shared = {
        "norm_mix_pre": f("norm_mix_pre"), "norm_mix_post": f("norm_mix_post"),
        "norm_ffn_pre": f("norm_ffn_pre"), "norm_ffn_post": f("norm_ffn_post"),
        "w_in": f("w_in")[0], "b_gate": f("b_gate"),
        "gmlp_ln_g": f("gmlp_ln_g"), "gmlp_ln_b": f("gmlp_ln_b"),
        "w_spatial": f("w_spatial")[0], "b_spatial": f("b_spatial")[0],
        "w_pool": f("w_pool")[0], "pool_scale": f("pool_scale"),
        "mem_norm": f("mem_norm"), "w_mem_kv": f("w_mem_kv")[0],
        "w_branch_a": f("w_branch_a")[0], "w_branch_b": f("w_branch_b")[0], "w_branch_c": f("w_branch_c")[0],
        "w_out": f("w_out")[0], "w_up": f("w_up")[0], "conv_w": f("conv_w")[0], "conv_b": f("conv_b"),
        "w_down": f("w_down")[0],
        "c_identb": identb, "c_identf": identf, "c_maskT": maskT, "c_rcnt": rcnt,
    }
    in_maps = []
    for c in range(NCORES):
        m = dict(shared)
        m["x_p"] = x_prompt[c]
        m["x_s"] = np.ascontiguousarray(x_sample[c * NB:(c + 1) * NB].reshape(NS, D))
        m["mem"] = mem_prompt[c]
        m["ck"] = np.ascontiguousarray(cache_k[c * NB:(c + 1) * NB])
        m["cv"] = np.ascontiguousarray(cache_v[c * NB:(c + 1) * NB])
        m["st_pool"] = np.ascontiguousarray(state_pool[c * NB:(c + 1) * NB])
        m["st_conv"] = np.ascontiguousarray(state_conv[c * NB:(c + 1) * NB])
        in_maps.append(m)
    return in_maps


def kernel(**inputs):
    in_maps = make_in_maps(inputs)
    if "nc" not in _CACHE:
        _CACHE["nc"] = build_program()
    nc = _CACHE["nc"]
    res = run_bass_kernel_spmd(nc, in_maps, core_ids=list(range(NCORES)))
    R = res.results
    y_prompt = np.stack([R[c]["y_p"] for c in range(NCORES)]).astype(np.float32)
    y_sample = np.concatenate([R[c]["y_s"].reshape(NB, 4, D) for c in range(NCORES)]).astype(np.float32)
    mk = np.stack([R[c]["o_mk"].reshape(MEM, 4, 64) for c in range(NCORES)])[None].astype(np.float32)
    mv = np.stack([R[c]["o_mv"].reshape(MEM, 4, 64) for c in range(NCORES)])[None].astype(np.float32)
    pool_p = np.stack([R[c]["o_pool_p"] for c in range(NCORES)])[None].astype(np.float32)
    conv_p = np.stack([R[c]["o_conv_p"] for c in range(NCORES)])[None].astype(np.float32)
    v_s = np.concatenate([R[c]["o_v_s"].reshape(NB, 4, AW) for c in range(NCORES)])[None].astype(np.float32)
    pool_s = np.concatenate([R[c]["o_pool_s"] for c in range(NCORES)])[None].astype(np.float32)
    conv_s = np.concatenate([R[c]["o_conv_s"] for c in range(NCORES)])[None].astype(np.float32)
    return (y_prompt, y_sample, mk, mv, pool_p, conv_p, v_s, pool_s, conv_s)
```

```python
import contextlib
from math import prod

import numpy as np
import ml_dtypes

import concourse.bass as bass
import concourse.mybir as mybir
from concourse.bass_utils import run_bass_kernel_spmd

F32 = mybir.dt.float32
BF16 = mybir.dt.bfloat16
AF = mybir.ActivationFunctionType
ALU = mybir.AluOpType
DTSZ = {F32: 4, BF16: 2}

import os
EVAC_ENGS = os.environ.get('KEVAC', 'dve,dve').split(',')
STRICT = os.environ.get('KSTRICT', '1') == '1'
SAME_ENG_GAP = 10 ** 9 if STRICT else int(os.environ.get('KGAP', '2'))
NCORES = 8
D = 1024
KC = 8
SEQ = 2048
NG = 512
NGRP = SEQ // NG
NS = 64
NB = 16
NCOL = NG + NS
AW = 512
BW = 256
DFF = 2816
FC = 22
MEM = 256
EPS = 1e-6
D_IN = 4608


class Op:
    __slots__ = ("eng", "fn", "deps", "signal", "sigval", "dma", "sem", "semval", "prevval", "is_out", "idx", "lidx")

    def __init__(self, eng, fn, dma=False):
        self.eng = eng
        self.fn = fn
        self.deps = set()
        self.signal = False
        self.sigval = 0
        self.dma = dma
        self.sem = None
        self.semval = 0
        self.prevval = 0
        self.is_out = False
        self.idx = 0
        self.lidx = 0


class Sched:
    def __init__(self, nc, es):
        self.nc = nc
        self.ops = []
        self.rowb = {}
        self.wrec = {}
        self.rrec = {}
        self.engs = {"pe": nc.tensor, "act": nc.scalar, "dve": nc.vector, "pool": nc.gpsimd, "sp": nc.sync}
        self.esem = {k: es.enter_context(nc.semaphore("s_" + k)) for k in ("pe", "act", "dve", "pool")}
        self.ndsem = 12
        self.dsem = {q: [es.enter_context(nc.semaphore(f"d_{q}{i}")) for i in range(self.ndsem)] for q in ("sp", "pool")}
        self.dcount = {"sp": 0, "pool": 0}
        self.lcount = {}
        self.marks = []

    def mark(self, name):
        self.marks.append((name, len(self.ops)))

    def region(self, ap):
        name = ap.name
        rb = self.rowb.get(name)
        if rb is None:
            return None
        if name.startswith("ps"):
            return name, 0, 128, [(0, 2048)]
        esz = DTSZ[ap.dtype]
        offb = ap.offset * esz
        pat = list(ap.ap)
        p0 = offb // rb
        f0 = offb % rb
        if pat[0][0] * esz == rb:
            pc = pat[0][1]
            dims = pat[1:]
        else:
            pc = 1
            dims = pat
        dims = [(s * esz, c) for s, c in dims if c > 1 and s != 0]
        if not dims:
            ivs = [(f0, f0 + esz)]
        else:
            inner = dims[-1]
            outer = dims[:-1]
            run = (inner[1] - 1) * inner[0] + esz
            nouter = prod(c for _, c in outer) if outer else 1
            if nouter <= 64:
                offs = [0]
                for s, c in outer:
                    offs = [o + i * s for o in offs for i in range(c)]
                ivs = sorted((f0 + o, f0 + o + run) for o in offs)
                merged = [ivs[0]]
                for lo, hi in ivs[1:]:
                    if lo <= merged[-1][1]:
                        merged[-1] = (merged[-1][0], max(hi, merged[-1][1]))
                    else:
                        merged.append((lo, hi))
                ivs = merged
            else:
                ivs = [(f0, f0 + sum((c - 1) * s for s, c in dims) + esz)]
        return name, p0, p0 + pc, ivs

    def _add_dep(self, op, prod_op, kind):
        if prod_op is op:
            return
        if not op.dma and not prod_op.dma and op.eng == prod_op.eng:
            if op.eng == "pe":
                return
            if kind != "raw" and not STRICT:
                return
        op.deps.add(prod_op)

    def _read(self, op, ap):
        r = self.region(ap)
        if r is None:
            return
        name, p0, p1, ivs = r
        wl = self.wrec.setdefault(name, [])
        for rec in wl:
            if rec[0] < p1 and p0 < rec[1]:
                for lo, hi in ivs:
                    if rec[2] < hi and lo < rec[3]:
                        self._add_dep(op, rec[4], "raw")
                        break
        rl = self.rrec.setdefault(name, [])
        if name.startswith("ps"):
            for rec in rl:
                if rec[4].eng != op.eng:
                    self._add_dep(op, rec[4], "rar")
        for lo, hi in ivs:
            if not op.dma:
                for i, rec in enumerate(rl):
                    if rec[0] == p0 and rec[1] == p1 and rec[2] == lo and rec[3] == hi and (not rec[4].dma) and rec[4].eng == op.eng:
                        rl[i] = (p0, p1, lo, hi, op)
                        break
                else:
                    rl.append((p0, p1, lo, hi, op))
            else:
                rl.append((p0, p1, lo, hi, op))

    def _write(self, op, ap):
        r = self.region(ap)
        if r is None:
            return
        name, p0, p1, ivs = r
        wl = self.wrec.setdefault(name, [])
        rl = self.rrec.setdefault(name, [])
        for lst, kind in ((wl, "waw"), (rl, "war")):
            keep = []
            for rec in lst:
                hit = False
                contained = False
                if rec[0] < p1 and p0 < rec[1]:
                    for lo, hi in ivs:
                        if rec[2] < hi and lo < rec[3]:
                            hit = True
                            if lo <= rec[2] and rec[3] <= hi and p0 <= rec[0] and rec[1] <= p1:
                                contained = True
                            break
                if hit:
                    self._add_dep(op, rec[4], kind)
                if not contained:
                    keep.append(rec)
            lst[:] = keep
        for lo, hi in ivs:
            wl.append((p0, p1, lo, hi, op))

    def add(self, eng, fn, reads, writes, dma=False):
        op = Op(eng, fn, dma)
        op.idx = len(self.ops)
        if not dma:
            op.lidx = self.lcount.get(eng, 0)
            self.lcount[eng] = op.lidx + 1
        for ap in reads:
            if ap is not None and not isinstance(ap, (int, float)):
                self._read(op, ap)
        for ap in writes:
            if ap is not None:
                self._write(op, ap)
        latest = {}
        keep = set()
        for p in op.deps:
            if p.dma:
                keep.add(p)
            elif p.eng not in latest or latest[p.eng].idx < p.idx:
                latest[p.eng] = p
        for e, p in latest.items():
            if (not dma) and e == eng and op.lidx - p.lidx > SAME_ENG_GAP:
                continue
            keep.add(p)
            p.signal = True
        op.deps = keep
        if dma:
            k = self.dcount[eng]
            self.dcount[eng] = k + 1
            op.sem = self.dsem[eng][k % self.ndsem]
            op.semval = 16 * (k // self.ndsem + 1)
            op.prevval = 16 * (k // self.ndsem)
        self.ops.append(op)
        return op

    def mm(self, out, lhsT, rhs, start=True, stop=True):
        return self.add("pe", lambda e: e.matmul(out, lhsT, rhs, start=start, stop=stop), [lhsT, rhs], [out])

    def tr(self, out, in_, ident):
        return self.add("pe", lambda e: e.transpose(out, in_, ident), [in_, ident], [out])

    def act(self, out, in_, func, bias=None, scale=None, accum_out=None):
        kw = {}
        if bias is not None:
            kw["bias"] = bias
        if scale is not None:
            kw["scale"] = scale
        if accum_out is not None:
            kw["accum_out"] = accum_out
        if func == AF.Copy and any(v is not None and not isinstance(v, (int, float)) for v in (bias, scale)):
            func = AF.Identity
        return self.add("act", lambda e: e.activation(out=out, in_=in_, func=func, **kw), [in_, bias, scale], [out, accum_out])

    def tt(self, eng, out, in0, in1, op):
        return self.add(eng, lambda e: e.tensor_tensor(out=out, in0=in0, in1=in1, op=op), [in0, in1], [out])

    def ts(self, eng, out, in0, s1, s2, op0, op1=None):
        if op1 is None:
            return self.add(eng, lambda e: e.tensor_scalar(out=out, in0=in0, scalar1=s1, scalar2=None, op0=op0), [in0, s1], [out])
        return self.add(eng, lambda e: e.tensor_scalar(out=out, in0=in0, scalar1=s1, scalar2=s2, op0=op0, op1=op1), [in0, s1, s2], [out])

    def stt(self, eng, out, in0, scalar, in1, op0, op1):
        return self.add(eng, lambda e: e.scalar_tensor_tensor(out=out, in0=in0, scalar=scalar, in1=in1, op0=op0, op1=op1), [in0, scalar, in1], [out])

    def copy(self, eng, out, in_):
        if eng == "act":
            return self.act(out, in_, AF.Copy)
        return self.add(eng, lambda e: e.tensor_copy(out=out, in_=in_), [in_], [out])

    def memset(self, eng, out, val):
        return self.add(eng, lambda e: e.memset(out, val), [], [out])

    def recip(self, out, in_):
        return self.add("dve", lambda e: e.reciprocal(out=out, in_=in_), [in_], [out])

    def bn_stats(self, out, in_):
        return self.add("dve", lambda e: e.bn_stats(out=out, in_=in_), [in_], [out])

    def bn_aggr(self, out, in_):
        return self.add("dve", lambda e: e.bn_aggr(out=out, in_=in_), [in_], [out])

    def dma(self, q, out, in_, is_out=False):
        op = self.add(q, lambda e: e.dma_start(out=out, in_=in_), [in_], [out], dma=True)
        op.is_out = is_out
        return op

    def emit(self, limit=None):
        if limit:
            self.ops = self.ops[:limit]
            for op in self.ops:
                if not op.dma:
                    op.signal = False
            live = set(map(id, self.ops))
            for op in self.ops:
                for p in op.deps:
                    if not p.dma:
                        p.signal = True
        cnt = {k: 0 for k in self.esem}
        for op in self.ops:
            if not op.dma and op.signal:
                cnt[op.eng] += 1
                op.sigval = cnt[op.eng]
        waited = {k: {} for k in self.engs}
        outs = []

        def wait(engname, sem, val):
            w = waited[engname]
            key = id(sem)
            if w.get(key, 0) >= val:
                return
            w[key] = val
            self.engs[engname].wait_ge(sem, val)

        for op in self.ops:
            need = {}
            for p in op.deps:
                if p.dma:
                    sem, val = p.sem, p.semval
                else:
                    sem, val = self.esem[p.eng], p.sigval
                k = id(sem)
                if k not in need or need[k][1] < val:
                    need[k] = (sem, val)
            if op.dma and op.prevval > 0:
                k = id(op.sem)
                if k not in need or need[k][1] < op.prevval:
                    need[k] = (op.sem, op.prevval)
            for sem, val in need.values():
                wait(op.eng, sem, val)
            inst = op.fn(self.engs[op.eng])
            if op.dma:
                inst.then_inc(op.sem, 16)
                if op.is_out:
                    outs.append(op)
            elif op.signal:
                inst.then_inc(self.esem[op.eng], 1)
        for engname in ("sp", "act", "pool"):
            for op in outs:
                wait(engname, op.sem, op.semval)


def rs(ap, *dims):
    names = "abcdefgh"[: len(dims)]
    pat = "p (" + " ".join(names) + ") -> p " + " ".join(names)
    return ap.rearrange(pat, **{n: d for n, d in zip(names, dims)})


class Arena:
    def __init__(self, nc, es, S, name, nbytes):
        self.nbytes = nbytes
        self.t = es.enter_context(nc.sbuf_tensor(name, [128, nbytes // 2], BF16))
        S.rowb[name] = nbytes
        self.off = 0
        self.hi = 0

    def at(self, off, dtype, *shape):
        n = prod(shape)
        nb = n * DTSZ[dtype]
        assert off % 4 == 0 and off + nb <= self.nbytes, (off, nb, self.nbytes)
        a = self.t[:, off // 2: (off + nb) // 2]
        if dtype != BF16:
            a = a.bitcast(dtype)
        if len(shape) > 1:
            a = rs(a, *shape)
        return a

    def alloc(self, dtype, *shape):
        nb = prod(shape) * DTSZ[dtype]
        off = self.off
        self.off = (off + nb + 63) // 64 * 64
        self.hi = max(self.hi, self.off)
        return self.at(off, dtype, *shape)


def build_program():
    nc = bass.Bass("TRN2", target_bir_lowering=False)
    es = contextlib.ExitStack()

    def din(name, shape, dt=F32):
        return nc.dram_tensor(name, list(shape), dt, kind="ExternalInput").ap()

    def dout(name, shape):
        return nc.dram_tensor(name, list(shape), F32, kind="ExternalOutput").ap()

    x_p = din("x_p", [SEQ, D])
    x_s = din("x_s", [NS, D])
    mem = din("mem", [MEM, D])
    ck = din("ck", [NB, MEM, 256])
    cv = din("cv", [NB, MEM, 256])
    st_pool = din("st_pool", [NB, 15, BW])
    st_conv = din("st_conv", [NB, 2, DFF])
    norm_mix_pre = din("norm_mix_pre", [1, D])
    norm_mix_post = din("norm_mix_post", [1, D])
    norm_ffn_pre = din("norm_ffn_pre", [1, D])
    norm_ffn_post = din("norm_ffn_post", [1, D])
    w_in = din("w_in", [D, D_IN])
    b_gate = din("b_gate", [1, 3 * D])
    gmlp_ln_g = din("gmlp_ln_g", [1, AW])
    gmlp_ln_b = din("gmlp_ln_b", [1, AW])
    w_spatial = din("w_spatial", [4, 128, 128])
    b_spatial = din("b_spatial", [4, 128])
    w_pool = din("w_pool", [4, 64, 64])
    pool_scale = din("pool_scale", [1, BW])
    mem_norm = din("mem_norm", [1, D])
    w_mem_kv = din("w_mem_kv", [D, 512])
    w_branch_a = din("w_branch_a", [AW, D])
    w_branch_b = din("w_branch_b", [BW, D])
    w_branch_c = din("w_branch_c", [BW, D])
    w_out = din("w_out", [D, D])
    w_up = din("w_up", [D, 2 * DFF])
    conv_w = din("conv_w", [3, DFF])
    conv_b = din("conv_b", [1, DFF])
    w_down = din("w_down", [DFF, D])
    c_identb = din("c_identb", [128, 128], BF16)
    c_identf = din("c_identf", [128, 128])
    c_maskT = din("c_maskT", [128, 128])
    c_rcnt = din("c_rcnt", [128, 32])

    y_p = dout("y_p", [SEQ, D])
    y_s = dout("y_s", [NS, D])
    o_mk = dout("o_mk", [MEM, 256])
    o_mv = dout("o_mv", [MEM, 256])
    o_pool_p = dout("o_pool_p", [15, BW])
    o_conv_p = dout("o_conv_p", [2, DFF])
    o_v_s = dout("o_v_s", [NS, AW])
    o_pool_s = dout("o_pool_s", [NB, 15, BW])
    o_conv_s = dout("o_conv_s", [NB, 2, DFF])

    with es:
        S = Sched(nc, es)
        banks = []
        for i in range(8):
            t = es.enter_context(nc.psum_tensor(f"ps{i}", [128, 512], F32))
            S.rowb[f"ps{i}"] = 2048
            banks.append(t)
        bank_i = [0]
        bank_n = [8]

        def nbank():
            b = banks[bank_i[0] % bank_n[0]]
            bank_i[0] += 1
            assert not (S.wrec.get(b.name) and not S.rrec.get(b.name)), ("PSUM bank reused before consumption", b.name)
            return b

        PB = 90624
        UB = 121856
        P = Arena(nc, es, S, "arenaP", PB)
        U = Arena(nc, es, S, "arenaU", UB)

        identb = P.alloc(BF16, 128)
        identf = P.alloc(F32, 128)
        onesb = P.alloc(BF16, 128)
        maskT = P.alloc(F32, 128)
        rcnt = P.alloc(F32, 2, 16)
        neghalf = P.alloc(F32, 8)
        V1T = P.alloc(F32, 72)
        V2T = P.alloc(F32, 66)
        hb = P.alloc(F32, 24)
        gpost1h = P.alloc(F32, D)
        gpost2 = P.alloc(F32, D)
        lng = P.alloc(F32, AW)
        lnb = P.alloc(F32, AW)
        bsb = P.alloc(F32, 4, 128)
        ws4 = P.alloc(F32, 4, 4, 4)
        bs4 = P.alloc(F32, 4, 4)
        wsT = P.alloc(BF16, 4, 128)
        Wbd = P.alloc(BF16, 2, 128)
        kT = P.alloc(BF16, 2, MEM)
        Vb = P.alloc(BF16, 2, 256)
        X = P.alloc(F32, 5, D)
        xnT = P.alloc(BF16, KC, NCOL)
        RING_N = 2
        ring = [P.alloc(BF16, 6144) for _ in range(RING_N)]
        bTs = P.alloc(F32, 2, NB, 19)
        gTs = P.alloc(F32, FC, NB, 6)
        halo_g = P.alloc(F32, FC, 2)
        halo_b = P.alloc(F32, 2, 15)
        PSt = P.alloc(F32, 2, 80)
        CS = P.alloc(F32, FC, 34)
        stats = P.alloc(F32, 128)
        assert P.hi <= PB, P.hi
        print('P arena used', P.hi, 'of', PB)

        U.off = 0
        uT = U.alloc(BF16, 4, NCOL)
        qT = U.alloc(BF16, 2, NCOL)
        attnT = U.alloc(BF16, 2, NCOL)
        mixedT = U.alloc(BF16, 2, NCOL)
        pooledT = U.alloc(BF16, 2, NCOL)
        bT = U.alloc(F32, 2, 15 + NG)
        TMP0 = U.off
        junk = U.alloc(BF16, D)
        xnb = [U.alloc(BF16, D) for _ in range(3)]
        vg = [U.alloc(F32, AW) for _ in range(4)]
        vn1 = U.alloc(F32, AW)
        vn2 = U.alloc(F32, AW)
        vnb = [U.alloc(BF16, AW) for _ in range(2)]
        vnS = U.alloc(F32, AW)
        spt = U.alloc(F32, 4, 128)
        vnTs = U.alloc(F32, 4, NB, 4)
        sgs = U.alloc(F32, 4, NB, 4)
        s2 = U.alloc(F32, 2, 15 + NG)
        s4 = U.alloc(F32, 2, 15 + NG)
        s8 = U.alloc(F32, 15 + NG)
        s16 = U.alloc(F32, 15 + NG)
        ptmp = U.alloc(F32, 2, 16)
        expT = [U.alloc(BF16, 2, 512) for _ in range(2)]
        rd = [U.alloc(F32, 512) for _ in range(2)]
        Ksb = [U.alloc(BF16, 2, 256) for _ in range(2)]
        Vsb = [U.alloc(BF16, 2, 256) for _ in range(2)]
        kTb = [U.alloc(BF16, 2, 256) for _ in range(2)]
        expS = [U.alloc(BF16, 2, 16) for _ in range(2)]
        rdS = U.alloc(F32, NB, 16)
        TMP1 = U.off
        stage = U.alloc(F32, 2816)
        U.off = TMP0
        tg = [U.alloc(F32, 512) for _ in range(3)]
        pj = [U.alloc(F32, 512) for _ in range(3)]
        dtmp = [U.alloc(F32, 512) for _ in range(2)]
        junk2 = U.alloc(BF16, D)
        hnb = [U.alloc(BF16, D) for _ in range(3)]
        assert U.off <= TMP1
        U.off = TMP1
        mergedT = U.alloc(BF16, KC, NCOL)
        wba = U.alloc(BF16, 4, D)
        WBB_OFF = U.off
        wbb = U.alloc(BF16, 2, D)
        wbc = U.alloc(BF16, 2, D)
        wout = U.alloc(BF16, KC, D)
        MIX_END = U.off
        assert MIX_END <= UB, MIX_END
        print('U mixer layout end', MIX_END, 'of', UB)
        U.off = 0
        wdn = U.alloc(BF16, FC, D)
        actT = U.alloc(BF16, FC, NCOL)
        gTp = [U.alloc(F32, 2 + NG) for _ in range(2)]
        cT = [U.alloc(F32, NG) for _ in range(2)]
        ge = [U.alloc(F32, NG) for _ in range(2)]
        cTs = U.alloc(F32, NB, 4)
        geS = U.alloc(F32, NB, 4)
        junk3 = U.alloc(BF16, 512)
        ftmp = [U.alloc(F32, 512) for _ in range(2)]
        PStT = U.alloc(F32, 256)
        CST = U.alloc(F32, DFF)
        junkA = U.alloc(BF16, D)
        xnbA = [U.alloc(BF16, D) for _ in range(3)]
        assert U.off <= UB, U.off
        print('U ffn layout end', U.off, 'of', UB)
        U.off = max(U.off, MIX_END)
        gffnb = U.alloc(F32, D)
        assert U.off <= UB, U.off

        sp = "sp"
        S.dma(sp, identb, c_identb[:, :])
        S.dma(sp, identf, c_identf[:, :])
        S.dma(sp, maskT, c_maskT[:, :])
        S.dma(sp, rcnt, c_rcnt[:, :].rearrange("p (c t) -> p c t", c=2))
        S.dma(sp, X[0:NS, 4, :], x_s[:, :])
        for i in range(NG // 128):
            S.dma(sp, X[:, i, :], x_p[i * 128:(i + 1) * 128, :])
        for mt in range(2):
            S.dma(sp, [s2.rearrange("p c l -> p (c l)")[:, 0:D], s4.rearrange("p c l -> p (c l)")[:, 0:D]][mt], mem[mt * 128:(mt + 1) * 128, :])
        S.memset("dve", onesb, 1.0)
        S.memset("dve", neghalf, -0.5)
        S.memset("dve", halo_g, 0.0)
        S.memset("dve", halo_b, 0.0)

        st1 = stage[:, 0:128]
        S.dma(sp, st1[0:8, :], norm_mix_pre.rearrange("o (r p) -> (o r) p", p=128))
        S.dma(sp, st1[8:16, :], norm_ffn_pre.rearrange("o (r p) -> (o r) p", p=128))
        S.dma(sp, st1[16:40, :], b_gate.rearrange("o (r p) -> (o r) p", p=128))
        S.dma(sp, st1[40:42, :], pool_scale.rearrange("o (r p) -> (o r) p", p=128))
        S.dma(sp, st1[42:64, :], conv_b.rearrange("o (r p) -> (o r) p", p=128))
        S.dma(sp, st1[64:72, :], mem_norm.rearrange("o (r p) -> (o r) p", p=128))
        bk = nbank()
        S.tr(bk[:, 0:72], st1[0:72, :], identf[0:72, 0:72])
        S.copy("dve", V1T, bk[:, 0:72])
        st2 = stage[:, 128:256]
        S.dma(sp, st2[0:66, :], conv_w.rearrange("k (r p) -> (k r) p", p=128))
        bk = nbank()
        S.tr(bk[:, 0:66], st2[0:66, :], identf[0:66, 0:66])
        S.copy("dve", V2T, bk[:, 0:66])
        S.ts("dve", hb, V1T[:, 16:40], 0.5, None, ALU.mult)

        S.dma(sp, gpost1h, norm_mix_post[0, :].partition_broadcast(128))
        S.ts("dve", gpost1h, gpost1h, 0.5, None, ALU.mult)
        S.dma(sp, gpost2, norm_ffn_post[0, :].partition_broadcast(128))
        S.dma(sp, gffnb, norm_ffn_pre[0, :].partition_broadcast(128))
        S.dma(sp, lng, gmlp_ln_g[0, :].partition_broadcast(128))
        S.dma(sp, lnb, gmlp_ln_b[0, :].partition_broadcast(128))
        S.dma(sp, bsb, b_spatial.partition_broadcast(128))
        for g in range(4):
            S.dma(sp, ws4[:, g, :, :], w_spatial[g, 0:4, 0:4].partition_broadcast(128))
        S.dma(sp, bs4, b_spatial[:, 0:4].partition_broadcast(128))

        wst = rs(stage[:, 256:768], 4, 128)
        for g in range(4):
            S.dma(sp, wst[:, g, :], w_spatial[g, :, :])
            bk = nbank()
            S.tr(bk[:, 0:128], wst[:, g, :], identf)
            S.tt("dve", wsT[:, g, :], bk[:, 0:128], maskT, ALU.mult)

        wbs = rs(stage[:, 768:1024], 2, 128)
        S.memset("dve", wbs, 0.0)
        for c in range(2):
            S.dma(sp, wbs[0:64, c, 0:64], w_pool[2 * c, :, :])
            S.dma(sp, wbs[64:128, c, 64:128], w_pool[2 * c + 1, :, :])
        S.copy("dve", Wbd, wbs)

        sps = rs(stage[:, 1024:1536], 2, 256)
        stp = st_pool.rearrange("b r f -> (b r) f")
        for hh in range(2):
            S.dma(sp, sps[0:120, hh, :], stp[hh * 120:(hh + 1) * 120, :])
        for hh in range(2):
            for c in range(2):
                bk = nbank()
                S.tr(bk[:, 0:120], sps[0:120, hh, c * 128:(c + 1) * 128], identf[0:120, 0:120])
                S.copy("dve", bTs[:, c, hh * 8:(hh + 1) * 8, 0:15], rs(bk[:, 0:120], 8, 15))
        S.dma(sp, o_pool_s[:, 0:11, :], st_pool[:, 4:15, :], is_out=True)

        scs = stage[0:32, 0:DFF]
        S.dma(sp, scs, st_conv.rearrange("b r f -> (b r) f"))
        for half in range(2):
            bk = nbank()
            for j in range(11):
                fc = half * 11 + j
                S.tr(bk[:, j * 32:(j + 1) * 32], scs[:, fc * 128:(fc + 1) * 128], identf[0:32, 0:32])
            S.copy("dve", gTs[:, half * 11:(half + 1) * 11, :, 0:2], rs(bk[:, 0:352], 11, NB, 2))

        S.mark('setup_done')
        def rstd_pool(out, acc, rows):
            S.ts("pool", out[0:rows], acc[0:rows], EPS, None, ALU.add)
            S.tt("pool", out[0:rows], out[0:rows], neghalf[0:rows, 0:1], ALU.pow)

        def pipeline(stages, n):
            ns = len(stages)
            for k in range(n + ns - 1):
                for st in range(ns - 1, -1, -1):
                    i = k - st
                    if 0 <= i < n:
                        stages[st](i)

        def norm_transpose_stages(items, gcol0, dstT, jk, xbs, col0, gb=None):
            st = {}

            def s_sq(i):
                src, rows, c0 = items[i]
                acc = stats[:, col0 + 2 * i: col0 + 2 * i + 1]
                S.act(jk[0:rows], src, AF.Square, scale=1.0 / 32.0, accum_out=acc[0:rows])

            def s_rstd(i):
                src, rows, c0 = items[i]
                rstd_pool(stats[:, col0 + 2 * i + 1: col0 + 2 * i + 2], stats[:, col0 + 2 * i: col0 + 2 * i + 1], rows)

            def s_norm(i):
                src, rows, c0 = items[i]
                xb = xbs[i % len(xbs)]
                if gb is not None:
                    S.stt("dve", xb[0:rows], src, stats[0:rows, col0 + 2 * i + 1: col0 + 2 * i + 2], gb[0:rows], ALU.mult, ALU.mult)
                else:
                    S.ts("dve", xb[0:rows], src, stats[0:rows, col0 + 2 * i + 1: col0 + 2 * i + 2], None, ALU.mult)

            def s_tr(i):
                src, rows, c0 = items[i]
                xb = xbs[i % len(xbs)]
                bk = nbank()
                st[i] = bk
                bkb = bk[:].bitcast(BF16)
                for kc in range(KC):
                    S.tr(bkb[:, kc * 128: kc * 128 + rows], xb[0:rows, kc * 128:(kc + 1) * 128], identb[0:rows, 0:rows])

            def s_evac(i):
                src, rows, c0 = items[i]
                bkb = st[i][:].bitcast(BF16)
                if gb is not None:
                    S.act(dstT[:, :, c0:c0 + rows], bkb.rearrange("p (k t) -> p k t", k=KC)[:, :, 0:rows], AF.Copy)
                    return
                for kc in range(KC):
                    S.ts("dve", dstT[:, kc, c0:c0 + rows], bkb[:, kc * 128: kc * 128 + rows], V1T[:, gcol0 + kc:gcol0 + kc + 1], None, ALU.mult)

            return [s_sq, s_rstd, s_norm, s_tr, s_evac]

        ring_i = [0]

        def ring_slot():
            r = ring[ring_i[0] % RING_N]
            ring_i[0] += 1
            return r

        def load_w(dst3, src2d):
            S.dma("pool", dst3, src2d.rearrange("(kc p) f -> p kc f", p=128))

        memX = [s2.rearrange("p c l -> p (c l)")[:, 0:D], s4.rearrange("p c l -> p (c l)")[:, 0:D]]
        wb_first = rs(ring_slot()[:, 0:KC * 512], KC, 512)
        load_w(wb_first, w_in[:, 0:512])
        wkv = U.at(WBB_OFF, BF16, KC, 512)
        load_w(wkv, w_mem_kv)
        wb1_first = rs(ring_slot()[:, 0:KC * 512], KC, 512)
        load_w(wb1_first, w_in[:, 512:1024])
        memnT = mergedT
        pipeline(norm_transpose_stages([(memX[mt], 128, mt * 128) for mt in range(2)], 64, memnT, junk, xnb, 112), 2)

        def kv_part2():
            for hp in range(2):
                bk = nbank()
                for kc in range(KC):
                    S.mm(bk[:, 0:MEM], wkv[:, kc, hp * 128:(hp + 1) * 128], memnT[:, kc, 0:MEM], start=kc == 0, stop=kc == KC - 1)
                S.copy("act", kT[:, hp, :], bk[:, 0:MEM])
            kvt = [pj[0], pj[1]]
            for mc in range(2):
                bk = nbank()
                for kc in range(KC):
                    S.mm(bk[:, :], memnT[:, kc, mc * 128:(mc + 1) * 128], wkv[:, kc, :], start=kc == 0, stop=kc == KC - 1)
                S.copy("act", kvt[mc], bk[:, :])
                S.copy("dve", Vb[:, mc, :], bk[:, 256:512])
                S.dma(sp, o_mk[mc * 128:(mc + 1) * 128, :], kvt[mc][:, 0:256], is_out=True)
                S.dma(sp, o_mv[mc * 128:(mc + 1) * 128, :], kvt[mc][:, 256:512], is_out=True)

        S.mark('kv_done')
        def pooling(xb4, s2v, s4v, s8v, s16v, outv, L, first):
            x = xb4
            S.tt("dve", s2v[:, :, :, 1:L], x[:, :, :, 1:L], x[:, :, :, 0:L - 1], ALU.add)
            S.tt("dve", s4v[64:128, 0, :, 3:L], s2v[64:128, 0, :, 3:L], s2v[64:128, 0, :, 1:L - 2], ALU.add)
            S.tt("dve", s4v[:, 1, :, 3:L], s2v[:, 1, :, 3:L], s2v[:, 1, :, 1:L - 2], ALU.add)
            S.tt("dve", s8v[:, :, 7:L], s4v[:, 1, :, 7:L], s4v[:, 1, :, 3:L - 4], ALU.add)
            S.tt("dve", s16v[64:128, :, 15:L], s8v[64:128, :, 15:L], s8v[64:128, :, 7:L - 8], ALU.add)
            srcs = [(0, 64, 0, s2v[0:64, 0], 0.5), (64, 128, 0, s4v[64:128, 0], 0.25),
                    (0, 64, 1, s8v[0:64], 0.125), (64, 128, 1, s16v[64:128], 1.0 / 16)]
            for p0, p1, c, sv, rw in srcs:
                S.stt("dve", outv[p0:p1, c, :, :], sv[:, :, 15:L], rw, x[p0:p1, c, :, 15:L], ALU.mult, ALU.subtract)
            if first:
                for p0, p1, c, sv, rw in srcs:
                    S.tt("dve", ptmp[p0:p1, c, :], sv[:, 0, 15:31], rcnt[p0:p1, c, :], ALU.mult)
                    S.tt("dve", outv[p0:p1, c, 0, 0:16], ptmp[p0:p1, c, :], x[p0:p1, c, 0, 15:31], ALU.subtract)

        def make_tiles(g):
            t = [((4 * g + i) % 5, 128, i * 128, i) for i in range(NG // 128)]
            if g == 0:
                t = [(4, NS, NG, -1)] + t
            return t

        def a_load(g, t):
            xi, rows, c0, pi = t
            if pi >= 0:
                S.dma(sp, X[:, xi, :], x_p[g * NG + pi * 128: g * NG + (pi + 1) * 128, :])
            else:
                S.dma(sp, X[0:NS, xi, :], x_s[:, :])

        def a_stages(tl):
            return norm_transpose_stages([(X[0:rows, xi, :], rows, c0) for (xi, rows, c0, pi) in tl], 0, xnT, junkA, xnbA, 0)

        def prefetch_win01():
            a = rs(ring_slot()[:, 0:KC * 512], KC, 512)
            load_w(a, w_in[:, 0:512])
            b = rs(ring_slot()[:, 0:KC * 512], KC, 512)
            load_w(b, w_in[:, 512:1024])
            return a, b

        class Stepper:
            def __init__(self, stages, n):
                self.stages, self.n, self.k = stages, n, 0

            def done(self):
                return self.k >= self.n + len(self.stages) - 1

            def step(self):
                ns = len(self.stages)
                for st in range(ns - 1, -1, -1):
                    i = self.k - st
                    if 0 <= i < self.n:
                        self.stages[st](i)
                self.k += 1

        for g in range(NGRP):
            first = g == 0
            last = g == NGRP - 1
            tiles = make_tiles(g)
            blocks = [(0, NG)]
            if first:
                blocks.append((NG, NS))

            if first:
                pipeline(a_stages(tiles), len(tiles))
                kv_part2()


            S.mark(f'g{g}_A_done')
            if first:
                wb = wb_first
                wb1 = wb1_first
            else:
                wb, wb1 = nxt_w01
            for (c0, n) in blocks:
                for fc in range(4):
                    bk = nbank()
                    for kc in range(KC):
                        S.mm(bk[:, 0:n], wb[:, kc, fc * 128:(fc + 1) * 128], xnT[:, kc, c0:c0 + n], start=kc == 0, stop=kc == KC - 1)
                    S.act(uT[:, fc, c0:c0 + n], bk[:, 0:n], AF.Gelu_apprx_tanh)

            S.mark(f'g{g}_B0_done')
            wb2 = rs(ring_slot()[:, 0:KC * 512], KC, 512)
            load_w(wb2, w_in[:, 1024:1536])
            b1 = {}
            SB1 = 48

            def b1_mm(ti):
                xi, rows, c0, pi = tiles[ti]
                bk = nbank()
                b1[ti] = bk
                for kc in range(KC):
                    S.mm(bk[0:rows, :], xnT[:, kc, c0:c0 + rows], wb1[:, kc, :], start=kc == 0, stop=kc == KC - 1)

            def b1_gelu(ti):
                xi, rows, c0, pi = tiles[ti]
                S.act(vg[ti % 4][0:rows], b1[ti][0:rows, :], AF.Gelu_apprx_tanh)

            def b1_stats(ti):
                xi, rows, c0, pi = tiles[ti]
                cb = SB1 + 9 * ti
                S.bn_stats(stats[0:rows, cb:cb + 6], vg[ti % 4][0:rows])
                S.bn_aggr(stats[0:rows, cb + 6:cb + 8], stats[0:rows, cb:cb + 6])

            def b1_rstd(ti):
                xi, rows, c0, pi = tiles[ti]
                cb = SB1 + 9 * ti
                rstd_pool(stats[:, cb + 8:cb + 9], stats[:, cb + 7:cb + 8], rows)

            def b1_norm(ti):
                xi, rows, c0, pi = tiles[ti]
                cb = SB1 + 9 * ti
                S.ts("dve", vn1[0:rows], vg[ti % 4][0:rows], stats[0:rows, cb + 6:cb + 7], stats[0:rows, cb + 8:cb + 9], ALU.subtract, ALU.mult)
                S.tt("dve", vn2[0:rows], vn1[0:rows], lng[0:rows], ALU.mult)
                if pi >= 0:
                    S.tt("dve", vnb[ti % 2], vn2, lnb, ALU.add)
                else:
                    S.tt("dve", vnS[0:NS], vn2[0:NS], lnb[0:NS], ALU.add)
                    S.dma(sp, o_v_s[:, :], vnS[0:NS, :], is_out=True)

            def b1_spatial(ti):
                xi, rows, c0, pi = tiles[ti]
                bk2 = nbank()
                b1[("sp", ti)] = bk2
                if pi >= 0:
                    vb = vnb[ti % 2]
                    for gch in range(4):
                        S.mm(bk2[:, gch * 128:(gch + 1) * 128], vb[:, gch * 128:(gch + 1) * 128], wsT[:, gch, :], start=True, stop=True)
                else:
                    for gch in range(4):
                        S.tr(bk2[:, gch * 64:(gch + 1) * 64], vnS[0:NS, gch * 128:(gch + 1) * 128], identf[0:NS, 0:NS])

            def b1_gate(ti):
                xi, rows, c0, pi = tiles[ti]
                bk2 = b1[("sp", ti)]
                if pi >= 0:
                    S.tt("dve", spt, rs(bk2[:, :], 4, 128), bsb, ALU.add)
                    S.tt("dve", uT[:, :, c0:c0 + 128], spt, uT[:, :, c0:c0 + 128], ALU.mult)
                else:
                    S.copy("dve", vnTs, rs(bk2[:, 0:256], 4, NB, 4))
                    for sidx in range(4):
                        for gch in range(4):
                            for t in range(sidx, 4):
                                acc = sgs[:, gch, :, t]
                                if sidx == 0:
                                    S.ts("dve", acc, vnTs[:, gch, :, 0], ws4[:, gch, t, 0:1], bs4[:, gch, t:t + 1], ALU.mult, ALU.add)
                                else:
                                    S.stt("dve", acc, vnTs[:, gch, :, sidx], ws4[:, gch, t, sidx:sidx + 1], acc, ALU.mult, ALU.add)
                    S.tt("dve", uT[:, :, NG:NG + NS], sgs.rearrange("p g b t -> p g (b t)"), uT[:, :, NG:NG + NS], ALU.mult)

            def b2_unit(c0, n, j):
                def f():
                    bk = nbank()
                    for kc in range(KC):
                        S.mm(bk[:, 0:n], wb2[:, kc, j * 128:(j + 1) * 128], xnT[:, kc, c0:c0 + n], start=kc == 0, stop=kc == KC - 1)
                    if j < 2:
                        if c0 == 0:
                            S.copy("act", bT[:, j, 15:15 + NG], bk[:, 0:NG])
                        else:
                            S.copy("act", bTs[:, j, :, 15:19], rs(bk[:, 0:NS], NB, 4))
                    else:
                        S.copy("act", qT[:, j - 2, c0:c0 + n], bk[:, 0:n])
                return f

            b2_units = [b2_unit(c0, n, j) for (c0, n) in blocks for j in range(4)]
            b1step = Stepper([b1_mm, b1_gelu, b1_stats, b1_rstd, b1_norm, b1_spatial, b1_gate], len(tiles))
            while not b1step.done():
                b1step.step()
                if b2_units and b1step.k >= 5:
                    b2_units.pop(0)()
            aT = uT

            def gate_block(i):
                wg = rs(ring_slot()[:, 0:KC * 768], KC, 3, 256)
                for j in range(3):
                    c = 1536 + j * 1024 + i * 256
                    load_w(wg[:, :, j, :], w_in[:, c:c + 256])
                return wg

            S.mark(f'g{g}_B1_done')
            while b2_units:
                b2_units.pop(0)()
            wg_next = gate_block(0)
            load_w(wba, w_branch_a)
            load_w(wbb, w_branch_b)
            load_w(wbc, w_branch_c)

            S.mark(f'g{g}_B2_done')
            S.mark(f'g{g}_pool_done')
            sb = {}

            def att_scores(h):
                hp, base = h // 2, (h % 2) * 64
                ex = expT[h % 2]
                for mc in range(2):
                    bk = nbank()
                    S.mm(bk[:, 0:NG], kT[base:base + 64, hp, mc * 128:(mc + 1) * 128], qT[base:base + 64, hp, 0:NG], start=True, stop=True)
                    S.act(ex[:, mc, :], bk[:, 0:NG], AF.Exp, scale=0.125)

            def att_pv(h):
                hp, base = h // 2, (h % 2) * 64
                ex = expT[h % 2]
                pv = nbank()
                for mc in range(2):
                    S.mm(pv[:, 0:NG], Vb[:, mc, hp * 128:(hp + 1) * 128], ex[:, mc, :], start=mc == 0, stop=mc == 1)
                dn = nbank()
                for mc in range(2):
                    S.mm(dn[:, 0:NG], onesb, ex[:, mc, :], start=mc == 0, stop=mc == 1)
                r = rd[h % 2]
                S.recip(r[base:base + 64, :], dn[base:base + 64, 0:NG])
                S.tt("dve", attnT[base:base + 64, hp, 0:NG], pv[base:base + 64, 0:NG], r[base:base + 64, :], ALU.mult)

            for i in range(5):
                if i < 4:
                    att_scores(i)
                if i >= 1:
                    att_pv(i - 1)

            L = 15 + NG
            S.copy("dve", bT[:, :, 0:15], halo_b)
            pooling(bT.rearrange("p c (b l) -> p c b l", b=1), s2.rearrange("p c (b l) -> p c b l", b=1),
                    s4.rearrange("p c (b l) -> p c b l", b=1), s8.rearrange("p (b l) -> p b l", b=1),
                    s16.rearrange("p (b l) -> p b l", b=1),
                    pooledT[:, :, 0:NG].rearrange("p c (b l) -> p c b l", b=1), L, first)
            if last:
                S.copy("dve", PSt[:, :, 64:79], bT[:, :, L - 15:L])
            else:
                S.copy("dve", halo_b, bT[:, :, L - 15:L])
            if first:
                s2s = rs(s2.rearrange("p c l -> p (c l)")[:, 0:2 * NB * 19], 2, NB, 19)
                s4s = rs(s4.rearrange("p c l -> p (c l)")[:, 0:2 * NB * 19], 2, NB, 19)
                s8s = rs(s8[:, 0:NB * 19], NB, 19)
                s16s = rs(s16[:, 0:NB * 19], NB, 19)
                pooling(bTs, s2s, s4s, s8s, s16s, pooledT[:, :, NG:NG + NS].rearrange("p c (b t) -> p c b t", b=NB), 19, False)
                S.copy("dve", PSt[:, :, 0:64].rearrange("p c (t b) -> p c b t", t=4), bTs[:, :, :, 15:19])

            def glin():
                for (c0, n) in blocks:
                    for c in range(2):
                        bk = nbank()
                        S.mm(bk[:, 0:n], Wbd[:, c, :], pooledT[:, c, c0:c0 + n], start=True, stop=True)
                        S.act(mixedT[:, c, c0:c0 + n], bk[:, 0:n], AF.Copy, scale=V1T[:, 40 + c:41 + c])

            glin_pending = [glin]

            S.mark(f'g{g}_attp_done')
            if first:
                scol = NG

                def s_load(b):
                    S.dma("pool", Ksb[b % 2], ck[b].rearrange("(mc m) f -> m mc f", m=128))
                    S.dma("pool", Vsb[b % 2], cv[b].rearrange("(mc m) f -> m mc f", m=128))

                def s_tr(b):
                    bk = nbank()
                    bkb = bk[:].bitcast(BF16)
                    for hp in range(2):
                        for mc in range(2):
                            S.tr(bkb[:, hp * 256 + mc * 128: hp * 256 + (mc + 1) * 128], Ksb[b % 2][:, mc, hp * 128:(hp + 1) * 128], identb)
                    S.copy("act", kTb[b % 2], rs(bkb[:, 0:512], 2, 256))

                bank_n[0] = 6
                OS = banks[6]
                DS = banks[7]
                OSv = rs(OS[:, 0:256], NB, 4, 4)
                DSv = rs(DS[:, 0:256], NB, 4, 4)

                def s_scores(b):
                    bkp = [nbank(), nbank()]
                    ex5 = expS[b % 2].rearrange("p mc (hp par t) -> p mc hp par t", hp=2, par=2)
                    for par in range(2):
                        base = par * 64
                        sv = rs(bkp[par][:, 0:16], 2, 2, 4)
                        for hp in range(2):
                            for mc in range(2):
                                S.mm(sv[:, mc, hp, :], kTb[b % 2][base:base + 64, hp, mc * 128:(mc + 1) * 128],
                                     qT[base:base + 64, hp, scol + b * 4: scol + b * 4 + 4], start=True, stop=True)
                    for par in range(2):
                        S.act(ex5[:, :, :, par, :], rs(bkp[par][:, 0:16], 2, 2, 4), AF.Exp, scale=0.125)

                def s_pv(b):
                    ex = rs(expS[b % 2].rearrange("p mc x -> p (mc x)"), 2, 4, 4)
                    for h in range(4):
                        hp = h // 2
                        for mc in range(2):
                            S.mm(OSv[:, b, h, :], Vsb[b % 2][:, mc, hp * 128:(hp + 1) * 128], ex[:, mc, h, :], start=mc == 0, stop=mc == 1)
                    for mc in range(2):
                        S.mm(DSv[:, b, :, :].rearrange("p h t -> p (h t)"), onesb, expS[b % 2][:, mc, :], start=mc == 0, stop=mc == 1)

                def s_loadk(b):
                    S.dma("pool", Ksb[b % 2], ck[b].rearrange("(mc m) f -> m mc f", m=128))

                def s_loadv(b):
                    S.dma("pool", Vsb[b % 2], cv[b].rearrange("(mc m) f -> m mc f", m=128))

                s_loadk(0)
                s_loadk(1)
                s_loadv(0)
                s_tr(0)
                for b in range(NB):
                    if b + 1 < NB:
                        s_tr(b + 1)
                    s_scores(b)
                    if b >= 1:
                        s_pv(b - 1)
                    if b + 2 < NB:
                        s_loadk(b + 2)
                    if b + 1 < NB:
                        s_loadv(b + 1)
                s_pv(NB - 1)
                S.recip(rdS.rearrange("p b x -> p (b x)"), DS[:, 0:256])
                rdv = rs(rdS.rearrange("p b x -> p (b x)"), NB, 4, 4)
                for h in range(4):
                    hp, base = h // 2, (h % 2) * 64
                    S.tt("dve", attnT[base:base + 64, hp, scol:scol + NS].rearrange("p (b t) -> p b t", b=NB),
                         OSv[base:base + 64, :, h, :], rdv[base:base + 64, :, h, :], ALU.mult)
                bank_n[0] = 8

            S.mark(f'g{g}_atts_done')
            load_w(wout, w_out)
            for i in range(4):
                wg = wg_next
                if i + 1 < 4:
                    wg_next = gate_block(i + 1)
                for fl in range(2):
                    fo = 2 * i + fl
                    for (c0, n) in blocks:
                        gb = []
                        for j in range(3):
                            bk = nbank()
                            for kc in range(KC):
                                S.mm(bk[:, 0:n], wg[:, kc, j, fl * 128:(fl + 1) * 128], xnT[:, kc, c0:c0 + n], start=kc == 0, stop=kc == KC - 1)
                            S.act(tg[j][:, 0:n], bk[:, 0:n], AF.Tanh, bias=hb[:, j * 8 + fo: j * 8 + fo + 1], scale=0.5)
                            gb.append(bk)
                        if glin_pending:
                            glin_pending.pop()()
                        srcs = [(wba, aT, 4), (wbb, mixedT, 2), (wbc, attnT, 2)]
                        for j, (wbr, actv, nk) in enumerate(srcs):
                            bk = nbank()
                            for kc in range(nk):
                                S.mm(bk[:, 0:n], wbr[:, kc, fo * 128:(fo + 1) * 128], actv[:, kc, c0:c0 + n], start=kc == 0, stop=kc == nk - 1)
                            S.stt("dve", pj[j][:, 0:n], tg[j][:, 0:n], 1.0, bk[:, 0:n], ALU.add, ALU.mult)
                        S.tt("pool", pj[0][:, 0:n], pj[0][:, 0:n], pj[1][:, 0:n], ALU.add)
                        S.tt("pool", mergedT[:, fo, c0:c0 + n], pj[0][:, 0:n], pj[2][:, 0:n], ALU.add)

            S.mark(f'g{g}_C_done')
            def up_block(jb):
                wu = rs(ring_slot()[:, 0:KC * 512], KC, 2, 256)
                load_w(wu[:, :, 0, :], w_up[:, jb * 256:(jb + 1) * 256])
                load_w(wu[:, :, 1, :], w_up[:, DFF + jb * 256: DFF + (jb + 1) * 256])
                return wu

            wu_first = up_block(0)
            for k0 in (0, 5, 10):
                k1 = min(k0 + 5, 14)
                load_w(wdn[:, k0:k1, :], w_down[k0 * 128:k1 * 128, :])
            dd = {}
            SD = 96

            def d_mm(ti):
                xi, rows, c0, pi = tiles[ti]
                bks = []
                for half in range(2):
                    bk = nbank()
                    for kc in range(KC):
                        S.mm(bk[0:rows, :], mergedT[:, kc, c0:c0 + rows], wout[:, kc, half * 512:(half + 1) * 512], start=kc == 0, stop=kc == KC - 1)
                    bks.append(bk)
                dd[ti] = bks

            def d_sq(ti):
                xi, rows, c0, pi = tiles[ti]
                for half in range(2):
                    S.act(junk2[0:rows, 0:512], dd[ti][half][0:rows, :], AF.Square, scale=1.0 / 64.0,
                          accum_out=stats[0:rows, SD + 3 * ti + half: SD + 3 * ti + half + 1])

            def d_rstd(ti):
                xi, rows, c0, pi = tiles[ti]
                racc = stats[:, SD + 3 * ti + 2: SD + 3 * ti + 3]
                S.tt("pool", racc[0:rows], stats[0:rows, SD + 3 * ti: SD + 3 * ti + 1], stats[0:rows, SD + 3 * ti + 1: SD + 3 * ti + 2], ALU.add)
                rstd_pool(racc, racc, rows)

            def d_res(ti):
                xi, rows, c0, pi = tiles[ti]
                racc = stats[:, SD + 3 * ti + 2: SD + 3 * ti + 3]
                for half in range(2):
                    dt_ = dtmp[half]
                    S.stt("dve", dt_[0:rows], dd[ti][half][0:rows, :], racc[0:rows], gpost1h[0:rows, half * 512:(half + 1) * 512], ALU.mult, ALU.mult)
                for half in range(2):
                    S.tt("dve" if half == 0 else "pool", X[0:rows, xi, half * 512:(half + 1) * 512], X[0:rows, xi, half * 512:(half + 1) * 512], dtmp[half][0:rows], ALU.add)

            hn_stages = norm_transpose_stages([(X[0:rows, xi, :], rows, c0) for (xi, rows, c0, pi) in tiles], 8, xnT, junk2, hnb, 16, gb=gffnb)
            def d_sq_rstd(ti):
                d_sq(ti)
                d_rstd(ti)

            def d_res_sq(ti):
                d_res(ti)
                hn_stages[0](ti)

            pipeline([d_mm, d_sq_rstd, d_res_sq] + hn_stages[1:], len(tiles))
            hnT = xnT

            S.mark(f'g{g}_D_done')
            if not last:
                ntiles = make_tiles(g + 1)
                nloaded = 0
                cur_slots = [t[0] for t in tiles]
                while nloaded < len(ntiles) and ntiles[nloaded][0] not in cur_slots:
                    a_load(g + 1, ntiles[nloaded])
                    nloaded += 1
                astep = Stepper(a_stages(ntiles), len(ntiles))
            wu_next = wu_first
            for jb in range(FC // 2):
                wu = wu_next
                if jb + 1 < FC // 2:
                    wu_next = up_block(jb + 1)
                if jb < 4:
                    k0 = 14 + 2 * jb
                    load_w(wdn[:, k0:k0 + 2, :], w_down[k0 * 128:(k0 + 2) * 128, :])
                for fl in range(2):
                    fc = 2 * jb + fl
                    cw0 = V2T[:, fc:fc + 1]
                    cw1 = V2T[:, 22 + fc:23 + fc]
                    cw2 = V2T[:, 44 + fc:45 + fc]
                    cbv = V1T[:, 42 + fc:43 + fc]
                    for (c0, n) in blocks:
                        gbk = nbank()
                        for kc in range(KC):
                            S.mm(gbk[:, 0:n], wu[:, kc, 0, fl * 128:(fl + 1) * 128], hnT[:, kc, c0:c0 + n], start=kc == 0, stop=kc == KC - 1)
                        ubk = nbank()
                        for kc in range(KC):
                            S.mm(ubk[:, 0:n], wu[:, kc, 1, fl * 128:(fl + 1) * 128], hnT[:, kc, c0:c0 + n], start=kc == 0, stop=kc == KC - 1)
                        if c0 == 0:
                            gt = gTp[fc % 2]
                            ct = cT[fc % 2]
                            gg = ge[fc % 2]
                            S.copy("pool", gt[:, 0:2], halo_g[:, fc, :])
                            S.copy("act", gt[:, 2:2 + NG], gbk[:, 0:NG])
                            if last:
                                S.copy("pool", CS[:, fc, 32:34], gt[:, NG:NG + 2])
                            else:
                                S.copy("pool", halo_g[:, fc, :], gt[:, NG:NG + 2])
                            S.ts("dve", ct, gt[:, 2:2 + NG], cw2, cbv, ALU.mult, ALU.add)
                            S.stt("dve", ct, gt[:, 1:1 + NG], cw1, ct, ALU.mult, ALU.add)
                            S.stt("dve", ct, gt[:, 0:NG], cw0, ct, ALU.mult, ALU.add)
                            S.act(gg, ct, AF.Gelu_apprx_tanh)
                            S.tt("dve", actT[:, fc, 0:NG], ubk[:, 0:NG], gg, ALU.mult)
                        else:
                            S.copy("act", gTs[:, fc, :, 2:6], rs(gbk[:, 0:NS], NB, 4))
                            S.copy("pool", CS[:, fc, 0:32].rearrange("p (r b) -> p b r", r=2), gTs[:, fc, :, 4:6])
                            S.ts("dve", cTs, gTs[:, fc, :, 2:6], cw2, cbv, ALU.mult, ALU.add)
                            S.stt("dve", cTs, gTs[:, fc, :, 1:5], cw1, cTs, ALU.mult, ALU.add)
                            S.stt("dve", cTs, gTs[:, fc, :, 0:4], cw0, cTs, ALU.mult, ALU.add)
                            S.act(geS, cTs, AF.Gelu_apprx_tanh)
                            S.tt("dve", actT[:, fc, NG:NG + NS], ubk[:, 0:NS], geS.rearrange("p b t -> p (b t)"), ALU.mult)

            S.mark(f'g{g}_E_done')
            if not last:
                nxt_w01 = prefetch_win01()
            for ti, (xi, rows, c0, pi) in enumerate(tiles):
                acc2 = stats[:, 40:42]
                racc = stats[:, 42:43]
                bks = []
                for half in range(2):
                    bk = nbank()
                    for kc in range(FC):
                        S.mm(bk[0:rows, :], actT[:, kc, c0:c0 + rows], wdn[:, kc, half * 512:(half + 1) * 512], start=kc == 0, stop=kc == FC - 1)
                    S.act(junk3[0:rows], bk[0:rows, :], AF.Square, scale=1.0 / 32.0, accum_out=acc2[0:rows, half:half + 1])
                    bks.append(bk)
                S.tt("pool", racc[0:rows], acc2[0:rows, 0:1], acc2[0:rows, 1:2], ALU.add)
                rstd_pool(racc, racc, rows)
                for half in range(2):
                    ft = ftmp[half]
                    S.stt("dve", ft[0:rows], bks[half][0:rows, :], racc[0:rows], gpost2[0:rows, half * 512:(half + 1) * 512], ALU.mult, ALU.mult)
                    S.tt("pool", X[0:rows, xi, half * 512:(half + 1) * 512], X[0:rows, xi, half * 512:(half + 1) * 512], ft[0:rows], ALU.add)
                if pi >= 0:
                    S.dma(sp, y_p[g * NG + pi * 128: g * NG + (pi + 1) * 128, :], X[:, xi, :], is_out=True)
                else:
                    S.dma(sp, y_s[:, :], X[0:NS, xi, :], is_out=True)
                if not last:
                    while nloaded < len(ntiles) and ntiles[nloaded][0] in [t[0] for t in tiles[:ti + 1]] + [x for x in range(5) if x not in cur_slots]:
                        a_load(g + 1, ntiles[nloaded])
                        nloaded += 1
                    while (not astep.done()) and min(astep.k, astep.n - 1) < nloaded:
                        astep.step()
                        if astep.k <= astep.n:
                            break
            if not last:
                assert nloaded == len(ntiles)
                while not astep.done():
                    astep.step()

        S.mark('groups_done')
        for c in range(2):
            bk = nbank()
            S.tr(bk[0:79, 0:128], PSt[:, c, 0:79], identf)
            S.copy("dve", PStT[0:79, c * 128:(c + 1) * 128], bk[0:79, 0:128])
        for t in range(4):
            S.dma(sp, o_pool_s[:, 11 + t, :], PStT[t * 16:(t + 1) * 16, :], is_out=True)
        S.dma(sp, o_pool_p[:, :], PStT[64:79, :], is_out=True)
        for q in range(6):
            bk = nbank()
            nfc = 4 if q < 5 else 2
            for j in range(nfc):
                fc = q * 4 + j
                S.tr(bk[0:34, j * 128:(j + 1) * 128], CS[:, fc, :], identf)
            S.copy("act" if q % 2 else "dve", CST[0:34, q * 512: q * 512 + nfc * 128], bk[0:34, 0:nfc * 128])
        for r in range(2):
            S.dma(sp, o_conv_s[:, r, :], CST[r * 16:(r + 1) * 16, :], is_out=True)
        S.dma(sp, o_conv_p[:, :], CST[32:34, :], is_out=True)

        import os
        lim = int(os.environ.get("KSTOP", "0")) or None
        if os.environ.get("KMARKS"):
            print("MARKS", S.marks, "total", len(S.ops))
        S.emit(limit=lim)
    return nc


_CACHE = {}


def _consts():
    identb = np.eye(128, dtype=np.float32).astype(ml_dtypes.bfloat16)
    identf = np.eye(128, dtype=np.float32)
    s = np.arange(128)
    maskT = (s[:, None] <= s[None, :]).astype(np.float32)
    rc = np.zeros((128, 2, 16), np.float32)
    wins = {(0, 0): 2, (1, 0): 4, (0, 1): 8, (1, 1): 16}
    for p in range(128):
        for c in range(2):
            w = wins[(p // 64, c)]
            for t in range(16):
                rc[p, c, t] = 1.0 / min(w, t + 1)
    return identb, identf, maskT, rc.reshape(128, 32)


def make_in_maps(inputs):
    f = lambda k: np.ascontiguousarray(np.asarray(inputs[k], dtype=np.float32))
    x_prompt = f("x_prompt")
    x_sample = f("x_sample")
    mem_prompt = f("mem_prompt")
    cache_k = f("cache_mem_k")[0].reshape(128, MEM, 256)
    cache_v = f("cache_mem_v")[0].reshape(128, MEM, 256)
    state_pool = f("state_pool")[0]
    state_conv = f("state_conv")[0]
    identb, identf, maskT, rcnt = _consts()
    shared = {
        "norm_mix_pre": f("norm_mix_pre"), "norm_mix_post": f("norm_mix_post"),
        "norm_ffn_pre": f("norm_ffn_pre"), "norm_ffn_post": f("norm_ffn_post"),
        "w_in": f("w_in")[0], "b_gate": f("b_gate"),
        "gmlp_ln_g": f("gmlp_ln_g"), "gmlp_ln_b": f("gmlp_ln_b"),
        "w_spatial": f("w_spatial")[0], "b_spatial": f("b_spatial")[0],
        "w_pool": f("w_pool")[0], "pool_scale": f("pool_scale"),
        "mem_norm": f("mem_norm"), "w_mem_kv": f("w_mem_kv")[0],
        "w_branch_a": f("w_branch_a")[0], "w_branch_b": f("w_branch_b")[0], "w_branch_c": f("w_branch_c")[0],
        "w_out": f("w_out")[0], "w_up": f("w_up")[0], "conv_w": f("conv_w")[0], "conv_b": f("conv_b"),
        "w_down": f("w_down")[0],
        "c_identb": identb, "c_identf": identf, "c_maskT": maskT, "c_rcnt": rcnt,
    }
    in_maps = []
    for c in range(NCORES):
        m = dict(shared)
        m["x_p"] = x_prompt[c]
        m["x_s"] = np.ascontiguousarray(x_sample[c * NB:(c + 1) * NB].reshape(NS, D))
        m["mem"] = mem_prompt[c]
        m["ck"] = np.ascontiguousarray(cache_k[c * NB:(c + 1) * NB])
        m["cv"] = np.ascontiguousarray(cache_v[c * NB:(c + 1) * NB])
        m["st_pool"] = np.ascontiguousarray(state_pool[c * NB:(c + 1) * NB])
        m["st_conv"] = np.ascontiguousarray(state_conv[c * NB:(c + 1) * NB])
        in_maps.append(m)
    return in_maps


def kernel(**inputs):
    in_maps = make_in_maps(inputs)
    if "nc" not in _CACHE:
        _CACHE["nc"] = build_program()
    nc = _CACHE["nc"]
    res = run_bass_kernel_spmd(nc, in_maps, core_ids=list(range(NCORES)))
    R = res.results
    y_prompt = np.stack([R[c]["y_p"] for c in range(NCORES)]).astype(np.float32)
    y_sample = np.concatenate([R[c]["y_s"].reshape(NB, 4, D) for c in range(NCORES)]).astype(np.float32)
    mk = np.stack([R[c]["o_mk"].reshape(MEM, 4, 64) for c in range(NCORES)])[None].astype(np.float32)
    mv = np.stack([R[c]["o_mv"].reshape(MEM, 4, 64) for c in range(NCORES)])[None].astype(np.float32)
    pool_p = np.stack([R[c]["o_pool_p"] for c in range(NCORES)])[None].astype(np.float32)
    conv_p = np.stack([R[c]["o_conv_p"] for c in range(NCORES)])[None].astype(np.float32)
    v_s = np.concatenate([R[c]["o_v_s"].reshape(NB, 4, AW) for c in range(NCORES)])[None].astype(np.float32)
    pool_s = np.concatenate([R[c]["o_pool_s"] for c in range(NCORES)])[None].astype(np.float32)
    conv_s = np.concatenate([R[c]["o_conv_s"] for c in range(NCORES)])[None].astype(np.float32)
    return (y_prompt, y_sample, mk, mv, pool_p, conv_p, v_s, pool_s, conv_s)
```

```python
import contextlib
from math import prod

import numpy as np
import ml_dtypes

import concourse.bass as bass
import concourse.mybir as mybir
from concourse.bass_utils import run_bass_kernel_spmd

F32 = mybir.dt.float32
BF16 = mybir.dt.bfloat16
AF = mybir.ActivationFunctionType
ALU = mybir.AluOpType
DTSZ = {F32: 4, BF16: 2}

import os
EVAC_ENGS = os.environ.get('KEVAC', 'dve,dve').split(',')
STRICT = os.environ.get('KSTRICT', '1') == '1'
SAME_ENG_GAP = 10 ** 9 if STRICT else int(os.environ.get('KGAP', '2'))
NCORES = 8
D = 1024
KC = 8
SEQ = 2048
NG = 512
NGRP = SEQ // NG
NS = 64
NB = 16
NCOL = NG + NS
AW = 512
BW = 256
DFF = 2816
FC = 22
MEM = 256
EPS = 1e-6
D_IN = 4608


class Op:
    __slots__ = ("eng", "fn", "deps", "signal", "sigval", "dma", "sem", "semval", "prevval", "is_out", "idx", "lidx")

    def __init__(self, eng, fn, dma=False):
        self.eng = eng
        self.fn = fn
        self.deps = set()
        self.signal = False
        self.sigval = 0
        self.dma = dma
        self.sem = None
        self.semval = 0
        self.prevval = 0
        self.is_out = False
        self.idx = 0
        self.lidx = 0


class Sched:
    def __init__(self, nc, es):
        self.nc = nc
        self.ops = []
        self.rowb = {}
        self.wrec = {}
        self.rrec = {}
        self.engs = {"pe": nc.tensor, "act": nc.scalar, "dve": nc.vector, "pool": nc.gpsimd, "sp": nc.sync}
        self.esem = {k: es.enter_context(nc.semaphore("s_" + k)) for k in ("pe", "act", "dve", "pool")}
        self.ndsem = 12
        self.dsem = {q: [es.enter_context(nc.semaphore(f"d_{q}{i}")) for i in range(self.ndsem)] for q in ("sp", "pool")}
        self.dcount = {"sp": 0, "pool": 0}
        self.lcount = {}
        self.marks = []

    def mark(self, name):
        self.marks.append((name, len(self.ops)))

    def region(self, ap):
        name = ap.name
        rb = self.rowb.get(name)
        if rb is None:
            return None
        if name.startswith("ps"):
            return name, 0, 128, [(0, 2048)]
        esz = DTSZ[ap.dtype]
        offb = ap.offset * esz
        pat = list(ap.ap)
        p0 = offb // rb
        f0 = offb % rb
        if pat[0][0] * esz == rb:
            pc = pat[0][1]
            dims = pat[1:]
        else:
            pc = 1
            dims = pat
        dims = [(s * esz, c) for s, c in dims if c > 1 and s != 0]
        if not dims:
            ivs = [(f0, f0 + esz)]
        else:
            inner = dims[-1]
            outer = dims[:-1]
            run = (inner[1] - 1) * inner[0] + esz
            nouter = prod(c for _, c in outer) if outer else 1
            if nouter <= 64:
                offs = [0]
                for s, c in outer:
                    offs = [o + i * s for o in offs for i in range(c)]
                ivs = sorted((f0 + o, f0 + o + run) for o in offs)
                merged = [ivs[0]]
                for lo, hi in ivs[1:]:
                    if lo <= merged[-1][1]:
                        merged[-1] = (merged[-1][0], max(hi, merged[-1][1]))
                    else:
                        merged.append((lo, hi))
                ivs = merged
            else:
                ivs = [(f0, f0 + sum((c - 1) * s for s, c in dims) + esz)]
        return name, p0, p0 + pc, ivs

    def _add_dep(self, op, prod_op, kind):
        if prod_op is op:
            return
        if not op.dma and not prod_op.dma and op.eng == prod_op.eng:
            if op.eng == "pe":
                return
            if kind == "war" or (kind != "raw" and not STRICT):
                return
        op.deps.add(prod_op)

    def _read(self, op, ap):
        r = self.region(ap)
        if r is None:
            return
        name, p0, p1, ivs = r
        wl = self.wrec.setdefault(name, [])
        for rec in wl:
            if rec[0] < p1 and p0 < rec[1]:
                for lo, hi in ivs:
                    if rec[2] < hi and lo < rec[3]:
                        self._add_dep(op, rec[4], "raw")
                        break
        rl = self.rrec.setdefault(name, [])
        if name.startswith("ps"):
            for rec in rl:
                if rec[4].eng != op.eng:
                    self._add_dep(op, rec[4], "rar")
        for lo, hi in ivs:
            if not op.dma:
                for i, rec in enumerate(rl):
                    if rec[0] == p0 and rec[1] == p1 and rec[2] == lo and rec[3] == hi and (not rec[4].dma) and rec[4].eng == op.eng:
                        rl[i] = (p0, p1, lo, hi, op)
                        break
                else:
                    rl.append((p0, p1, lo, hi, op))
            else:
                rl.append((p0, p1, lo, hi, op))

    def _write(self, op, ap):
        r = self.region(ap)
        if r is None:
            return
        name, p0, p1, ivs = r
        wl = self.wrec.setdefault(name, [])
        rl = self.rrec.setdefault(name, [])
        for lst, kind in ((wl, "waw"), (rl, "war")):
            keep = []
            for rec in lst:
                hit = False
                contained = False
                if rec[0] < p1 and p0 < rec[1]:
                    for lo, hi in ivs:
                        if rec[2] < hi and lo < rec[3]:
                            hit = True
                            if lo <= rec[2] and rec[3] <= hi and p0 <= rec[0] and rec[1] <= p1:
                                contained = True
                            break
                if hit:
                    self._add_dep(op, rec[4], kind)
                if not contained:
                    keep.append(rec)
            lst[:] = keep
        for lo, hi in ivs:
            wl.append((p0, p1, lo, hi, op))

    def add(self, eng, fn, reads, writes, dma=False):
        op = Op(eng, fn, dma)
        op.idx = len(self.ops)
        if not dma:
            op.lidx = self.lcount.get(eng, 0)
            self.lcount[eng] = op.lidx + 1
        for ap in reads:
            if ap is not None and not isinstance(ap, (int, float)):
                self._read(op, ap)
        for ap in writes:
            if ap is not None:
                self._write(op, ap)
        latest = {}
        keep = set()
        for p in op.deps:
            if p.dma:
                keep.add(p)
            elif p.eng not in latest or latest[p.eng].idx < p.idx:
                latest[p.eng] = p
        for e, p in latest.items():
            if (not dma) and e == eng and op.lidx - p.lidx > SAME_ENG_GAP:
                continue
            keep.add(p)
            p.signal = True
        op.deps = keep
        if dma:
            k = self.dcount[eng]
            self.dcount[eng] = k + 1
            op.sem = self.dsem[eng][k % self.ndsem]
            op.semval = 16 * (k // self.ndsem + 1)
            op.prevval = 16 * (k // self.ndsem)
        self.ops.append(op)
        return op

    def mm(self, out, lhsT, rhs, start=True, stop=True):
        return self.add("pe", lambda e: e.matmul(out, lhsT, rhs, start=start, stop=stop), [lhsT, rhs], [out])

    def tr(self, out, in_, ident):
        return self.add("pe", lambda e: e.transpose(out, in_, ident), [in_, ident], [out])

    def act(self, out, in_, func, bias=None, scale=None, accum_out=None):
        kw = {}
        if bias is not None:
            kw["bias"] = bias
        if scale is not None:
            kw["scale"] = scale
        if accum_out is not None:
            kw["accum_out"] = accum_out
        if func == AF.Copy and any(v is not None and not isinstance(v, (int, float)) for v in (bias, scale)):
            func = AF.Identity
        return self.add("act", lambda e: e.activation(out=out, in_=in_, func=func, **kw), [in_, bias, scale], [out, accum_out])

    def tt(self, eng, out, in0, in1, op):
        return self.add(eng, lambda e: e.tensor_tensor(out=out, in0=in0, in1=in1, op=op), [in0, in1], [out])

    def ts(self, eng, out, in0, s1, s2, op0, op1=None):
        if op1 is None:
            return self.add(eng, lambda e: e.tensor_scalar(out=out, in0=in0, scalar1=s1, scalar2=None, op0=op0), [in0, s1], [out])
        return self.add(eng, lambda e: e.tensor_scalar(out=out, in0=in0, scalar1=s1, scalar2=s2, op0=op0, op1=op1), [in0, s1, s2], [out])

    def stt(self, eng, out, in0, scalar, in1, op0, op1):
        return self.add(eng, lambda e: e.scalar_tensor_tensor(out=out, in0=in0, scalar=scalar, in1=in1, op0=op0, op1=op1), [in0, scalar, in1], [out])

    def copy(self, eng, out, in_):
        if eng == "act":
            return self.act(out, in_, AF.Copy)
        return self.add(eng, lambda e: e.tensor_copy(out=out, in_=in_), [in_], [out])

    def memset(self, eng, out, val):
        return self.add(eng, lambda e: e.memset(out, val), [], [out])

    def recip(self, out, in_):
        return self.add("dve", lambda e: e.reciprocal(out=out, in_=in_), [in_], [out])

    def bn_stats(self, out, in_):
        return self.add("dve", lambda e: e.bn_stats(out=out, in_=in_), [in_], [out])

    def bn_aggr(self, out, in_):
        return self.add("dve", lambda e: e.bn_aggr(out=out, in_=in_), [in_], [out])

    def dma(self, q, out, in_, is_out=False):
        op = self.add(q, lambda e: e.dma_start(out=out, in_=in_), [in_], [out], dma=True)
        op.is_out = is_out
        return op

    def emit(self, limit=None):
        if limit:
            self.ops = self.ops[:limit]
            for op in self.ops:
                if not op.dma:
                    op.signal = False
            live = set(map(id, self.ops))
            for op in self.ops:
                for p in op.deps:
                    if not p.dma:
                        p.signal = True
        cnt = {k: 0 for k in self.esem}
        for op in self.ops:
            if not op.dma and op.signal:
                cnt[op.eng] += 1
                op.sigval = cnt[op.eng]
        waited = {k: {} for k in self.engs}
        outs = []

        def wait(engname, sem, val):
            w = waited[engname]
            key = id(sem)
            if w.get(key, 0) >= val:
                return
            w[key] = val
            self.engs[engname].wait_ge(sem, val)

        for op in self.ops:
            need = {}
            for p in op.deps:
                if p.dma:
                    sem, val = p.sem, p.semval
                else:
                    sem, val = self.esem[p.eng], p.sigval
                k = id(sem)
                if k not in need or need[k][1] < val:
                    need[k] = (sem, val)
            if op.dma and op.prevval > 0:
                k = id(op.sem)
                if k not in need or need[k][1] < op.prevval:
                    need[k] = (op.sem, op.prevval)
            for sem, val in need.values():
                wait(op.eng, sem, val)
            inst = op.fn(self.engs[op.eng])
            if op.dma:
                inst.then_inc(op.sem, 16)
                if op.is_out:
                    outs.append(op)
            elif op.signal:
                inst.then_inc(self.esem[op.eng], 1)
        for engname in ("sp", "act", "pool"):
            for op in outs:
                wait(engname, op.sem, op.semval)


def rs(ap, *dims):
    names = "abcdefgh"[: len(dims)]
    pat = "p (" + " ".join(names) + ") -> p " + " ".join(names)
    return ap.rearrange(pat, **{n: d for n, d in zip(names, dims)})


class Arena:
    def __init__(self, nc, es, S, name, nbytes):
        self.nbytes = nbytes
        self.t = es.enter_context(nc.sbuf_tensor(name, [128, nbytes // 2], BF16))
        S.rowb[name] = nbytes
        self.off = 0
        self.hi = 0

    def at(self, off, dtype, *shape):
        n = prod(shape)
        nb = n * DTSZ[dtype]
        assert off % 4 == 0 and off + nb <= self.nbytes, (off, nb, self.nbytes)
        a = self.t[:, off // 2: (off + nb) // 2]
        if dtype != BF16:
            a = a.bitcast(dtype)
        if len(shape) > 1:
            a = rs(a, *shape)
        return a

    def alloc(self, dtype, *shape):
        nb = prod(shape) * DTSZ[dtype]
        off = self.off
        self.off = (off + nb + 63) // 64 * 64
        self.hi = max(self.hi, self.off)
        return self.at(off, dtype, *shape)


def build_program():
    nc = bass.Bass("TRN2", target_bir_lowering=False)
    es = contextlib.ExitStack()

    def din(name, shape, dt=F32):
        return nc.dram_tensor(name, list(shape), dt, kind="ExternalInput").ap()

    def dout(name, shape):
        return nc.dram_tensor(name, list(shape), F32, kind="ExternalOutput").ap()

    x_p = din("x_p", [SEQ, D])
    x_s = din("x_s", [NS, D])
    mem = din("mem", [MEM, D])
    ck = din("ck", [NB, MEM, 256])
    cv = din("cv", [NB, MEM, 256])
    st_pool = din("st_pool", [NB, 15, BW])
    st_conv = din("st_conv", [NB, 2, DFF])
    norm_mix_pre = din("norm_mix_pre", [1, D])
    norm_mix_post = din("norm_mix_post", [1, D])
    norm_ffn_pre = din("norm_ffn_pre", [1, D])
    norm_ffn_post = din("norm_ffn_post", [1, D])
    w_in = din("w_in", [D, D_IN])
    b_gate = din("b_gate", [1, 3 * D])
    gmlp_ln_g = din("gmlp_ln_g", [1, AW])
    gmlp_ln_b = din("gmlp_ln_b", [1, AW])
    w_spatial = din("w_spatial", [4, 128, 128])
    b_spatial = din("b_spatial", [4, 128])
    w_pool = din("w_pool", [4, 64, 64])
    pool_scale = din("pool_scale", [1, BW])
    mem_norm = din("mem_norm", [1, D])
    w_mem_kv = din("w_mem_kv", [D, 512])
    w_branch_a = din("w_branch_a", [AW, D])
    w_branch_b = din("w_branch_b", [BW, D])
    w_branch_c = din("w_branch_c", [BW, D])
    w_out = din("w_out", [D, D])
    w_up = din("w_up", [D, 2 * DFF])
    conv_w = din("conv_w", [3, DFF])
    conv_b = din("conv_b", [1, DFF])
    w_down = din("w_down", [DFF, D])
    c_identb = din("c_identb", [128, 128], BF16)
    c_identf = din("c_identf", [128, 128])
    c_maskT = din("c_maskT", [128, 128])
    c_rcnt = din("c_rcnt", [128, 32])

    y_p = dout("y_p", [SEQ, D])
    y_s = dout("y_s", [NS, D])
    o_mk = dout("o_mk", [MEM, 256])
    o_mv = dout("o_mv", [MEM, 256])
    o_pool_p = dout("o_pool_p", [15, BW])
    o_conv_p = dout("o_conv_p", [2, DFF])
    o_v_s = dout("o_v_s", [NS, AW])
    o_pool_s = dout("o_pool_s", [NB, 15, BW])
    o_conv_s = dout("o_conv_s", [NB, 2, DFF])

    with es:
        S = Sched(nc, es)
        banks = []
        for i in range(8):
            t = es.enter_context(nc.psum_tensor(f"ps{i}", [128, 512], F32))
            S.rowb[f"ps{i}"] = 2048
            banks.append(t)
        bank_i = [0]
        bank_n = [8]

        def nbank():
            b = banks[bank_i[0] % bank_n[0]]
            bank_i[0] += 1
            assert not (S.wrec.get(b.name) and not S.rrec.get(b.name)), ("PSUM bank reused before consumption", b.name)
            return b

        PB = 90624
        UB = 121856
        P = Arena(nc, es, S, "arenaP", PB)
        U = Arena(nc, es, S, "arenaU", UB)

        identb = P.alloc(BF16, 128)
        identf = P.alloc(F32, 128)
        onesb = P.alloc(BF16, 128)
        maskT = P.alloc(F32, 128)
        rcnt = P.alloc(F32, 2, 16)
        neghalf = P.alloc(F32, 8)
        V1T = P.alloc(F32, 72)
        V2T = P.alloc(F32, 66)
        hb = P.alloc(F32, 24)
        gpost1h = P.alloc(F32, D)
        gpost2 = P.alloc(F32, D)
        lng = P.alloc(F32, AW)
        lnb = P.alloc(F32, AW)
        bsb = P.alloc(F32, 4, 128)
        ws4 = P.alloc(F32, 4, 4, 4)
        bs4 = P.alloc(F32, 4, 4)
        wsT = P.alloc(BF16, 4, 128)
        Wbd = P.alloc(BF16, 2, 128)
        kT = P.alloc(BF16, 2, MEM)
        Vb = P.alloc(BF16, 2, 256)
        X = P.alloc(F32, 5, D)
        xnT = P.alloc(BF16, KC, NCOL)
        RING_N = 2
        ring = [P.alloc(BF16, 6144) for _ in range(RING_N)]
        bTs = P.alloc(F32, 2, NB, 19)
        gTs = P.alloc(F32, FC, NB, 6)
        halo_g = P.alloc(F32, FC, 2)
        halo_b = P.alloc(F32, 2, 15)
        PSt = P.alloc(F32, 2, 80)
        CS = P.alloc(F32, FC, 34)
        stats = P.alloc(F32, 128)
        assert P.hi <= PB, P.hi
        print('P arena used', P.hi, 'of', PB)

        U.off = 0
        uT = U.alloc(BF16, 4, NCOL)
        qT = U.alloc(BF16, 2, NCOL)
        attnT = U.alloc(BF16, 2, NCOL)
        mixedT = U.alloc(BF16, 2, NCOL)
        pooledT = U.alloc(BF16, 2, NCOL)
        bT = U.alloc(F32, 2, 15 + NG)
        TMP0 = U.off
        junk = U.alloc(BF16, D)
        xnb = [U.alloc(BF16, D) for _ in range(3)]
        vg = [U.alloc(F32, AW) for _ in range(4)]
        vn1 = U.alloc(F32, AW)
        vn2 = U.alloc(F32, AW)
        vnb = [U.alloc(BF16, AW) for _ in range(2)]
        vnS = U.alloc(F32, AW)
        spt = U.alloc(F32, 4, 128)
        vnTs = U.alloc(F32, 4, NB, 4)
        sgs = U.alloc(F32, 4, NB, 4)
        s2 = U.alloc(F32, 2, 15 + NG)
        s4 = U.alloc(F32, 2, 15 + NG)
        s8 = U.alloc(F32, 15 + NG)
        s16 = U.alloc(F32, 15 + NG)
        ptmp = U.alloc(F32, 2, 16)
        expT = [U.alloc(BF16, 2, 512) for _ in range(2)]
        rd = [U.alloc(F32, 512) for _ in range(2)]
        Ksb = [U.alloc(BF16, 2, 256) for _ in range(2)]
        Vsb = [U.alloc(BF16, 2, 256) for _ in range(2)]
        kTb = [U.alloc(BF16, 2, 256) for _ in range(2)]
        expS = [U.alloc(BF16, 2, 16) for _ in range(2)]
        rdS = U.alloc(F32, NB, 16)
        TMP1 = U.off
        stage = U.alloc(F32, 2816)
        U.off = TMP0
        tg = [U.alloc(F32, 512) for _ in range(3)]
        pj = [U.alloc(F32, 512) for _ in range(3)]
        dtmp = [U.alloc(F32, 512) for _ in range(2)]
        junk2 = U.alloc(BF16, D)
        hnb = [U.alloc(BF16, D) for _ in range(3)]
        assert U.off <= TMP1
        U.off = TMP1
        mergedT = U.alloc(BF16, KC, NCOL)
        wba = U.alloc(BF16, 4, D)
        WBB_OFF = U.off
        wbb = U.alloc(BF16, 2, D)
        wbc = U.alloc(BF16, 2, D)
        wout = U.alloc(BF16, KC, D)
        MIX_END = U.off
        assert MIX_END <= UB, MIX_END
        print('U mixer layout end', MIX_END, 'of', UB)
        U.off = 0
        wdn = U.alloc(BF16, FC, D)
        actT = U.alloc(BF16, FC, NCOL)
        gTp = [U.alloc(F32, 2 + NG) for _ in range(2)]
        cT = [U.alloc(F32, NG) for _ in range(2)]
        ge = [U.alloc(F32, NG) for _ in range(2)]
        cTs = U.alloc(F32, NB, 4)
        geS = U.alloc(F32, NB, 4)
        junk3 = U.alloc(BF16, 512)
        ftmp = [U.alloc(F32, 512) for _ in range(2)]
        PStT = U.alloc(F32, 256)
        CST = U.alloc(F32, DFF)
        junkA = U.alloc(BF16, D)
        xnbA = [U.alloc(BF16, D) for _ in range(3)]
        assert U.off <= UB, U.off
        print('U ffn layout end', U.off, 'of', UB)
        U.off = max(U.off, MIX_END)
        gffnb = U.alloc(F32, D)
        assert U.off <= UB, U.off

        sp = "sp"
        S.dma(sp, identb, c_identb[:, :])
        S.dma(sp, identf, c_identf[:, :])
        S.dma(sp, maskT, c_maskT[:, :])
        S.dma(sp, rcnt, c_rcnt[:, :].rearrange("p (c t) -> p c t", c=2))
        S.dma(sp, X[0:NS, 4, :], x_s[:, :])
        for i in range(NG // 128):
            S.dma(sp, X[:, i, :], x_p[i * 128:(i + 1) * 128, :])
        for mt in range(2):
            S.dma(sp, [s2.rearrange("p c l -> p (c l)")[:, 0:D], s4.rearrange("p c l -> p (c l)")[:, 0:D]][mt], mem[mt * 128:(mt + 1) * 128, :])
        S.memset("dve", onesb, 1.0)
        S.memset("dve", neghalf, -0.5)
        S.memset("dve", halo_g, 0.0)
        S.memset("dve", halo_b, 0.0)

        st1 = stage[:, 0:128]
        S.dma(sp, st1[0:8, :], norm_mix_pre.rearrange("o (r p) -> (o r) p", p=128))
        S.dma(sp, st1[8:16, :], norm_ffn_pre.rearrange("o (r p) -> (o r) p", p=128))
        S.dma(sp, st1[16:40, :], b_gate.rearrange("o (r p) -> (o r) p", p=128))
        S.dma(sp, st1[40:42, :], pool_scale.rearrange("o (r p) -> (o r) p", p=128))
        S.dma(sp, st1[42:64, :], conv_b.rearrange("o (r p) -> (o r) p", p=128))
        S.dma(sp, st1[64:72, :], mem_norm.rearrange("o (r p) -> (o r) p", p=128))
        bk = nbank()
        S.tr(bk[:, 0:72], st1[0:72, :], identf[0:72, 0:72])
        S.copy("dve", V1T, bk[:, 0:72])
        st2 = stage[:, 128:256]
        S.dma(sp, st2[0:66, :], conv_w.rearrange("k (r p) -> (k r) p", p=128))
        bk = nbank()
        S.tr(bk[:, 0:66], st2[0:66, :], identf[0:66, 0:66])
        S.copy("dve", V2T, bk[:, 0:66])
        S.ts("dve", hb, V1T[:, 16:40], 0.5, None, ALU.mult)

        S.dma(sp, gpost1h, norm_mix_post[0, :].partition_broadcast(128))
        S.ts("dve", gpost1h, gpost1h, 0.5, None, ALU.mult)
        S.dma(sp, gpost2, norm_ffn_post[0, :].partition_broadcast(128))
        S.dma(sp, gffnb, norm_ffn_pre[0, :].partition_broadcast(128))
        S.dma(sp, lng, gmlp_ln_g[0, :].partition_broadcast(128))
        S.dma(sp, lnb, gmlp_ln_b[0, :].partition_broadcast(128))
        S.dma(sp, bsb, b_spatial.partition_broadcast(128))
        for g in range(4):
            S.dma(sp, ws4[:, g, :, :], w_spatial[g, 0:4, 0:4].partition_broadcast(128))
        S.dma(sp, bs4, b_spatial[:, 0:4].partition_broadcast(128))

        wst = rs(stage[:, 256:768], 4, 128)
        for g in range(4):
            S.dma(sp, wst[:, g, :], w_spatial[g, :, :])
            bk = nbank()
            S.tr(bk[:, 0:128], wst[:, g, :], identf)
            S.tt("dve", wsT[:, g, :], bk[:, 0:128], maskT, ALU.mult)

        wbs = rs(stage[:, 768:1024], 2, 128)
        S.memset("dve", wbs, 0.0)
        for c in range(2):
            S.dma(sp, wbs[0:64, c, 0:64], w_pool[2 * c, :, :])
            S.dma(sp, wbs[64:128, c, 64:128], w_pool[2 * c + 1, :, :])
        S.copy("dve", Wbd, wbs)

        sps = rs(stage[:, 1024:1536], 2, 256)
        stp = st_pool.rearrange("b r f -> (b r) f")
        for hh in range(2):
            S.dma(sp, sps[0:120, hh, :], stp[hh * 120:(hh + 1) * 120, :])
        for hh in range(2):
            for c in range(2):
                bk = nbank()
                S.tr(bk[:, 0:120], sps[0:120, hh, c * 128:(c + 1) * 128], identf[0:120, 0:120])
                S.copy("dve", bTs[:, c, hh * 8:(hh + 1) * 8, 0:15], rs(bk[:, 0:120], 8, 15))
        S.dma(sp, o_pool_s[:, 0:11, :], st_pool[:, 4:15, :], is_out=True)

        scs = stage[0:32, 0:DFF]
        S.dma(sp, scs, st_conv.rearrange("b r f -> (b r) f"))
        for half in range(2):
            bk = nbank()
            for j in range(11):
                fc = half * 11 + j
                S.tr(bk[:, j * 32:(j + 1) * 32], scs[:, fc * 128:(fc + 1) * 128], identf[0:32, 0:32])
            S.copy("dve", gTs[:, half * 11:(half + 1) * 11, :, 0:2], rs(bk[:, 0:352], 11, NB, 2))

        S.mark('setup_done')
        def rstd_pool(out, acc, rows):
            S.ts("pool", out[0:rows], acc[0:rows], EPS, None, ALU.add)
            S.tt("pool", out[0:rows], out[0:rows], neghalf[0:rows, 0:1], ALU.pow)

        def pipeline(stages, n):
            ns = len(stages)
            for k in range(n + ns - 1):
                for st in range(ns - 1, -1, -1):
                    i = k - st
                    if 0 <= i < n:
                        stages[st](i)

        def norm_transpose_stages(items, gcol0, dstT, jk, xbs, col0, gb=None):
            st = {}

            def s_sq(i):
                src, rows, c0 = items[i]
                acc = stats[:, col0 + 2 * i: col0 + 2 * i + 1]
                S.act(jk[0:rows], src, AF.Square, scale=1.0 / 32.0, accum_out=acc[0:rows])

            def s_rstd(i):
                src, rows, c0 = items[i]
                rstd_pool(stats[:, col0 + 2 * i + 1: col0 + 2 * i + 2], stats[:, col0 + 2 * i: col0 + 2 * i + 1], rows)

            def s_norm(i):
                src, rows, c0 = items[i]
                xb = xbs[i % len(xbs)]
                if gb is not None:
                    S.stt("dve", xb[0:rows], src, stats[0:rows, col0 + 2 * i + 1: col0 + 2 * i + 2], gb[0:rows], ALU.mult, ALU.mult)
                else:
                    S.ts("dve", xb[0:rows], src, stats[0:rows, col0 + 2 * i + 1: col0 + 2 * i + 2], None, ALU.mult)

            def s_tr(i):
                src, rows, c0 = items[i]
                xb = xbs[i % len(xbs)]
                bk = nbank()
                st[i] = bk
                bkb = bk[:].bitcast(BF16)
                for kc in range(KC):
                    S.tr(bkb[:, kc * 128: kc * 128 + rows], xb[0:rows, kc * 128:(kc + 1) * 128], identb[0:rows, 0:rows])

            def s_evac(i):
                src, rows, c0 = items[i]
                bkb = st[i][:].bitcast(BF16)
                if gb is not None:
                    S.act(dstT[:, :, c0:c0 + rows], bkb.rearrange("p (k t) -> p k t", k=KC)[:, :, 0:rows], AF.Copy)
                    return
                for kc in range(KC):
                    S.ts("dve", dstT[:, kc, c0:c0 + rows], bkb[:, kc * 128: kc * 128 + rows], V1T[:, gcol0 + kc:gcol0 + kc + 1], None, ALU.mult)

            return [s_sq, s_rstd, s_norm, s_tr, s_evac]

        ring_i = [0]

        def ring_slot():
            r = ring[ring_i[0] % RING_N]
            ring_i[0] += 1
            return r

        def load_w(dst3, src2d):
            S.dma("pool", dst3, src2d.rearrange("(kc p) f -> p kc f", p=128))

        memX = [s2.rearrange("p c l -> p (c l)")[:, 0:D], s4.rearrange("p c l -> p (c l)")[:, 0:D]]
        wb_first = rs(ring_slot()[:, 0:KC * 512], KC, 512)
        load_w(wb_first, w_in[:, 0:512])
        wkv = U.at(WBB_OFF, BF16, KC, 512)
        load_w(wkv, w_mem_kv)
        wb1_first = rs(ring_slot()[:, 0:KC * 512], KC, 512)
        load_w(wb1_first, w_in[:, 512:1024])
        memnT = mergedT
        pipeline(norm_transpose_stages([(memX[mt], 128, mt * 128) for mt in range(2)], 64, memnT, junk, xnb, 112), 2)

        def kv_part2():
            for hp in range(2):
                bk = nbank()
                for kc in range(KC):
                    S.mm(bk[:, 0:MEM], wkv[:, kc, hp * 128:(hp + 1) * 128], memnT[:, kc, 0:MEM], start=kc == 0, stop=kc == KC - 1)
                S.copy("act", kT[:, hp, :], bk[:, 0:MEM])
            kvt = [pj[0], pj[1]]
            for mc in range(2):
                bk = nbank()
                for kc in range(KC):
                    S.mm(bk[:, :], memnT[:, kc, mc * 128:(mc + 1) * 128], wkv[:, kc, :], start=kc == 0, stop=kc == KC - 1)
                S.copy("act", kvt[mc], bk[:, :])
                S.copy("dve", Vb[:, mc, :], bk[:, 256:512])
                S.dma(sp, o_mk[mc * 128:(mc + 1) * 128, :], kvt[mc][:, 0:256], is_out=True)
                S.dma(sp, o_mv[mc * 128:(mc + 1) * 128, :], kvt[mc][:, 256:512], is_out=True)

        S.mark('kv_done')
        def pooling(xb4, s2v, s4v, s8v, s16v, outv, L, first):
            x = xb4
            S.tt("dve", s2v[:, :, :, 1:L], x[:, :, :, 1:L], x[:, :, :, 0:L - 1], ALU.add)
            S.tt("dve", s4v[64:128, 0, :, 3:L], s2v[64:128, 0, :, 3:L], s2v[64:128, 0, :, 1:L - 2], ALU.add)
            S.tt("dve", s4v[:, 1, :, 3:L], s2v[:, 1, :, 3:L], s2v[:, 1, :, 1:L - 2], ALU.add)
            S.tt("dve", s8v[:, :, 7:L], s4v[:, 1, :, 7:L], s4v[:, 1, :, 3:L - 4], ALU.add)
            S.tt("dve", s16v[64:128, :, 15:L], s8v[64:128, :, 15:L], s8v[64:128, :, 7:L - 8], ALU.add)
            srcs = [(0, 64, 0, s2v[0:64, 0], 0.5), (64, 128, 0, s4v[64:128, 0], 0.25),
                    (0, 64, 1, s8v[0:64], 0.125), (64, 128, 1, s16v[64:128], 1.0 / 16)]
            for p0, p1, c, sv, rw in srcs:
                S.stt("dve", outv[p0:p1, c, :, :], sv[:, :, 15:L], rw, x[p0:p1, c, :, 15:L], ALU.mult, ALU.subtract)
            if first:
                for p0, p1, c, sv, rw in srcs:
                    S.tt("dve", ptmp[p0:p1, c, :], sv[:, 0, 15:31], rcnt[p0:p1, c, :], ALU.mult)
                    S.tt("dve", outv[p0:p1, c, 0, 0:16], ptmp[p0:p1, c, :], x[p0:p1, c, 0, 15:31], ALU.subtract)

        def make_tiles(g):
            t = [((4 * g + i) % 5, 128, i * 128, i) for i in range(NG // 128)]
            if g == 0:
                t = [(4, NS, NG, -1)] + t
            return t

        def a_load(g, t):
            xi, rows, c0, pi = t
            if pi >= 0:
                S.dma(sp, X[:, xi, :], x_p[g * NG + pi * 128: g * NG + (pi + 1) * 128, :])
            else:
                S.dma(sp, X[0:NS, xi, :], x_s[:, :])

        def a_stages(tl):
            return norm_transpose_stages([(X[0:rows, xi, :], rows, c0) for (xi, rows, c0, pi) in tl], 0, xnT, junkA, xnbA, 0)

        def prefetch_win01():
            a = rs(ring_slot()[:, 0:KC * 512], KC, 512)
            load_w(a, w_in[:, 0:512])
            b = rs(ring_slot()[:, 0:KC * 512], KC, 512)
            load_w(b, w_in[:, 512:1024])
            return a, b

        class Stepper:
            def __init__(self, stages, n):
                self.stages, self.n, self.k = stages, n, 0

            def done(self):
                return self.k >= self.n + len(self.stages) - 1

            def step(self):
                ns = len(self.stages)
                for st in range(ns - 1, -1, -1):
                    i = self.k - st
                    if 0 <= i < self.n:
                        self.stages[st](i)
                self.k += 1

        for g in range(NGRP):
            first = g == 0
            last = g == NGRP - 1
            tiles = make_tiles(g)
            blocks = [(0, NG)]
            if first:
                blocks.append((NG, NS))

            if first:
                pipeline(a_stages(tiles), len(tiles))
                kv_part2()


            S.mark(f'g{g}_A_done')
            if first:
                wb = wb_first
                wb1 = wb1_first
            else:
                wb, wb1 = nxt_w01
            for (c0, n) in blocks:
                for fc in range(4):
                    bk = nbank()
                    for kc in range(KC):
                        S.mm(bk[:, 0:n], wb[:, kc, fc * 128:(fc + 1) * 128], xnT[:, kc, c0:c0 + n], start=kc == 0, stop=kc == KC - 1)
                    S.act(uT[:, fc, c0:c0 + n], bk[:, 0:n], AF.Gelu_apprx_tanh)

            S.mark(f'g{g}_B0_done')
            wb2 = rs(ring_slot()[:, 0:KC * 512], KC, 512)
            load_w(wb2, w_in[:, 1024:1536])
            b1 = {}
            SB1 = 48

            def b1_mm(ti):
                xi, rows, c0, pi = tiles[ti]
                bk = nbank()
                b1[ti] = bk
                for kc in range(KC):
                    S.mm(bk[0:rows, :], xnT[:, kc, c0:c0 + rows], wb1[:, kc, :], start=kc == 0, stop=kc == KC - 1)

            def b1_gelu(ti):
                xi, rows, c0, pi = tiles[ti]
                S.act(vg[ti % 4][0:rows], b1[ti][0:rows, :], AF.Gelu_apprx_tanh)

            def b1_stats(ti):
                xi, rows, c0, pi = tiles[ti]
                cb = SB1 + 9 * ti
                S.bn_stats(stats[0:rows, cb:cb + 6], vg[ti % 4][0:rows])
                S.bn_aggr(stats[0:rows, cb + 6:cb + 8], stats[0:rows, cb:cb + 6])

            def b1_rstd(ti):
                xi, rows, c0, pi = tiles[ti]
                cb = SB1 + 9 * ti
                rstd_pool(stats[:, cb + 8:cb + 9], stats[:, cb + 7:cb + 8], rows)

            def b1_norm(ti):
                xi, rows, c0, pi = tiles[ti]
                cb = SB1 + 9 * ti
                S.ts("dve", vn1[0:rows], vg[ti % 4][0:rows], stats[0:rows, cb + 6:cb + 7], stats[0:rows, cb + 8:cb + 9], ALU.subtract, ALU.mult)
                S.tt("dve", vn2[0:rows], vn1[0:rows], lng[0:rows], ALU.mult)
                if pi >= 0:
                    S.tt("dve", vnb[ti % 2], vn2, lnb, ALU.add)
                else:
                    S.tt("dve", vnS[0:NS], vn2[0:NS], lnb[0:NS], ALU.add)
                    S.dma(sp, o_v_s[:, :], vnS[0:NS, :], is_out=True)

            def b1_spatial(ti):
                xi, rows, c0, pi = tiles[ti]
                bk2 = nbank()
                b1[("sp", ti)] = bk2
                if pi >= 0:
                    vb = vnb[ti % 2]
                    for gch in range(4):
                        S.mm(bk2[:, gch * 128:(gch + 1) * 128], vb[:, gch * 128:(gch + 1) * 128], wsT[:, gch, :], start=True, stop=True)
                else:
                    for gch in range(4):
                        S.tr(bk2[:, gch * 64:(gch + 1) * 64], vnS[0:NS, gch * 128:(gch + 1) * 128], identf[0:NS, 0:NS])

            def b1_gate(ti):
                xi, rows, c0, pi = tiles[ti]
                bk2 = b1[("sp", ti)]
                if pi >= 0:
                    S.tt("dve", spt, rs(bk2[:, :], 4, 128), bsb, ALU.add)
                    S.tt("dve", uT[:, :, c0:c0 + 128], spt, uT[:, :, c0:c0 + 128], ALU.mult)
                else:
                    S.copy("dve", vnTs, rs(bk2[:, 0:256], 4, NB, 4))
                    for sidx in range(4):
                        for gch in range(4):
                            for t in range(sidx, 4):
                                acc = sgs[:, gch, :, t]
                                if sidx == 0:
                                    S.ts("dve", acc, vnTs[:, gch, :, 0], ws4[:, gch, t, 0:1], bs4[:, gch, t:t + 1], ALU.mult, ALU.add)
                                else:
                                    S.stt("dve", acc, vnTs[:, gch, :, sidx], ws4[:, gch, t, sidx:sidx + 1], acc, ALU.mult, ALU.add)
                    S.tt("dve", uT[:, :, NG:NG + NS], sgs.rearrange("p g b t -> p g (b t)"), uT[:, :, NG:NG + NS], ALU.mult)

            def b2_unit(c0, n, j):
                def f():
                    bk = nbank()
                    for kc in range(KC):
                        S.mm(bk[:, 0:n], wb2[:, kc, j * 128:(j + 1) * 128], xnT[:, kc, c0:c0 + n], start=kc == 0, stop=kc == KC - 1)
                    if j < 2:
                        if c0 == 0:
                            S.copy("act", bT[:, j, 15:15 + NG], bk[:, 0:NG])
                        else:
                            S.copy("act", bTs[:, j, :, 15:19], rs(bk[:, 0:NS], NB, 4))
                    else:
                        S.copy("act", qT[:, j - 2, c0:c0 + n], bk[:, 0:n])
                return f

            b2_units = [b2_unit(c0, n, j) for (c0, n) in blocks for j in range(4)]
            b1step = Stepper([b1_mm, b1_gelu, b1_stats, b1_rstd, b1_norm, b1_spatial, b1_gate], len(tiles))
            while not b1step.done():
                b1step.step()
                if b2_units and b1step.k >= 5:
                    b2_units.pop(0)()
            aT = uT

            def gate_block(i):
                wg = rs(ring_slot()[:, 0:KC * 768], KC, 3, 256)
                for j in range(3):
                    c = 1536 + j * 1024 + i * 256
                    load_w(wg[:, :, j, :], w_in[:, c:c + 256])
                return wg

            S.mark(f'g{g}_B1_done')
            while b2_units:
                b2_units.pop(0)()
            wg_next = gate_block(0)
            load_w(wba, w_branch_a)
            load_w(wbb, w_branch_b)
            load_w(wbc, w_branch_c)

            S.mark(f'g{g}_B2_done')
            S.mark(f'g{g}_pool_done')
            sb = {}

            def att_scores(h):
                hp, base = h // 2, (h % 2) * 64
                ex = expT[h % 2]
                for mc in range(2):
                    bk = nbank()
                    S.mm(bk[:, 0:NG], kT[base:base + 64, hp, mc * 128:(mc + 1) * 128], qT[base:base + 64, hp, 0:NG], start=True, stop=True)
                    S.act(ex[:, mc, :], bk[:, 0:NG], AF.Exp, scale=0.125)

            def att_pv(h):
                hp, base = h // 2, (h % 2) * 64
                ex = expT[h % 2]
                pv = nbank()
                for mc in range(2):
                    S.mm(pv[:, 0:NG], Vb[:, mc, hp * 128:(hp + 1) * 128], ex[:, mc, :], start=mc == 0, stop=mc == 1)
                dn = nbank()
                for mc in range(2):
                    S.mm(dn[:, 0:NG], onesb, ex[:, mc, :], start=mc == 0, stop=mc == 1)
                r = rd[h % 2]
                S.recip(r[base:base + 64, :], dn[base:base + 64, 0:NG])
                S.tt("dve", attnT[base:base + 64, hp, 0:NG], pv[base:base + 64, 0:NG], r[base:base + 64, :], ALU.mult)

            for i in range(5):
                if i < 4:
                    att_scores(i)
                if i >= 1:
                    att_pv(i - 1)

            L = 15 + NG
            S.copy("dve", bT[:, :, 0:15], halo_b)
            pooling(bT.rearrange("p c (b l) -> p c b l", b=1), s2.rearrange("p c (b l) -> p c b l", b=1),
                    s4.rearrange("p c (b l) -> p c b l", b=1), s8.rearrange("p (b l) -> p b l", b=1),
                    s16.rearrange("p (b l) -> p b l", b=1),
                    pooledT[:, :, 0:NG].rearrange("p c (b l) -> p c b l", b=1), L, first)
            if last:
                S.copy("dve", PSt[:, :, 64:79], bT[:, :, L - 15:L])
            else:
                S.copy("dve", halo_b, bT[:, :, L - 15:L])
            if first:
                s2s = rs(s2.rearrange("p c l -> p (c l)")[:, 0:2 * NB * 19], 2, NB, 19)
                s4s = rs(s4.rearrange("p c l -> p (c l)")[:, 0:2 * NB * 19], 2, NB, 19)
                s8s = rs(s8[:, 0:NB * 19], NB, 19)
                s16s = rs(s16[:, 0:NB * 19], NB, 19)
                pooling(bTs, s2s, s4s, s8s, s16s, pooledT[:, :, NG:NG + NS].rearrange("p c (b t) -> p c b t", b=NB), 19, False)
                S.copy("dve", PSt[:, :, 0:64].rearrange("p c (t b) -> p c b t", t=4), bTs[:, :, :, 15:19])

            def glin():
                for (c0, n) in blocks:
                    for c in range(2):
                        bk = nbank()
                        S.mm(bk[:, 0:n], Wbd[:, c, :], pooledT[:, c, c0:c0 + n], start=True, stop=True)
                        S.act(mixedT[:, c, c0:c0 + n], bk[:, 0:n], AF.Copy, scale=V1T[:, 40 + c:41 + c])

            glin_pending = [glin]

            S.mark(f'g{g}_attp_done')
            if first:
                scol = NG

                def s_load(b):
                    S.dma("pool", Ksb[b % 2], ck[b].rearrange("(mc m) f -> m mc f", m=128))
                    S.dma("pool", Vsb[b % 2], cv[b].rearrange("(mc m) f -> m mc f", m=128))

                def s_tr(b):
                    bk = nbank()
                    bkb = bk[:].bitcast(BF16)
                    for hp in range(2):
                        for mc in range(2):
                            S.tr(bkb[:, hp * 256 + mc * 128: hp * 256 + (mc + 1) * 128], Ksb[b % 2][:, mc, hp * 128:(hp + 1) * 128], identb)
                    S.copy("act", kTb[b % 2], rs(bkb[:, 0:512], 2, 256))

                bank_n[0] = 6
                OS = banks[6]
                DS = banks[7]
                OSv = rs(OS[:, 0:256], NB, 4, 4)
                DSv = rs(DS[:, 0:256], NB, 4, 4)

                def s_scores(b):
                    bkp = [nbank(), nbank()]
                    ex5 = expS[b % 2].rearrange("p mc (hp par t) -> p mc hp par t", hp=2, par=2)
                    for par in range(2):
                        base = par * 64
                        sv = rs(bkp[par][:, 0:16], 2, 2, 4)
                        for hp in range(2):
                            for mc in range(2):
                                S.mm(sv[:, mc, hp, :], kTb[b % 2][base:base + 64, hp, mc * 128:(mc + 1) * 128],
                                     qT[base:base + 64, hp, scol + b * 4: scol + b * 4 + 4], start=True, stop=True)
                    for par in range(2):
                        S.act(ex5[:, :, :, par, :], rs(bkp[par][:, 0:16], 2, 2, 4), AF.Exp, scale=0.125)

                def s_pv(b):
                    ex = rs(expS[b % 2].rearrange("p mc x -> p (mc x)"), 2, 4, 4)
                    for h in range(4):
                        hp = h // 2
                        for mc in range(2):
                            S.mm(OSv[:, b, h, :], Vsb[b % 2][:, mc, hp * 128:(hp + 1) * 128], ex[:, mc, h, :], start=mc == 0, stop=mc == 1)
                    for mc in range(2):
                        S.mm(DSv[:, b, :, :].rearrange("p h t -> p (h t)"), onesb, expS[b % 2][:, mc, :], start=mc == 0, stop=mc == 1)

                def s_loadk(b):
                    S.dma("pool", Ksb[b % 2], ck[b].rearrange("(mc m) f -> m mc f", m=128))

                def s_loadv(b):
                    S.dma("pool", Vsb[b % 2], cv[b].rearrange("(mc m) f -> m mc f", m=128))

                s_loadk(0)
                s_loadk(1)
                s_loadv(0)
                s_tr(0)
                for b in range(NB):
                    if b + 1 < NB:
                        s_tr(b + 1)
                    s_scores(b)
                    if b >= 1:
                        s_pv(b - 1)
                    if b + 2 < NB:
                        s_loadk(b + 2)
                    if b + 1 < NB:
                        s_loadv(b + 1)
                s_pv(NB - 1)
                S.recip(rdS.rearrange("p b x -> p (b x)"), DS[:, 0:256])
                rdv = rs(rdS.rearrange("p b x -> p (b x)"), NB, 4, 4)
                for h in range(4):
                    hp, base = h // 2, (h % 2) * 64
                    S.tt("dve", attnT[base:base + 64, hp, scol:scol + NS].rearrange("p (b t) -> p b t", b=NB),
                         OSv[base:base + 64, :, h, :], rdv[base:base + 64, :, h, :], ALU.mult)
                bank_n[0] = 8

            S.mark(f'g{g}_atts_done')
            load_w(wout, w_out)
            for i in range(4):
                wg = wg_next
                if i + 1 < 4:
                    wg_next = gate_block(i + 1)
                for fl in range(2):
                    fo = 2 * i + fl
                    for (c0, n) in blocks:
                        gb = []
                        for j in range(3):
                            bk = nbank()
                            for kc in range(KC):
                                S.mm(bk[:, 0:n], wg[:, kc, j, fl * 128:(fl + 1) * 128], xnT[:, kc, c0:c0 + n], start=kc == 0, stop=kc == KC - 1)
                            S.act(tg[j][:, 0:n], bk[:, 0:n], AF.Tanh, bias=hb[:, j * 8 + fo: j * 8 + fo + 1], scale=0.5)
                            gb.append(bk)
                        if glin_pending:
                            glin_pending.pop()()
                        srcs = [(wba, aT, 4), (wbb, mixedT, 2), (wbc, attnT, 2)]
                        for j, (wbr, actv, nk) in enumerate(srcs):
                            bk = nbank()
                            for kc in range(nk):
                                S.mm(bk[:, 0:n], wbr[:, kc, fo * 128:(fo + 1) * 128], actv[:, kc, c0:c0 + n], start=kc == 0, stop=kc == nk - 1)
                            S.stt("dve", pj[j][:, 0:n], tg[j][:, 0:n], 1.0, bk[:, 0:n], ALU.add, ALU.mult)
                        S.tt("pool", pj[0][:, 0:n], pj[0][:, 0:n], pj[1][:, 0:n], ALU.add)
                        S.tt("pool", mergedT[:, fo, c0:c0 + n], pj[0][:, 0:n], pj[2][:, 0:n], ALU.add)

            S.mark(f'g{g}_C_done')
            def up_block(jb):
                wu = rs(ring_slot()[:, 0:KC * 512], KC, 2, 256)
                load_w(wu[:, :, 0, :], w_up[:, jb * 256:(jb + 1) * 256])
                load_w(wu[:, :, 1, :], w_up[:, DFF + jb * 256: DFF + (jb + 1) * 256])
                return wu

            wu_first = up_block(0)
            for k0 in (0, 5, 10):
                k1 = min(k0 + 5, 14)
                load_w(wdn[:, k0:k1, :], w_down[k0 * 128:k1 * 128, :])
            dd = {}
            SD = 96

            def d_mm(ti):
                xi, rows, c0, pi = tiles[ti]
                bks = []
                for half in range(2):
                    bk = nbank()
                    for kc in range(KC):
                        S.mm(bk[0:rows, :], mergedT[:, kc, c0:c0 + rows], wout[:, kc, half * 512:(half + 1) * 512], start=kc == 0, stop=kc == KC - 1)
                    bks.append(bk)
                dd[ti] = bks

            def d_sq(ti):
                xi, rows, c0, pi = tiles[ti]
                for half in range(2):
                    S.act(junk2[0:rows, 0:512], dd[ti][half][0:rows, :], AF.Square, scale=1.0 / 64.0,
                          accum_out=stats[0:rows, SD + 3 * ti + half: SD + 3 * ti + half + 1])

            def d_rstd(ti):
                xi, rows, c0, pi = tiles[ti]
                racc = stats[:, SD + 3 * ti + 2: SD + 3 * ti + 3]
                S.tt("pool", racc[0:rows], stats[0:rows, SD + 3 * ti: SD + 3 * ti + 1], stats[0:rows, SD + 3 * ti + 1: SD + 3 * ti + 2], ALU.add)
                rstd_pool(racc, racc, rows)

            def d_res(ti):
                xi, rows, c0, pi = tiles[ti]
                racc = stats[:, SD + 3 * ti + 2: SD + 3 * ti + 3]
                for half in range(2):
                    dt_ = dtmp[half]
                    S.stt("dve", dt_[0:rows], dd[ti][half][0:rows, :], racc[0:rows], gpost1h[0:rows, half * 512:(half + 1) * 512], ALU.mult, ALU.mult)
                for half in range(2):
                    S.tt("dve" if half == 0 else "pool", X[0:rows, xi, half * 512:(half + 1) * 512], X[0:rows, xi, half * 512:(half + 1) * 512], dtmp[half][0:rows], ALU.add)

            hn_stages = norm_transpose_stages([(X[0:rows, xi, :], rows, c0) for (xi, rows, c0, pi) in tiles], 8, xnT, junk2, hnb, 16, gb=gffnb)
            def d_sq_rstd(ti):
                d_sq(ti)
                d_rstd(ti)

            def d_res_sq(ti):
                d_res(ti)
                hn_stages[0](ti)

            pipeline([d_mm, d_sq_rstd, d_res_sq] + hn_stages[1:], len(tiles))
            hnT = xnT

            S.mark(f'g{g}_D_done')
            if not last:
                ntiles = make_tiles(g + 1)
                nloaded = 0
                cur_slots = [t[0] for t in tiles]
                while nloaded < len(ntiles) and ntiles[nloaded][0] not in cur_slots:
                    a_load(g + 1, ntiles[nloaded])
                    nloaded += 1
                astep = Stepper(a_stages(ntiles), len(ntiles))
            wu_next = wu_first
            for jb in range(FC // 2):
                wu = wu_next
                if jb + 1 < FC // 2:
                    wu_next = up_block(jb + 1)
                if jb < 4:
                    k0 = 14 + 2 * jb
                    load_w(wdn[:, k0:k0 + 2, :], w_down[k0 * 128:(k0 + 2) * 128, :])
                for fl in range(2):
                    fc = 2 * jb + fl
                    cw0 = V2T[:, fc:fc + 1]
                    cw1 = V2T[:, 22 + fc:23 + fc]
                    cw2 = V2T[:, 44 + fc:45 + fc]
                    cbv = V1T[:, 42 + fc:43 + fc]
                    for (c0, n) in blocks:
                        gbk = nbank()
                        for kc in range(KC):
                            S.mm(gbk[:, 0:n], wu[:, kc, 0, fl * 128:(fl + 1) * 128], hnT[:, kc, c0:c0 + n], start=kc == 0, stop=kc == KC - 1)
                        ubk = nbank()
                        for kc in range(KC):
                            S.mm(ubk[:, 0:n], wu[:, kc, 1, fl * 128:(fl + 1) * 128], hnT[:, kc, c0:c0 + n], start=kc == 0, stop=kc == KC - 1)
                        if c0 == 0:
                            gt = gTp[fc % 2]
                            ct = cT[fc % 2]
                            gg = ge[fc % 2]
                            S.copy("pool", gt[:, 0:2], halo_g[:, fc, :])
                            S.copy("act", gt[:, 2:2 + NG], gbk[:, 0:NG])
                            if last:
                                S.copy("pool", CS[:, fc, 32:34], gt[:, NG:NG + 2])
                            else:
                                S.copy("pool", halo_g[:, fc, :], gt[:, NG:NG + 2])
                            S.ts("dve", ct, gt[:, 2:2 + NG], cw2, cbv, ALU.mult, ALU.add)
                            S.stt("dve", ct, gt[:, 1:1 + NG], cw1, ct, ALU.mult, ALU.add)
                            S.stt("dve", ct, gt[:, 0:NG], cw0, ct, ALU.mult, ALU.add)
                            S.act(gg, ct, AF.Gelu_apprx_tanh)
                            S.tt("dve", actT[:, fc, 0:NG], ubk[:, 0:NG], gg, ALU.mult)
                        else:
                            S.copy("act", gTs[:, fc, :, 2:6], rs(gbk[:, 0:NS], NB, 4))
                            S.copy("pool", CS[:, fc, 0:32].rearrange("p (r b) -> p b r", r=2), gTs[:, fc, :, 4:6])
                            S.ts("dve", cTs, gTs[:, fc, :, 2:6], cw2, cbv, ALU.mult, ALU.add)
                            S.stt("dve", cTs, gTs[:, fc, :, 1:5], cw1, cTs, ALU.mult, ALU.add)
                            S.stt("dve", cTs, gTs[:, fc, :, 0:4], cw0, cTs, ALU.mult, ALU.add)
                            S.act(geS, cTs, AF.Gelu_apprx_tanh)
                            S.tt("dve", actT[:, fc, NG:NG + NS], ubk[:, 0:NS], geS.rearrange("p b t -> p (b t)"), ALU.mult)

            S.mark(f'g{g}_E_done')
            if not last:
                nxt_w01 = prefetch_win01()
            for ti, (xi, rows, c0, pi) in enumerate(tiles):
                acc2 = stats[:, 40:42]
                racc = stats[:, 42:43]
                bks = []
                for half in range(2):
                    bk = nbank()
                    for kc in range(FC):
                        S.mm(bk[0:rows, :], actT[:, kc, c0:c0 + rows], wdn[:, kc, half * 512:(half + 1) * 512], start=kc == 0, stop=kc == FC - 1)
                    S.act(junk3[0:rows], bk[0:rows, :], AF.Square, scale=1.0 / 32.0, accum_out=acc2[0:rows, half:half + 1])
                    bks.append(bk)
                S.tt("pool", racc[0:rows], acc2[0:rows, 0:1], acc2[0:rows, 1:2], ALU.add)
                rstd_pool(racc, racc, rows)
                for half in range(2):
                    ft = ftmp[half]
                    S.stt("dve", ft[0:rows], bks[half][0:rows, :], racc[0:rows], gpost2[0:rows, half * 512:(half + 1) * 512], ALU.mult, ALU.mult)
                    S.tt("pool", X[0:rows, xi, half * 512:(half + 1) * 512], X[0:rows, xi, half * 512:(half + 1) * 512], ft[0:rows], ALU.add)
                if pi >= 0:
                    S.dma(sp, y_p[g * NG + pi * 128: g * NG + (pi + 1) * 128, :], X[:, xi, :], is_out=True)
                else:
                    S.dma(sp, y_s[:, :], X[0:NS, xi, :], is_out=True)
                if not last:
                    while nloaded < len(ntiles) and ntiles[nloaded][0] in [t[0] for t in tiles[:ti + 1]] + [x for x in range(5) if x not in cur_slots]:
                        a_load(g + 1, ntiles[nloaded])
                        nloaded += 1
                    while (not astep.done()) and min(astep.k, astep.n - 1) < nloaded:
                        astep.step()
                        if astep.k <= astep.n:
                            break
            if not last:
                assert nloaded == len(ntiles)
                while not astep.done():
                    astep.step()

        S.mark('groups_done')
        for c in range(2):
            bk = nbank()
            S.tr(bk[0:79, 0:128], PSt[:, c, 0:79], identf)
            S.copy("dve", PStT[0:79, c * 128:(c + 1) * 128], bk[0:79, 0:128])
        for t in range(4):
            S.dma(sp, o_pool_s[:, 11 + t, :], PStT[t * 16:(t + 1) * 16, :], is_out=True)
        S.dma(sp, o_pool_p[:, :], PStT[64:79, :], is_out=True)
        for q in range(6):
            bk = nbank()
            nfc = 4 if q < 5 else 2
            for j in range(nfc):
                fc = q * 4 + j
                S.tr(bk[0:34, j * 128:(j + 1) * 128], CS[:, fc, :], identf)
            S.copy("act" if q % 2 else "dve", CST[0:34, q * 512: q * 512 + nfc * 128], bk[0:34, 0:nfc * 128])
        for r in range(2):
            S.dma(sp, o_conv_s[:, r, :], CST[r * 16:(r + 1) * 16, :], is_out=True)
        S.dma(sp, o_conv_p[:, :], CST[32:34, :], is_out=True)

        import os
        lim = int(os.environ.get("KSTOP", "0")) or None
        if os.environ.get("KMARKS"):
            print("MARKS", S.marks, "total", len(S.ops))
        S.emit(limit=lim)
    return nc


_CACHE = {}


def _consts():
    identb = np.eye(128, dtype=np.float32).astype(ml_dtypes.bfloat16)
    identf = np.eye(128, dtype=np.float32)
    s = np.arange(128)
    maskT = (s[:, None] <= s[None, :]).astype(np.float32)
    rc = np.zeros((128, 2, 16), np.float32)
    wins = {(0, 0): 2, (1, 0): 4, (0, 1): 8, (1, 1): 16}
    for p in range(128):
        for c in range(2):
            w = wins[(p // 64, c)]
            for t in range(16):
                rc[p, c, t] = 1.0 / min(w, t + 1)
    return identb, identf, maskT, rc.reshape(128, 32)


def make_in_maps(inputs):
    f = lambda k: np.ascontiguousarray(np.asarray(inputs[k], dtype=np.float32))
    x_prompt = f("x_prompt")
    x_sample = f("x_sample")
    mem_prompt = f("mem_prompt")
    cache_k = f("cache_mem_k")[0].reshape(128, MEM, 256)
    cache_v = f("cache_mem_v")[0].reshape(128, MEM, 256)
    state_pool = f("state_pool")[0]
    state_conv = f("state_conv")[0]
    identb, identf, maskT, rcnt = _consts()
    shared = {
        "norm_mix_pre": f("norm_mix_pre"), "norm_mix_post": f("norm_mix_post"),
        "norm_ffn_pre": f("norm_ffn_pre"), "norm_ffn_post": f("norm_ffn_post"),
        "w_in": f("w_in")[0], "b_gate": f("b_gate"),
        "gmlp_ln_g": f("gmlp_ln_g"), "gmlp_ln_b": f("gmlp_ln_b"),
        "w_spatial": f("w_spatial")[0], "b_spatial": f("b_spatial")[0],
        "w_pool": f("w_pool")[0], "pool_scale": f("pool_scale"),
        "mem_norm": f("mem_norm"), "w_mem_kv": f("w_mem_kv")[0],
        "w_branch_a": f("w_branch_a")[0], "w_branch_b": f("w_branch_b")[0], "w_branch_c": f("w_branch_c")[0],
        "w_out": f("w_out")[0], "w_up": f("w_up")[0], "conv_w": f("conv_w")[0], "conv_b": f("conv_b"),
        "w_down": f("w_down")[0],
        "c_identb": identb, "c_identf": identf, "c_maskT": maskT, "c_rcnt": rcnt,
    }
    in_maps = []
    for c in range(NCORES):
        m = dict(shared)
        m["x_p"] = x_prompt[c]
        m["x_s"] = np.ascontiguousarray(x_sample[c * NB:(c + 1) * NB].reshape(NS, D))
        m["mem"] = mem_prompt[c]
        m["ck"] = np.ascontiguousarray(cache_k[c * NB:(c + 1) * NB])
        m["cv"] = np.ascontiguousarray(cache_v[c * NB:(c + 1) * NB])
        m["st_pool"] = np.ascontiguousarray(state_pool[c * NB:(c + 1) * NB])
        m["st_conv"] = np.ascontiguousarray(state_conv[c * NB:(c + 1) * NB])
        in_maps.append(m)
    return in_maps


def kernel(**inputs):
    in_maps = make_in_maps(inputs)
    if "nc" not in _CACHE:
        _CACHE["nc"] = build_program()
    nc = _CACHE["nc"]
    res = run_bass_kernel_spmd(nc, in_maps, core_ids=list(range(NCORES)))
    R = res.results
    y_prompt = np.stack([R[c]["y_p"] for c in range(NCORES)]).astype(np.float32)
    y_sample = np.concatenate([R[c]["y_s"].reshape(NB, 4, D) for c in range(NCORES)]).astype(np.float32)
    mk = np.stack([R[c]["o_mk"].reshape(MEM, 4, 64) for c in range(NCORES)])[None].astype(np.float32)
    mv = np.stack([R[c]["o_mv"].reshape(MEM, 4, 64) for c in range(NCORES)])[None].astype(np.float32)
    pool_p = np.stack([R[c]["o_pool_p"] for c in range(NCORES)])[None].astype(np.float32)
    conv_p = np.stack([R[c]["o_conv_p"] for c in range(NCORES)])[None].astype(np.float32)
    v_s = np.concatenate([R[c]["o_v_s"].reshape(NB, 4, AW) for c in range(NCORES)])[None].astype(np.float32)
    pool_s = np.concatenate([R[c]["o_pool_s"] for c in range(NCORES)])[None].astype(np.float32)
    conv_s = np.concatenate([R[c]["o_conv_s"] for c in range(NCORES)])[None].astype(np.float32)
    return (y_prompt, y_sample, mk, mv, pool_p, conv_p, v_s, pool_s, conv_s)
```

```python
import contextlib
from math import prod

import numpy as np
import ml_dtypes

import concourse.bass as bass
import concourse.mybir as mybir
from concourse.bass_utils import run_bass_kernel_spmd

F32 = mybir.dt.float32
BF16 = mybir.dt.bfloat16
AF = mybir.ActivationFunctionType
ALU = mybir.AluOpType
DTSZ = {F32: 4, BF16: 2}

import os
EVAC_ENGS = os.environ.get('KEVAC', 'dve,dve').split(',')
STRICT = os.environ.get('KSTRICT', '1') == '1'
SAME_ENG_GAP = 10 ** 9 if STRICT else int(os.environ.get('KGAP', '2'))
NCORES = 8
D = 1024
KC = 8
SEQ = 2048
NG = 512
NGRP = SEQ // NG
NS = 64
NB = 16
NCOL = NG + NS
AW = 512
BW = 256
DFF = 2816
FC = 22
MEM = 256
EPS = 1e-6
D_IN = 4608


class Op:
    __slots__ = ("eng", "fn", "deps", "signal", "sigval", "dma", "sem", "semval", "prevval", "is_out", "idx", "lidx")

    def __init__(self, eng, fn, dma=False):
        self.eng = eng
        self.fn = fn
        self.deps = set()
        self.signal = False
        self.sigval = 0
        self.dma = dma
        self.sem = None
        self.semval = 0
        self.prevval = 0
        self.is_out = False
        self.idx = 0
        self.lidx = 0


class Sched:
    def __init__(self, nc, es):
        self.nc = nc
        self.ops = []
        self.rowb = {}
        self.wrec = {}
        self.rrec = {}
        self.engs = {"pe": nc.tensor, "act": nc.scalar, "dve": nc.vector, "pool": nc.gpsimd, "sp": nc.sync}
        self.esem = {k: es.enter_context(nc.semaphore("s_" + k)) for k in ("pe", "act", "dve", "pool")}
        self.ndsem = 12
        self.dsem = {q: [es.enter_context(nc.semaphore(f"d_{q}{i}")) for i in range(self.ndsem)] for q in ("sp", "pool")}
        self.dcount = {"sp": 0, "pool": 0}
        self.lcount = {}
        self.marks = []

    def mark(self, name):
        self.marks.append((name, len(self.ops)))

    def region(self, ap):
        name = ap.name
        rb = self.rowb.get(name)
        if rb is None:
            return None
        if name.startswith("ps"):
            return name, 0, 128, [(0, 2048)]
        esz = DTSZ[ap.dtype]
        offb = ap.offset * esz
        pat = list(ap.ap)
        p0 = offb // rb
        f0 = offb % rb
        if pat[0][0] * esz == rb:
            pc = pat[0][1]
            dims = pat[1:]
        else:
            pc = 1
            dims = pat
        dims = [(s * esz, c) for s, c in dims if c > 1 and s != 0]
        if not dims:
            ivs = [(f0, f0 + esz)]
        else:
            inner = dims[-1]
            outer = dims[:-1]
            run = (inner[1] - 1) * inner[0] + esz
            nouter = prod(c for _, c in outer) if outer else 1
            if nouter <= 64:
                offs = [0]
                for s, c in outer:
                    offs = [o + i * s for o in offs for i in range(c)]
                ivs = sorted((f0 + o, f0 + o + run) for o in offs)
                merged = [ivs[0]]
                for lo, hi in ivs[1:]:
                    if lo <= merged[-1][1]:
                        merged[-1] = (merged[-1][0], max(hi, merged[-1][1]))
                    else:
                        merged.append((lo, hi))
                ivs = merged
            else:
                ivs = [(f0, f0 + sum((c - 1) * s for s, c in dims) + esz)]
        return name, p0, p0 + pc, ivs

    def _add_dep(self, op, prod_op, kind):
        if prod_op is op:
            return
        if not op.dma and not prod_op.dma and op.eng == prod_op.eng:
            if op.eng == "pe":
                return
            if kind == "war" or (kind != "raw" and not STRICT):
                return
        op.deps.add(prod_op)

    def _read(self, op, ap):
        r = self.region(ap)
        if r is None:
            return
        name, p0, p1, ivs = r
        wl = self.wrec.setdefault(name, [])
        for rec in wl:
            if rec[0] < p1 and p0 < rec[1]:
                for lo, hi in ivs:
                    if rec[2] < hi and lo < rec[3]:
                        self._add_dep(op, rec[4], "raw")
                        break
        rl = self.rrec.setdefault(name, [])
        if name.startswith("ps"):
            for rec in rl:
                if rec[4].eng != op.eng:
                    self._add_dep(op, rec[4], "rar")
        for lo, hi in ivs:
            if not op.dma:
                for i, rec in enumerate(rl):
                    if rec[0] == p0 and rec[1] == p1 and rec[2] == lo and rec[3] == hi and (not rec[4].dma) and rec[4].eng == op.eng:
                        rl[i] = (p0, p1, lo, hi, op)
                        break
                else:
                    rl.append((p0, p1, lo, hi, op))
            else:
                rl.append((p0, p1, lo, hi, op))

    def _write(self, op, ap):
        r = self.region(ap)
        if r is None:
            return
        name, p0, p1, ivs = r
        wl = self.wrec.setdefault(name, [])
        rl = self.rrec.setdefault(name, [])
        for lst, kind in ((wl, "waw"), (rl, "war")):
            keep = []
            for rec in lst:
                hit = False
                contained = False
                if rec[0] < p1 and p0 < rec[1]:
                    for lo, hi in ivs:
                        if rec[2] < hi and lo < rec[3]:
                            hit = True
                            if lo <= rec[2] and rec[3] <= hi and p0 <= rec[0] and rec[1] <= p1:
                                contained = True
                            break
                if hit:
                    self._add_dep(op, rec[4], kind)
                if not contained:
                    keep.append(rec)
            lst[:] = keep
        for lo, hi in ivs:
            wl.append((p0, p1, lo, hi, op))

    def add(self, eng, fn, reads, writes, dma=False):
        op = Op(eng, fn, dma)
        op.idx = len(self.ops)
        if not dma:
            op.lidx = self.lcount.get(eng, 0)
            self.lcount[eng] = op.lidx + 1
        for ap in reads:
            if ap is not None and not isinstance(ap, (int, float)):
                self._read(op, ap)
        for ap in writes:
            if ap is not None:
                self._write(op, ap)
        latest = {}
        keep = set()
        for p in op.deps:
            if p.dma:
                keep.add(p)
            elif p.eng not in latest or latest[p.eng].idx < p.idx:
                latest[p.eng] = p
        for e, p in latest.items():
            if (not dma) and e == eng and op.lidx - p.lidx > SAME_ENG_GAP:
                continue
            keep.add(p)
            p.signal = True
        op.deps = keep
        if dma:
            k = self.dcount[eng]
            self.dcount[eng] = k + 1
            op.sem = self.dsem[eng][k % self.ndsem]
            op.semval = 16 * (k // self.ndsem + 1)
            op.prevval = 16 * (k // self.ndsem)
        self.ops.append(op)
        return op

    def mm(self, out, lhsT, rhs, start=True, stop=True):
        return self.add("pe", lambda e: e.matmul(out, lhsT, rhs, start=start, stop=stop), [lhsT, rhs], [out])

    def tr(self, out, in_, ident):
        return self.add("pe", lambda e: e.transpose(out, in_, ident), [in_, ident], [out])

    def act(self, out, in_, func, bias=None, scale=None, accum_out=None):
        kw = {}
        if bias is not None:
            kw["bias"] = bias
        if scale is not None:
            kw["scale"] = scale
        if accum_out is not None:
            kw["accum_out"] = accum_out
        if func == AF.Copy and any(v is not None and not isinstance(v, (int, float)) for v in (bias, scale)):
            func = AF.Identity
        return self.add("act", lambda e: e.activation(out=out, in_=in_, func=func, **kw), [in_, bias, scale], [out, accum_out])

    def tt(self, eng, out, in0, in1, op):
        return self.add(eng, lambda e: e.tensor_tensor(out=out, in0=in0, in1=in1, op=op), [in0, in1], [out])

    def ts(self, eng, out, in0, s1, s2, op0, op1=None):
        if op1 is None:
            return self.add(eng, lambda e: e.tensor_scalar(out=out, in0=in0, scalar1=s1, scalar2=None, op0=op0), [in0, s1], [out])
        return self.add(eng, lambda e: e.tensor_scalar(out=out, in0=in0, scalar1=s1, scalar2=s2, op0=op0, op1=op1), [in0, s1, s2], [out])

    def stt(self, eng, out, in0, scalar, in1, op0, op1):
        return self.add(eng, lambda e: e.scalar_tensor_tensor(out=out, in0=in0, scalar=scalar, in1=in1, op0=op0, op1=op1), [in0, scalar, in1], [out])

    def copy(self, eng, out, in_):
        if eng == "act":
            return self.act(out, in_, AF.Copy)
        return self.add(eng, lambda e: e.tensor_copy(out=out, in_=in_), [in_], [out])

    def memset(self, eng, out, val):
        return self.add(eng, lambda e: e.memset(out, val), [], [out])

    def recip(self, out, in_):
        return self.add("dve", lambda e: e.reciprocal(out=out, in_=in_), [in_], [out])

    def bn_stats(self, out, in_):
        return self.add("dve", lambda e: e.bn_stats(out=out, in_=in_), [in_], [out])

    def bn_aggr(self, out, in_):
        return self.add("dve", lambda e: e.bn_aggr(out=out, in_=in_), [in_], [out])

    def dma(self, q, out, in_, is_out=False):
        op = self.add(q, lambda e: e.dma_start(out=out, in_=in_), [in_], [out], dma=True)
        op.is_out = is_out
        return op

    def emit(self, limit=None):
        if limit:
            self.ops = self.ops[:limit]
            for op in self.ops:
                if not op.dma:
                    op.signal = False
            live = set(map(id, self.ops))
            for op in self.ops:
                for p in op.deps:
                    if not p.dma:
                        p.signal = True
        cnt = {k: 0 for k in self.esem}
        for op in self.ops:
            if not op.dma and op.signal:
                cnt[op.eng] += 1
                op.sigval = cnt[op.eng]
        waited = {k: {} for k in self.engs}
        outs = []

        def wait(engname, sem, val):
            w = waited[engname]
            key = id(sem)
            if w.get(key, 0) >= val:
                return
            w[key] = val
            self.engs[engname].wait_ge(sem, val)

        for op in self.ops:
            need = {}
            for p in op.deps:
                if p.dma:
                    sem, val = p.sem, p.semval
                else:
                    sem, val = self.esem[p.eng], p.sigval
                k = id(sem)
                if k not in need or need[k][1] < val:
                    need[k] = (sem, val)
            if op.dma and op.prevval > 0:
                k = id(op.sem)
                if k not in need or need[k][1] < op.prevval:
                    need[k] = (op.sem, op.prevval)
            for sem, val in need.values():
                wait(op.eng, sem, val)
            inst = op.fn(self.engs[op.eng])
            if op.dma:
                inst.then_inc(op.sem, 16)
                if op.is_out:
                    outs.append(op)
            elif op.signal:
                inst.then_inc(self.esem[op.eng], 1)
        for engname in ("sp", "act", "pool"):
            for op in outs:
                wait(engname, op.sem, op.semval)


def rs(ap, *dims):
    names = "abcdefgh"[: len(dims)]
    pat = "p (" + " ".join(names) + ") -> p " + " ".join(names)
    return ap.rearrange(pat, **{n: d for n, d in zip(names, dims)})


class Arena:
    def __init__(self, nc, es, S, name, nbytes):
        self.nbytes = nbytes
        self.t = es.enter_context(nc.sbuf_tensor(name, [128, nbytes // 2], BF16))
        S.rowb[name] = nbytes
        self.off = 0
        self.hi = 0

    def at(self, off, dtype, *shape):
        n = prod(shape)
        nb = n * DTSZ[dtype]
        assert off % 4 == 0 and off + nb <= self.nbytes, (off, nb, self.nbytes)
        a = self.t[:, off // 2: (off + nb) // 2]
        if dtype != BF16:
            a = a.bitcast(dtype)
        if len(shape) > 1:
            a = rs(a, *shape)
        return a

    def alloc(self, dtype, *shape):
        nb = prod(shape) * DTSZ[dtype]
        off = self.off
        self.off = (off + nb + 63) // 64 * 64
        self.hi = max(self.hi, self.off)
        return self.at(off, dtype, *shape)


def build_program():
    nc = bass.Bass("TRN2", target_bir_lowering=False)
    es = contextlib.ExitStack()

    def din(name, shape, dt=F32):
        return nc.dram_tensor(name, list(shape), dt, kind="ExternalInput").ap()

    def dout(name, shape):
        return nc.dram_tensor(name, list(shape), F32, kind="ExternalOutput").ap()

    x_p = din("x_p", [SEQ, D])
    x_s = din("x_s", [NS, D])
    mem = din("mem", [MEM, D])
    ck = din("ck", [NB, MEM, 256])
    cv = din("cv", [NB, MEM, 256])
    st_pool = din("st_pool", [NB, 15, BW])
    st_conv = din("st_conv", [NB, 2, DFF])
    norm_mix_pre = din("norm_mix_pre", [1, D])
    norm_mix_post = din("norm_mix_post", [1, D])
    norm_ffn_pre = din("norm_ffn_pre", [1, D])
    norm_ffn_post = din("norm_ffn_post", [1, D])
    w_in = din("w_in", [D, D_IN])
    b_gate = din("b_gate", [1, 3 * D])
    gmlp_ln_g = din("gmlp_ln_g", [1, AW])
    gmlp_ln_b = din("gmlp_ln_b", [1, AW])
    w_spatial = din("w_spatial", [4, 128, 128])
    b_spatial = din("b_spatial", [4, 128])
    w_pool = din("w_pool", [4, 64, 64])
    pool_scale = din("pool_scale", [1, BW])
    mem_norm = din("mem_norm", [1, D])
    w_mem_kv = din("w_mem_kv", [D, 512])
    w_branch_a = din("w_branch_a", [AW, D])
    w_branch_b = din("w_branch_b", [BW, D])
    w_branch_c = din("w_branch_c", [BW, D])
    w_out = din("w_out", [D, D])
    w_up = din("w_up", [D, 2 * DFF])
    conv_w = din("conv_w", [3, DFF])
    conv_b = din("conv_b", [1, DFF])
    w_down = din("w_down", [DFF, D])
    c_identb = din("c_identb", [128, 128], BF16)
    c_identf = din("c_identf", [128, 128])
    c_maskT = din("c_maskT", [128, 128])
    c_rcnt = din("c_rcnt", [128, 32])

    y_p = dout("y_p", [SEQ, D])
    y_s = dout("y_s", [NS, D])
    o_mk = dout("o_mk", [MEM, 256])
    o_mv = dout("o_mv", [MEM, 256])
    o_pool_p = dout("o_pool_p", [15, BW])
    o_conv_p = dout("o_conv_p", [2, DFF])
    o_v_s = dout("o_v_s", [NS, AW])
    o_pool_s = dout("o_pool_s", [NB, 15, BW])
    o_conv_s = dout("o_conv_s", [NB, 2, DFF])

    with es:
        S = Sched(nc, es)
        banks = []
        for i in range(8):
            t = es.enter_context(nc.psum_tensor(f"ps{i}", [128, 512], F32))
            S.rowb[f"ps{i}"] = 2048
            banks.append(t)
        bank_i = [0]
        bank_n = [8]

        def nbank():
            b = banks[bank_i[0] % bank_n[0]]
            bank_i[0] += 1
            assert not (S.wrec.get(b.name) and not S.rrec.get(b.name)), ("PSUM bank reused before consumption", b.name)
            return b

        PB = 90624
        UB = 121856
        P = Arena(nc, es, S, "arenaP", PB)
        U = Arena(nc, es, S, "arenaU", UB)

        identb = P.alloc(BF16, 128)
        identf = P.alloc(F32, 128)
        onesb = P.alloc(BF16, 128)
        maskT = P.alloc(F32, 128)
        rcnt = P.alloc(F32, 2, 16)
        neghalf = P.alloc(F32, 8)
        V1T = P.alloc(F32, 72)
        V2T = P.alloc(F32, 66)
        hb = P.alloc(F32, 24)
        gpost1h = P.alloc(F32, D)
        gpost2 = P.alloc(F32, D)
        lng = P.alloc(F32, AW)
        lnb = P.alloc(F32, AW)
        bsb = P.alloc(F32, 4, 128)
        ws4 = P.alloc(F32, 4, 4, 4)
        bs4 = P.alloc(F32, 4, 4)
        wsT = P.alloc(BF16, 4, 128)
        Wbd = P.alloc(BF16, 2, 128)
        kT = P.alloc(BF16, 2, MEM)
        Vb = P.alloc(BF16, 2, 256)
        X = P.alloc(F32, 5, D)
        xnT = P.alloc(BF16, KC, NCOL)
        RING_N = 2
        ring = [P.alloc(BF16, 6144) for _ in range(RING_N)]
        bTs = P.alloc(F32, 2, NB, 19)
        gTs = P.alloc(F32, FC, NB, 6)
        halo_g = P.alloc(F32, FC, 2)
        halo_b = P.alloc(F32, 2, 15)
        PSt = P.alloc(F32, 2, 80)
        CS = P.alloc(F32, FC, 34)
        stats = P.alloc(F32, 128)
        assert P.hi <= PB, P.hi
        print('P arena used', P.hi, 'of', PB)

        U.off = 0
        uT = U.alloc(BF16, 4, NCOL)
        qT = U.alloc(BF16, 2, NCOL)
        attnT = U.alloc(BF16, 2, NCOL)
        mixedT = U.alloc(BF16, 2, NCOL)
        pooledT = U.alloc(BF16, 2, NCOL)
        bT = U.alloc(F32, 2, 15 + NG)
        TMP0 = U.off
        junk = U.alloc(BF16, D)
        xnb = [U.alloc(BF16, D) for _ in range(3)]
        vg = [U.alloc(F32, AW) for _ in range(4)]
        vn1 = U.alloc(F32, AW)
        vn2 = U.alloc(F32, AW)
        vnb = [U.alloc(BF16, AW) for _ in range(2)]
        vnS = U.alloc(F32, AW)
        spt = U.alloc(F32, 4, 128)
        vnTs = U.alloc(F32, 4, NB, 4)
        sgs = U.alloc(F32, 4, NB, 4)
        s2 = U.alloc(F32, 2, 15 + NG)
        s4 = U.alloc(F32, 2, 15 + NG)
        s8 = U.alloc(F32, 15 + NG)
        s16 = U.alloc(F32, 15 + NG)
        ptmp = U.alloc(F32, 2, 16)
        expT = [U.alloc(BF16, 2, 512) for _ in range(2)]
        rd = [U.alloc(F32, 512) for _ in range(2)]
        Ksb = [U.alloc(BF16, 2, 256) for _ in range(2)]
        Vsb = [U.alloc(BF16, 2, 256) for _ in range(2)]
        kTb = [U.alloc(BF16, 2, 256) for _ in range(2)]
        expS = [U.alloc(BF16, 2, 16) for _ in range(2)]
        rdS = U.alloc(F32, NB, 16)
        TMP1 = U.off
        stage = U.alloc(F32, 2816)
        U.off = TMP0
        tg = [U.alloc(F32, 512) for _ in range(3)]
        pj = [U.alloc(F32, 512) for _ in range(3)]
        dtmp = [U.alloc(F32, 512) for _ in range(2)]
        junk2 = U.alloc(BF16, D)
        hnb = [U.alloc(BF16, D) for _ in range(3)]
        assert U.off <= TMP1
        U.off = TMP1
        mergedT = U.alloc(BF16, KC, NCOL)
        wba = U.alloc(BF16, 4, D)
        WBB_OFF = U.off
        wbb = U.alloc(BF16, 2, D)
        wbc = U.alloc(BF16, 2, D)
        wout = U.alloc(BF16, KC, D)
        MIX_END = U.off
        assert MIX_END <= UB, MIX_END
        print('U mixer layout end', MIX_END, 'of', UB)
        U.off = 0
        wdn = U.alloc(BF16, FC, D)
        actT = U.alloc(BF16, FC, NCOL)
        gTp = [U.alloc(F32, 2 + NG) for _ in range(2)]
        cT = [U.alloc(F32, NG) for _ in range(2)]
        ge = [U.alloc(F32, NG) for _ in range(2)]
        cTs = U.alloc(F32, NB, 4)
        geS = U.alloc(F32, NB, 4)
        junk3 = U.alloc(BF16, 512)
        ftmp = [U.alloc(F32, 512) for _ in range(2)]
        PStT = U.alloc(F32, 256)
        CST = U.alloc(F32, DFF)
        junkA = U.alloc(BF16, D)
        xnbA = [U.alloc(BF16, D) for _ in range(3)]
        assert U.off <= UB, U.off
        print('U ffn layout end', U.off, 'of', UB)
        U.off = max(U.off, MIX_END)
        gffnb = U.alloc(F32, D)
        assert U.off <= UB, U.off

        sp = "sp"
        S.dma(sp, identb, c_identb[:, :])
        S.dma(sp, identf, c_identf[:, :])
        S.dma(sp, maskT, c_maskT[:, :])
        S.dma(sp, rcnt, c_rcnt[:, :].rearrange("p (c t) -> p c t", c=2))
        S.dma(sp, X[0:NS, 4, :], x_s[:, :])
        for i in range(NG // 128):
            S.dma(sp, X[:, i, :], x_p[i * 128:(i + 1) * 128, :])
        for mt in range(2):
            S.dma(sp, [s2.rearrange("p c l -> p (c l)")[:, 0:D], s4.rearrange("p c l -> p (c l)")[:, 0:D]][mt], mem[mt * 128:(mt + 1) * 128, :])
        S.memset("dve", onesb, 1.0)
        S.memset("dve", neghalf, -0.5)
        S.memset("dve", halo_g, 0.0)
        S.memset("dve", halo_b, 0.0)

        st1 = stage[:, 0:128]
        S.dma(sp, st1[0:8, :], norm_mix_pre.rearrange("o (r p) -> (o r) p", p=128))
        S.dma(sp, st1[8:16, :], norm_ffn_pre.rearrange("o (r p) -> (o r) p", p=128))
        S.dma(sp, st1[16:40, :], b_gate.rearrange("o (r p) -> (o r) p", p=128))
        S.dma(sp, st1[40:42, :], pool_scale.rearrange("o (r p) -> (o r) p", p=128))
        S.dma(sp, st1[42:64, :], conv_b.rearrange("o (r p) -> (o r) p", p=128))
        S.dma(sp, st1[64:72, :], mem_norm.rearrange("o (r p) -> (o r) p", p=128))
        bk = nbank()
        S.tr(bk[:, 0:72], st1[0:72, :], identf[0:72, 0:72])
        S.copy("dve", V1T, bk[:, 0:72])
        st2 = stage[:, 128:256]
        S.dma(sp, st2[0:66, :], conv_w.rearrange("k (r p) -> (k r) p", p=128))
        bk = nbank()
        S.tr(bk[:, 0:66], st2[0:66, :], identf[0:66, 0:66])
        S.copy("dve", V2T, bk[:, 0:66])
        S.ts("dve", hb, V1T[:, 16:40], 0.5, None, ALU.mult)

        S.dma(sp, gpost1h, norm_mix_post[0, :].partition_broadcast(128))
        S.ts("dve", gpost1h, gpost1h, 0.5, None, ALU.mult)
        S.dma(sp, gpost2, norm_ffn_post[0, :].partition_broadcast(128))
        S.dma(sp, gffnb, norm_ffn_pre[0, :].partition_broadcast(128))
        S.dma(sp, lng, gmlp_ln_g[0, :].partition_broadcast(128))
        S.dma(sp, lnb, gmlp_ln_b[0, :].partition_broadcast(128))
        S.dma(sp, bsb, b_spatial.partition_broadcast(128))
        for g in range(4):
            S.dma(sp, ws4[:, g, :, :], w_spatial[g, 0:4, 0:4].partition_broadcast(128))
        S.dma(sp, bs4, b_spatial[:, 0:4].partition_broadcast(128))

        wst = rs(stage[:, 256:768], 4, 128)
        for g in range(4):
            S.dma(sp, wst[:, g, :], w_spatial[g, :, :])
            bk = nbank()
            S.tr(bk[:, 0:128], wst[:, g, :], identf)
            S.tt("dve", wsT[:, g, :], bk[:, 0:128], maskT, ALU.mult)

        wbs = rs(stage[:, 768:1024], 2, 128)
        S.memset("dve", wbs, 0.0)
        for c in range(2):
            S.dma(sp, wbs[0:64, c, 0:64], w_pool[2 * c, :, :])
            S.dma(sp, wbs[64:128, c, 64:128], w_pool[2 * c + 1, :, :])
        S.copy("dve", Wbd, wbs)

        sps = rs(stage[:, 1024:1536], 2, 256)
        stp = st_pool.rearrange("b r f -> (b r) f")
        for hh in range(2):
            S.dma(sp, sps[0:120, hh, :], stp[hh * 120:(hh + 1) * 120, :])
        for hh in range(2):
            for c in range(2):
                bk = nbank()
                S.tr(bk[:, 0:120], sps[0:120, hh, c * 128:(c + 1) * 128], identf[0:120, 0:120])
                S.copy("dve", bTs[:, c, hh * 8:(hh + 1) * 8, 0:15], rs(bk[:, 0:120], 8, 15))
        S.dma(sp, o_pool_s[:, 0:11, :], st_pool[:, 4:15, :], is_out=True)

        scs = stage[0:32, 0:DFF]
        S.dma(sp, scs, st_conv.rearrange("b r f -> (b r) f"))
        for half in range(2):
            bk = nbank()
            for j in range(11):
                fc = half * 11 + j
                S.tr(bk[:, j * 32:(j + 1) * 32], scs[:, fc * 128:(fc + 1) * 128], identf[0:32, 0:32])
            S.copy("dve", gTs[:, half * 11:(half + 1) * 11, :, 0:2], rs(bk[:, 0:352], 11, NB, 2))

        S.mark('setup_done')
        def rstd_pool(out, acc, rows):
            S.ts("pool", out[0:rows], acc[0:rows], EPS, None, ALU.add)
            S.tt("pool", out[0:rows], out[0:rows], neghalf[0:rows, 0:1], ALU.pow)

        def pipeline(stages, n):
            ns = len(stages)
            for k in range(n + ns - 1):
                for st in range(ns - 1, -1, -1):
                    i = k - st
                    if 0 <= i < n:
                        stages[st](i)

        def norm_transpose_stages(items, gcol0, dstT, jk, xbs, col0, gb=None):
            st = {}

            def s_sq(i):
                src, rows, c0 = items[i]
                acc = stats[:, col0 + 2 * i: col0 + 2 * i + 1]
                S.act(jk[0:rows], src, AF.Square, scale=1.0 / 32.0, accum_out=acc[0:rows])

            def s_rstd(i):
                src, rows, c0 = items[i]
                rstd_pool(stats[:, col0 + 2 * i + 1: col0 + 2 * i + 2], stats[:, col0 + 2 * i: col0 + 2 * i + 1], rows)

            def s_norm(i):
                src, rows, c0 = items[i]
                xb = xbs[i % len(xbs)]
                if gb is not None:
                    S.stt("dve", xb[0:rows], src, stats[0:rows, col0 + 2 * i + 1: col0 + 2 * i + 2], gb[0:rows], ALU.mult, ALU.mult)
                else:
                    S.ts("dve", xb[0:rows], src, stats[0:rows, col0 + 2 * i + 1: col0 + 2 * i + 2], None, ALU.mult)

            def s_tr(i):
                src, rows, c0 = items[i]
                xb = xbs[i % len(xbs)]
                bk = nbank()
                st[i] = bk
                bkb = bk[:].bitcast(BF16)
                for kc in range(KC):
                    S.tr(bkb[:, kc * 128: kc * 128 + rows], xb[0:rows, kc * 128:(kc + 1) * 128], identb[0:rows, 0:rows])

            def s_evac(i):
                src, rows, c0 = items[i]
                bkb = st[i][:].bitcast(BF16)
                if gb is not None:
                    S.act(dstT[:, :, c0:c0 + rows], bkb.rearrange("p (k t) -> p k t", k=KC)[:, :, 0:rows], AF.Copy)
                    return
                for kc in range(KC):
                    S.ts("dve", dstT[:, kc, c0:c0 + rows], bkb[:, kc * 128: kc * 128 + rows], V1T[:, gcol0 + kc:gcol0 + kc + 1], None, ALU.mult)

            return [s_sq, s_rstd, s_norm, s_tr, s_evac]

        ring_i = [0]

        def ring_slot():
            r = ring[ring_i[0] % RING_N]
            ring_i[0] += 1
            return r

        def load_w(dst3, src2d):
            S.dma("pool", dst3, src2d.rearrange("(kc p) f -> p kc f", p=128))

        memX = [s2.rearrange("p c l -> p (c l)")[:, 0:D], s4.rearrange("p c l -> p (c l)")[:, 0:D]]
        wb_first = rs(ring_slot()[:, 0:KC * 512], KC, 512)
        load_w(wb_first, w_in[:, 0:512])
        wkv = U.at(WBB_OFF, BF16, KC, 512)
        load_w(wkv, w_mem_kv)
        wb1_first = rs(ring_slot()[:, 0:KC * 512], KC, 512)
        load_w(wb1_first, w_in[:, 512:1024])
        memnT = mergedT
        pipeline(norm_transpose_stages([(memX[mt], 128, mt * 128) for mt in range(2)], 64, memnT, junk, xnb, 112), 2)

        def kv_part2():
            for hp in range(2):
                bk = nbank()
                for kc in range(KC):
                    S.mm(bk[:, 0:MEM], wkv[:, kc, hp * 128:(hp + 1) * 128], memnT[:, kc, 0:MEM], start=kc == 0, stop=kc == KC - 1)
                S.copy("act", kT[:, hp, :], bk[:, 0:MEM])
            kvt = [pj[0], pj[1]]
            for mc in range(2):
                bk = nbank()
                for kc in range(KC):
                    S.mm(bk[:, :], memnT[:, kc, mc * 128:(mc + 1) * 128], wkv[:, kc, :], start=kc == 0, stop=kc == KC - 1)
                S.copy("act", kvt[mc], bk[:, :])
                S.copy("dve", Vb[:, mc, :], bk[:, 256:512])
                S.dma(sp, o_mk[mc * 128:(mc + 1) * 128, :], kvt[mc][:, 0:256], is_out=True)
                S.dma(sp, o_mv[mc * 128:(mc + 1) * 128, :], kvt[mc][:, 256:512], is_out=True)

        S.mark('kv_done')
        def pooling(xb4, s2v, s4v, s8v, s16v, outv, L, first):
            x = xb4
            S.tt("dve", s2v[:, :, :, 1:L], x[:, :, :, 1:L], x[:, :, :, 0:L - 1], ALU.add)
            S.tt("dve", s4v[64:128, 0, :, 3:L], s2v[64:128, 0, :, 3:L], s2v[64:128, 0, :, 1:L - 2], ALU.add)
            S.tt("dve", s4v[:, 1, :, 3:L], s2v[:, 1, :, 3:L], s2v[:, 1, :, 1:L - 2], ALU.add)
            S.tt("dve", s8v[:, :, 7:L], s4v[:, 1, :, 7:L], s4v[:, 1, :, 3:L - 4], ALU.add)
            S.tt("dve", s16v[64:128, :, 15:L], s8v[64:128, :, 15:L], s8v[64:128, :, 7:L - 8], ALU.add)
            srcs = [(0, 64, 0, s2v[0:64, 0], 0.5), (64, 128, 0, s4v[64:128, 0], 0.25),
                    (0, 64, 1, s8v[0:64], 0.125), (64, 128, 1, s16v[64:128], 1.0 / 16)]
            for p0, p1, c, sv, rw in srcs:
                S.stt("dve", outv[p0:p1, c, :, :], sv[:, :, 15:L], rw, x[p0:p1, c, :, 15:L], ALU.mult, ALU.subtract)
            if first:
                for p0, p1, c, sv, rw in srcs:
                    S.tt("dve", ptmp[p0:p1, c, :], sv[:, 0, 15:31], rcnt[p0:p1, c, :], ALU.mult)
                    S.tt("dve", outv[p0:p1, c, 0, 0:16], ptmp[p0:p1, c, :], x[p0:p1, c, 0, 15:31], ALU.subtract)

        def make_tiles(g):
            t = [((4 * g + i) % 5, 128, i * 128, i) for i in range(NG // 128)]
            if g == 0:
                t = [(4, NS, NG, -1)] + t
            return t

        def a_load(g, t):
            xi, rows, c0, pi = t
            if pi >= 0:
                S.dma(sp, X[:, xi, :], x_p[g * NG + pi * 128: g * NG + (pi + 1) * 128, :])
            else:
                S.dma(sp, X[0:NS, xi, :], x_s[:, :])

        def a_stages(tl):
            return norm_transpose_stages([(X[0:rows, xi, :], rows, c0) for (xi, rows, c0, pi) in tl], 0, xnT, junkA, xnbA, 0)

        def prefetch_win01():
            a = rs(ring_slot()[:, 0:KC * 512], KC, 512)
            load_w(a, w_in[:, 0:512])
            b = rs(ring_slot()[:, 0:KC * 512], KC, 512)
            load_w(b, w_in[:, 512:1024])
            return a, b

        class Stepper:
            def __init__(self, stages, n):
                self.stages, self.n, self.k = stages, n, 0

            def done(self):
                return self.k >= self.n + len(self.stages) - 1

            def step(self):
                ns = len(self.stages)
                for st in range(ns - 1, -1, -1):
                    i = self.k - st
                    if 0 <= i < self.n:
                        self.stages[st](i)
                self.k += 1

        for g in range(NGRP):
            first = g == 0
            last = g == NGRP - 1
            tiles = make_tiles(g)
            blocks = [(0, NG)]
            if first:
                blocks.append((NG, NS))

            if first:
                pipeline(a_stages(tiles), len(tiles))
                kv_part2()


            S.mark(f'g{g}_A_done')
            if first:
                wb = wb_first
                wb1 = wb1_first
            else:
                wb, wb1 = nxt_w01
            for (c0, n) in blocks:
                for fc in range(4):
                    bk = nbank()
                    for kc in range(KC):
                        S.mm(bk[:, 0:n], wb[:, kc, fc * 128:(fc + 1) * 128], xnT[:, kc, c0:c0 + n], start=kc == 0, stop=kc == KC - 1)
                    S.act(uT[:, fc, c0:c0 + n], bk[:, 0:n], AF.Gelu_apprx_tanh)

            S.mark(f'g{g}_B0_done')
            wb2 = rs(ring_slot()[:, 0:KC * 512], KC, 512)
            load_w(wb2, w_in[:, 1024:1536])
            b1 = {}
            b1_tiles = tiles[1:] + tiles[:1] if first else tiles
            SB1 = 48

            def b1_mm(ti):
                xi, rows, c0, pi = b1_tiles[ti]
                bk = nbank()
                b1[ti] = bk
                for kc in range(KC):
                    S.mm(bk[0:rows, :], xnT[:, kc, c0:c0 + rows], wb1[:, kc, :], start=kc == 0, stop=kc == KC - 1)

            def b1_gelu(ti):
                xi, rows, c0, pi = b1_tiles[ti]
                S.act(vg[ti % 4][0:rows], b1[ti][0:rows, :], AF.Gelu_apprx_tanh)

            def b1_stats(ti):
                xi, rows, c0, pi = b1_tiles[ti]
                cb = SB1 + 9 * ti
                S.bn_stats(stats[0:rows, cb:cb + 6], vg[ti % 4][0:rows])
                S.bn_aggr(stats[0:rows, cb + 6:cb + 8], stats[0:rows, cb:cb + 6])

            def b1_rstd(ti):
                xi, rows, c0, pi = b1_tiles[ti]
                cb = SB1 + 9 * ti
                rstd_pool(stats[:, cb + 8:cb + 9], stats[:, cb + 7:cb + 8], rows)

            def b1_norm(ti):
                xi, rows, c0, pi = b1_tiles[ti]
                cb = SB1 + 9 * ti
                S.ts("dve", vn1[0:rows], vg[ti % 4][0:rows], stats[0:rows, cb + 6:cb + 7], stats[0:rows, cb + 8:cb + 9], ALU.subtract, ALU.mult)
                S.tt("dve", vn2[0:rows], vn1[0:rows], lng[0:rows], ALU.mult)
                if pi >= 0:
                    S.tt("dve", vnb[ti % 2], vn2, lnb, ALU.add)
                else:
                    S.tt("dve", vnS[0:NS], vn2[0:NS], lnb[0:NS], ALU.add)
                    S.dma(sp, o_v_s[:, :], vnS[0:NS, :], is_out=True)

            def b1_spatial(ti):
                xi, rows, c0, pi = b1_tiles[ti]
                bk2 = nbank()
                b1[("sp", ti)] = bk2
                if pi >= 0:
                    vb = vnb[ti % 2]
                    for gch in range(4):
                        S.mm(bk2[:, gch * 128:(gch + 1) * 128], vb[:, gch * 128:(gch + 1) * 128], wsT[:, gch, :], start=True, stop=True)
                else:
                    for gch in range(4):
                        S.tr(bk2[:, gch * 64:(gch + 1) * 64], vnS[0:NS, gch * 128:(gch + 1) * 128], identf[0:NS, 0:NS])

            def b1_gate(ti):
                xi, rows, c0, pi = b1_tiles[ti]
                bk2 = b1[("sp", ti)]
                if pi >= 0:
                    S.tt("dve", spt, rs(bk2[:, :], 4, 128), bsb, ALU.add)
                    S.tt("dve", uT[:, :, c0:c0 + 128], spt, uT[:, :, c0:c0 + 128], ALU.mult)
                else:
                    S.copy("dve", vnTs, rs(bk2[:, 0:256], 4, NB, 4))
                    for sidx in range(4):
                        for gch in range(4):
                            for t in range(sidx, 4):
                                acc = sgs[:, gch, :, t]
                                if sidx == 0:
                                    S.ts("dve", acc, vnTs[:, gch, :, 0], ws4[:, gch, t, 0:1], bs4[:, gch, t:t + 1], ALU.mult, ALU.add)
                                else:
                                    S.stt("dve", acc, vnTs[:, gch, :, sidx], ws4[:, gch, t, sidx:sidx + 1], acc, ALU.mult, ALU.add)
                    S.tt("dve", uT[:, :, NG:NG + NS], sgs.rearrange("p g b t -> p g (b t)"), uT[:, :, NG:NG + NS], ALU.mult)

            def b2_unit(c0, n, j):
                def f():
                    bk = nbank()
                    for kc in range(KC):
                        S.mm(bk[:, 0:n], wb2[:, kc, j * 128:(j + 1) * 128], xnT[:, kc, c0:c0 + n], start=kc == 0, stop=kc == KC - 1)
                    if j < 2:
                        if c0 == 0:
                            S.copy("act", bT[:, j, 15:15 + NG], bk[:, 0:NG])
                        else:
                            S.copy("act", bTs[:, j, :, 15:19], rs(bk[:, 0:NS], NB, 4))
                    else:
                        S.copy("act", qT[:, j - 2, c0:c0 + n], bk[:, 0:n])
                return f

            b2_units = [b2_unit(c0, n, j) for (c0, n) in blocks for j in range(4)]
            b1step = Stepper([b1_mm, b1_gelu, b1_stats, b1_rstd, b1_norm, b1_spatial, b1_gate], len(tiles))
            while not b1step.done():
                b1step.step()
                if b2_units and b1step.k >= 5:
                    b2_units.pop(0)()
            aT = uT

            def gate_block(i):
                wg = rs(ring_slot()[:, 0:KC * 768], KC, 3, 256)
                for j in range(3):
                    c = 1536 + j * 1024 + i * 256
                    load_w(wg[:, :, j, :], w_in[:, c:c + 256])
                return wg

            S.mark(f'g{g}_B1_done')
            while b2_units:
                b2_units.pop(0)()
            wg_next = gate_block(0)
            load_w(wba, w_branch_a)
            load_w(wbb, w_branch_b)
            load_w(wbc, w_branch_c)

            S.mark(f'g{g}_B2_done')
            S.mark(f'g{g}_pool_done')
            sb = {}

            def att_scores(h):
                hp, base = h // 2, (h % 2) * 64
                ex = expT[h % 2]
                for mc in range(2):
                    bk = nbank()
                    S.mm(bk[:, 0:NG], kT[base:base + 64, hp, mc * 128:(mc + 1) * 128], qT[base:base + 64, hp, 0:NG], start=True, stop=True)
                    S.act(ex[:, mc, :], bk[:, 0:NG], AF.Exp, scale=0.125)

            def att_pv(h):
                hp, base = h // 2, (h % 2) * 64
                ex = expT[h % 2]
                pv = nbank()
                for mc in range(2):
                    S.mm(pv[:, 0:NG], Vb[:, mc, hp * 128:(hp + 1) * 128], ex[:, mc, :], start=mc == 0, stop=mc == 1)
                dn = nbank()
                for mc in range(2):
                    S.mm(dn[:, 0:NG], onesb, ex[:, mc, :], start=mc == 0, stop=mc == 1)
                r = rd[h % 2]
                S.recip(r[base:base + 64, :], dn[base:base + 64, 0:NG])
                S.tt("dve", attnT[base:base + 64, hp, 0:NG], pv[base:base + 64, 0:NG], r[base:base + 64, :], ALU.mult)

            for i in range(5):
                if i < 4:
                    att_scores(i)
                if i >= 1:
                    att_pv(i - 1)

            L = 15 + NG
            S.copy("dve", bT[:, :, 0:15], halo_b)
            pooling(bT.rearrange("p c (b l) -> p c b l", b=1), s2.rearrange("p c (b l) -> p c b l", b=1),
                    s4.rearrange("p c (b l) -> p c b l", b=1), s8.rearrange("p (b l) -> p b l", b=1),
                    s16.rearrange("p (b l) -> p b l", b=1),
                    pooledT[:, :, 0:NG].rearrange("p c (b l) -> p c b l", b=1), L, first)
            if last:
                S.copy("dve", PSt[:, :, 64:79], bT[:, :, L - 15:L])
            else:
                S.copy("dve", halo_b, bT[:, :, L - 15:L])
            if first:
                s2s = rs(s2.rearrange("p c l -> p (c l)")[:, 0:2 * NB * 19], 2, NB, 19)
                s4s = rs(s4.rearrange("p c l -> p (c l)")[:, 0:2 * NB * 19], 2, NB, 19)
                s8s = rs(s8[:, 0:NB * 19], NB, 19)
                s16s = rs(s16[:, 0:NB * 19], NB, 19)
                pooling(bTs, s2s, s4s, s8s, s16s, pooledT[:, :, NG:NG + NS].rearrange("p c (b t) -> p c b t", b=NB), 19, False)
                S.copy("dve", PSt[:, :, 0:64].rearrange("p c (t b) -> p c b t", t=4), bTs[:, :, :, 15:19])

            def glin():
                for (c0, n) in blocks:
                    for c in range(2):
                        bk = nbank()
                        S.mm(bk[:, 0:n], Wbd[:, c, :], pooledT[:, c, c0:c0 + n], start=True, stop=True)
                        S.act(mixedT[:, c, c0:c0 + n], bk[:, 0:n], AF.Copy, scale=V1T[:, 40 + c:41 + c])

            glin_pending = [glin]

            S.mark(f'g{g}_attp_done')
            if first:
                scol = NG

                def s_load(b):
                    S.dma("pool", Ksb[b % 2], ck[b].rearrange("(mc m) f -> m mc f", m=128))
                    S.dma("pool", Vsb[b % 2], cv[b].rearrange("(mc m) f -> m mc f", m=128))

                def s_tr(b):
                    bk = nbank()
                    bkb = bk[:].bitcast(BF16)
                    for hp in range(2):
                        for mc in range(2):
                            S.tr(bkb[:, hp * 256 + mc * 128: hp * 256 + (mc + 1) * 128], Ksb[b % 2][:, mc, hp * 128:(hp + 1) * 128], identb)
                    S.copy("act", kTb[b % 2], rs(bkb[:, 0:512], 2, 256))

                bank_n[0] = 6
                OS = banks[6]
                DS = banks[7]
                OSv = rs(OS[:, 0:256], NB, 4, 4)
                DSv = rs(DS[:, 0:256], NB, 4, 4)

                def s_scores(b):
                    bkp = [nbank(), nbank()]
                    ex5 = expS[b % 2].rearrange("p mc (hp par t) -> p mc hp par t", hp=2, par=2)
                    for par in range(2):
                        base = par * 64
                        sv = rs(bkp[par][:, 0:16], 2, 2, 4)
                        for hp in range(2):
                            for mc in range(2):
                                S.mm(sv[:, mc, hp, :], kTb[b % 2][base:base + 64, hp, mc * 128:(mc + 1) * 128],
                                     qT[base:base + 64, hp, scol + b * 4: scol + b * 4 + 4], start=True, stop=True)
                    for par in range(2):
                        S.act(ex5[:, :, :, par, :], rs(bkp[par][:, 0:16], 2, 2, 4), AF.Exp, scale=0.125)

                def s_pv(b):
                    ex = rs(expS[b % 2].rearrange("p mc x -> p (mc x)"), 2, 4, 4)
                    for h in range(4):
                        hp = h // 2
                        for mc in range(2):
                            S.mm(OSv[:, b, h, :], Vsb[b % 2][:, mc, hp * 128:(hp + 1) * 128], ex[:, mc, h, :], start=mc == 0, stop=mc == 1)
                    for mc in range(2):
                        S.mm(DSv[:, b, :, :].rearrange("p h t -> p (h t)"), onesb, expS[b % 2][:, mc, :], start=mc == 0, stop=mc == 1)

                def s_loadk(b):
                    S.dma("pool", Ksb[b % 2], ck[b].rearrange("(mc m) f -> m mc f", m=128))

                def s_loadv(b):
                    S.dma("pool", Vsb[b % 2], cv[b].rearrange("(mc m) f -> m mc f", m=128))

                s_loadk(0)
                s_loadk(1)
                s_loadv(0)
                s_tr(0)
                for b in range(NB):
                    if b + 1 < NB:
                        s_tr(b + 1)
                    s_scores(b)
                    if b >= 1:
                        s_pv(b - 1)
                    if b + 2 < NB:
                        s_loadk(b + 2)
                    if b + 1 < NB:
                        s_loadv(b + 1)
                s_pv(NB - 1)
                S.recip(rdS.rearrange("p b x -> p (b x)"), DS[:, 0:256])
                rdv = rs(rdS.rearrange("p b x -> p (b x)"), NB, 4, 4)
                for h in range(4):
                    hp, base = h // 2, (h % 2) * 64
                    S.tt("dve", attnT[base:base + 64, hp, scol:scol + NS].rearrange("p (b t) -> p b t", b=NB),
                         OSv[base:base + 64, :, h, :], rdv[base:base + 64, :, h, :], ALU.mult)
                bank_n[0] = 8

            S.mark(f'g{g}_atts_done')
            load_w(wout, w_out)
            for i in range(4):
                wg = wg_next
                if i + 1 < 4:
                    wg_next = gate_block(i + 1)
                for fl in range(2):
                    fo = 2 * i + fl
                    for (c0, n) in blocks:
                        gb = []
                        for j in range(3):
                            bk = nbank()
                            for kc in range(KC):
                                S.mm(bk[:, 0:n], wg[:, kc, j, fl * 128:(fl + 1) * 128], xnT[:, kc, c0:c0 + n], start=kc == 0, stop=kc == KC - 1)
                            S.act(tg[j][:, 0:n], bk[:, 0:n], AF.Tanh, bias=hb[:, j * 8 + fo: j * 8 + fo + 1], scale=0.5)
                            gb.append(bk)
                        if glin_pending:
                            glin_pending.pop()()
                        srcs = [(wba, aT, 4), (wbb, mixedT, 2), (wbc, attnT, 2)]
                        for j, (wbr, actv, nk) in enumerate(srcs):
                            bk = nbank()
                            for kc in range(nk):
                                S.mm(bk[:, 0:n], wbr[:, kc, fo * 128:(fo + 1) * 128], actv[:, kc, c0:c0 + n], start=kc == 0, stop=kc == nk - 1)
                            S.stt("dve", pj[j][:, 0:n], tg[j][:, 0:n], 1.0, bk[:, 0:n], ALU.add, ALU.mult)
                        S.tt("pool", pj[0][:, 0:n], pj[0][:, 0:n], pj[1][:, 0:n], ALU.add)
                        S.tt("pool", mergedT[:, fo, c0:c0 + n], pj[0][:, 0:n], pj[2][:, 0:n], ALU.add)

            S.mark(f'g{g}_C_done')
            def up_block(jb):
                wu = rs(ring_slot()[:, 0:KC * 512], KC, 2, 256)
                load_w(wu[:, :, 0, :], w_up[:, jb * 256:(jb + 1) * 256])
                load_w(wu[:, :, 1, :], w_up[:, DFF + jb * 256: DFF + (jb + 1) * 256])
                return wu

            wu_first = up_block(0)
            for k0 in (0, 5, 10):
                k1 = min(k0 + 5, 14)
                load_w(wdn[:, k0:k1, :], w_down[k0 * 128:k1 * 128, :])
            dd = {}
            SD = 96

            def d_mm(ti):
                xi, rows, c0, pi = tiles[ti]
                bks = []
                for half in range(2):
                    bk = nbank()
                    for kc in range(KC):
                        S.mm(bk[0:rows, :], mergedT[:, kc, c0:c0 + rows], wout[:, kc, half * 512:(half + 1) * 512], start=kc == 0, stop=kc == KC - 1)
                    bks.append(bk)
                dd[ti] = bks

            def d_sq(ti):
                xi, rows, c0, pi = tiles[ti]
                for half in range(2):
                    S.act(junk2[0:rows, 0:512], dd[ti][half][0:rows, :], AF.Square, scale=1.0 / 64.0,
                          accum_out=stats[0:rows, SD + 3 * ti + half: SD + 3 * ti + half + 1])

            def d_rstd(ti):
                xi, rows, c0, pi = tiles[ti]
                racc = stats[:, SD + 3 * ti + 2: SD + 3 * ti + 3]
                S.tt("pool", racc[0:rows], stats[0:rows, SD + 3 * ti: SD + 3 * ti + 1], stats[0:rows, SD + 3 * ti + 1: SD + 3 * ti + 2], ALU.add)
                rstd_pool(racc, racc, rows)

            def d_res(ti):
                xi, rows, c0, pi = tiles[ti]
                racc = stats[:, SD + 3 * ti + 2: SD + 3 * ti + 3]
                for half in range(2):
                    dt_ = dtmp[half]
                    S.stt("dve", dt_[0:rows], dd[ti][half][0:rows, :], racc[0:rows], gpost1h[0:rows, half * 512:(half + 1) * 512], ALU.mult, ALU.mult)
                for half in range(2):
                    S.tt("dve" if half == 0 else "pool", X[0:rows, xi, half * 512:(half + 1) * 512], X[0:rows, xi, half * 512:(half + 1) * 512], dtmp[half][0:rows], ALU.add)

            hn_stages = norm_transpose_stages([(X[0:rows, xi, :], rows, c0) for (xi, rows, c0, pi) in tiles], 8, xnT, junk2, hnb, 16, gb=gffnb)
            def d_sq_rstd(ti):
                d_sq(ti)
                d_rstd(ti)

            def d_res_sq(ti):
                d_res(ti)
                hn_stages[0](ti)

            pipeline([d_mm, d_sq_rstd, d_res_sq] + hn_stages[1:], len(tiles))
            hnT = xnT

            S.mark(f'g{g}_D_done')
            if not last:
                ntiles = make_tiles(g + 1)
                nloaded = 0
                cur_slots = [t[0] for t in tiles]
                while nloaded < len(ntiles) and ntiles[nloaded][0] not in cur_slots:
                    a_load(g + 1, ntiles[nloaded])
                    nloaded += 1
                astep = Stepper(a_stages(ntiles), len(ntiles))
            wu_next = wu_first
            for jb in range(FC // 2):
                wu = wu_next
                if jb + 1 < FC // 2:
                    wu_next = up_block(jb + 1)
                if jb < 4:
                    k0 = 14 + 2 * jb
                    load_w(wdn[:, k0:k0 + 2, :], w_down[k0 * 128:(k0 + 2) * 128, :])
                for fl in range(2):
                    fc = 2 * jb + fl
                    cw0 = V2T[:, fc:fc + 1]
                    cw1 = V2T[:, 22 + fc:23 + fc]
                    cw2 = V2T[:, 44 + fc:45 + fc]
                    cbv = V1T[:, 42 + fc:43 + fc]
                    for (c0, n) in blocks:
                        gbk = nbank()
                        for kc in range(KC):
                            S.mm(gbk[:, 0:n], wu[:, kc, 0, fl * 128:(fl + 1) * 128], hnT[:, kc, c0:c0 + n], start=kc == 0, stop=kc == KC - 1)
                        ubk = nbank()
                        for kc in range(KC):
                            S.mm(ubk[:, 0:n], wu[:, kc, 1, fl * 128:(fl + 1) * 128], hnT[:, kc, c0:c0 + n], start=kc == 0, stop=kc == KC - 1)
                        if c0 == 0:
                            gt = gTp[fc % 2]
                            ct = cT[fc % 2]
                            gg = ge[fc % 2]
                            S.copy("pool", gt[:, 0:2], halo_g[:, fc, :])
                            S.copy("act", gt[:, 2:2 + NG], gbk[:, 0:NG])
                            if last:
                                S.copy("pool", CS[:, fc, 32:34], gt[:, NG:NG + 2])
                            else:
                                S.copy("pool", halo_g[:, fc, :], gt[:, NG:NG + 2])
                            S.ts("dve", ct, gt[:, 2:2 + NG], cw2, cbv, ALU.mult, ALU.add)
                            S.stt("dve", ct, gt[:, 1:1 + NG], cw1, ct, ALU.mult, ALU.add)
                            S.stt("dve", ct, gt[:, 0:NG], cw0, ct, ALU.mult, ALU.add)
                            S.act(gg, ct, AF.Gelu_apprx_tanh)
                            S.tt("dve", actT[:, fc, 0:NG], ubk[:, 0:NG], gg, ALU.mult)
                        else:
                            S.copy("act", gTs[:, fc, :, 2:6], rs(gbk[:, 0:NS], NB, 4))
                            S.copy("pool", CS[:, fc, 0:32].rearrange("p (r b) -> p b r", r=2), gTs[:, fc, :, 4:6])
                            S.ts("dve", cTs, gTs[:, fc, :, 2:6], cw2, cbv, ALU.mult, ALU.add)
                            S.stt("dve", cTs, gTs[:, fc, :, 1:5], cw1, cTs, ALU.mult, ALU.add)
                            S.stt("dve", cTs, gTs[:, fc, :, 0:4], cw0, cTs, ALU.mult, ALU.add)
                            S.act(geS, cTs, AF.Gelu_apprx_tanh)
                            S.tt("dve", actT[:, fc, NG:NG + NS], ubk[:, 0:NS], geS.rearrange("p b t -> p (b t)"), ALU.mult)

            S.mark(f'g{g}_E_done')
            if not last:
                nxt_w01 = prefetch_win01()
            for ti, (xi, rows, c0, pi) in enumerate(tiles):
                acc2 = stats[:, 40:42]
                racc = stats[:, 42:43]
                bks = []
                for half in range(2):
                    bk = nbank()
                    for kc in range(FC):
                        S.mm(bk[0:rows, :], actT[:, kc, c0:c0 + rows], wdn[:, kc, half * 512:(half + 1) * 512], start=kc == 0, stop=kc == FC - 1)
                    S.act(junk3[0:rows], bk[0:rows, :], AF.Square, scale=1.0 / 32.0, accum_out=acc2[0:rows, half:half + 1])
                    bks.append(bk)
                S.tt("pool", racc[0:rows], acc2[0:rows, 0:1], acc2[0:rows, 1:2], ALU.add)
                rstd_pool(racc, racc, rows)
                for half in range(2):
                    ft = ftmp[half]
                    S.stt("dve", ft[0:rows], bks[half][0:rows, :], racc[0:rows], gpost2[0:rows, half * 512:(half + 1) * 512], ALU.mult, ALU.mult)
                    S.tt("pool", X[0:rows, xi, half * 512:(half + 1) * 512], X[0:rows, xi, half * 512:(half + 1) * 512], ft[0:rows], ALU.add)
                if pi >= 0:
                    S.dma(sp, y_p[g * NG + pi * 128: g * NG + (pi + 1) * 128, :], X[:, xi, :], is_out=True)
                else:
                    S.dma(sp, y_s[:, :], X[0:NS, xi, :], is_out=True)
                if not last:
                    while nloaded < len(ntiles) and ntiles[nloaded][0] in [t[0] for t in tiles[:ti + 1]] + [x for x in range(5) if x not in cur_slots]:
                        a_load(g + 1, ntiles[nloaded])
                        nloaded += 1
                    while (not astep.done()) and min(astep.k, astep.n - 1) < nloaded:
                        astep.step()
                        if astep.k <= astep.n:
                            break
            if not last:
                assert nloaded == len(ntiles)
                while not astep.done():
                    astep.step()

        S.mark('groups_done')
        for c in range(2):
            bk = nbank()
            S.tr(bk[0:79, 0:128], PSt[:, c, 0:79], identf)
            S.copy("dve", PStT[0:79, c * 128:(c + 1) * 128], bk[0:79, 0:128])
        for t in range(4):
            S.dma(sp, o_pool_s[:, 11 + t, :], PStT[t * 16:(t + 1) * 16, :], is_out=True)
        S.dma(sp, o_pool_p[:, :], PStT[64:79, :], is_out=True)
        for q in range(6):
            bk = nbank()
            nfc = 4 if q < 5 else 2
            for j in range(nfc):
                fc = q * 4 + j
                S.tr(bk[0:34, j * 128:(j + 1) * 128], CS[:, fc, :], identf)
            S.copy("act" if q % 2 else "dve", CST[0:34, q * 512: q * 512 + nfc * 128], bk[0:34, 0:nfc * 128])
        for r in range(2):
            S.dma(sp, o_conv_s[:, r, :], CST[r * 16:(r + 1) * 16, :], is_out=True)
        S.dma(sp, o_conv_p[:, :], CST[32:34, :], is_out=True)

        import os
        lim = int(os.environ.get("KSTOP", "0")) or None
        if os.environ.get("KMARKS"):
            print("MARKS", S.marks, "total", len(S.ops))
        S.emit(limit=lim)
    return nc


_CACHE = {}


def _consts():
    identb = np.eye(128, dtype=np.float32).astype(ml_dtypes.bfloat16)
    identf = np.eye(128, dtype=np.float32)
    s = np.arange(128)
    maskT = (s[:, None] <= s[None, :]).astype(np.float32)
    rc = np.zeros((128, 2, 16), np.float32)
    wins = {(0, 0): 2, (1, 0): 4, (0, 1): 8, (1, 1): 16}
    for p in range(128):
        for c in range(2):
            w = wins[(p // 64, c)]
            for t in range(16):
                rc[p, c, t] = 1.0 / min(w, t + 1)
    return identb, identf, maskT, rc.reshape(128, 32)


def make_in_maps(inputs):
    f = lambda k: np.ascontiguousarray(np.asarray(inputs[k], dtype=np.float32))
    x_prompt = f("x_prompt")
    x_sample = f("x_sample")
    mem_prompt = f("mem_prompt")
    cache_k = f("cache_mem_k")[0].reshape(128, MEM, 256)
    cache_v = f("cache_mem_v")[0].reshape(128, MEM, 256)
    state_pool = f("state_pool")[0]
    state_conv = f("state_conv")[0]
    identb, identf, maskT, rcnt = _consts()
    shared = {
        "norm_mix_pre": f("norm_mix_pre"), "norm_mix_post": f("norm_mix_post"),
        "norm_ffn_pre": f("norm_ffn_pre"), "norm_ffn_post": f("norm_ffn_post"),
        "w_in": f("w_in")[0], "b_gate": f("b_gate"),
        "gmlp_ln_g": f("gmlp_ln_g"), "gmlp_ln_b": f("gmlp_ln_b"),
        "w_spatial": f("w_spatial")[0], "b_spatial": f("b_spatial")[0],
        "w_pool": f("w_pool")[0], "pool_scale": f("pool_scale"),
        "mem_norm": f("mem_norm"), "w_mem_kv": f("w_mem_kv")[0],
        "w_branch_a": f("w_branch_a")[0], "w_branch_b": f("w_branch_b")[0], "w_branch_c": f("w_branch_c")[0],
        "w_out": f("w_out")[0], "w_up": f("w_up")[0], "conv_w": f("conv_w")[0], "conv_b": f("conv_b"),
        "w_down": f("w_down")[0],
        "c_identb": identb, "c_identf": identf, "c_maskT": maskT, "c_rcnt": rcnt,
    }
    in_maps = []
    for c in range(NCORES):
        m = dict(shared)
        m["x_p"] = x_prompt[c]
        m["x_s"] = np.ascontiguousarray(x_sample[c * NB:(c + 1) * NB].reshape(NS, D))
        m["mem"] = mem_prompt[c]
        m["ck"] = np.ascontiguousarray(cache_k[c * NB:(c + 1) * NB])
        m["cv"] = np.ascontiguousarray(cache_v[c * NB:(c + 1) * NB])
        m["st_pool"] = np.ascontiguousarray(state_pool[c * NB:(c + 1) * NB])
        m["st_conv"] = np.ascontiguousarray(state_conv[c * NB:(c + 1) * NB])
        in_maps.append(m)
    return in_maps


def kernel(**inputs):
    in_maps = make_in_maps(inputs)
    if "nc" not in _CACHE:
        _CACHE["nc"] = build_program()
    nc = _CACHE["nc"]
    res = run_bass_kernel_spmd(nc, in_maps, core_ids=list(range(NCORES)))
    R = res.results
    y_prompt = np.stack([R[c]["y_p"] for c in range(NCORES)]).astype(np.float32)
    y_sample = np.concatenate([R[c]["y_s"].reshape(NB, 4, D) for c in range(NCORES)]).astype(np.float32)
    mk = np.stack([R[c]["o_mk"].reshape(MEM, 4, 64) for c in range(NCORES)])[None].astype(np.float32)
    mv = np.stack([R[c]["o_mv"].reshape(MEM, 4, 64) for c in range(NCORES)])[None].astype(np.float32)
    pool_p = np.stack([R[c]["o_pool_p"] for c in range(NCORES)])[None].astype(np.float32)
    conv_p = np.stack([R[c]["o_conv_p"] for c in range(NCORES)])[None].astype(np.float32)
    v_s = np.concatenate([R[c]["o_v_s"].reshape(NB, 4, AW) for c in range(NCORES)])[None].astype(np.float32)
    pool_s = np.concatenate([R[c]["o_pool_s"] for c in range(NCORES)])[None].astype(np.float32)
    conv_s = np.concatenate([R[c]["o_conv_s"] for c in range(NCORES)])[None].astype(np.float32)
    return (y_prompt, y_sample, mk, mv, pool_p, conv_p, v_s, pool_s, conv_s)
```
